# Optimizing a Trainium2 kernel written in Bass

```python
import jax, jax.numpy as jnp
from jax import lax
import numpy as np

D_MODEL = 2048
BATCH = 4
SEQ = 2048
DEPTH = 1
DEC_BATCH = 32
DEC_SEQ = 32
PAST_LEN = 1024

CHUNK = 64
N_HEADS = 16
QK_NOPE = 128
QK_ROPE = 64
V_HEAD = 128
Q_LORA = 512
KV_LORA = 512
POOL_WINDOWS = (2, 4, 8, 16)
POOL_GROUPS = len(POOL_WINDOWS)
POOL_WIDTH = D_MODEL // 2
POOL_GC = POOL_WIDTH // POOL_GROUPS
POOL_HIST = max(POOL_WINDOWS) - 1
D_FF = 5632
N_BRANCH = 2
D_IN = POOL_WIDTH + Q_LORA + KV_LORA + QK_ROPE + N_BRANCH * D_MODEL
ROPE_THETA = 10000.0
EPS = 1e-6
Q_BLOCK = 128
SM_SCALE = (QK_NOPE + QK_ROPE) ** -0.5
NEG_INF = -1e30

kernel_name = 'hybrid_pool_mla_macaron_stream_step'


def rmsnorm(x, g):
    xf = x.astype(jnp.float32)
    out = xf * lax.rsqrt(jnp.mean(xf * xf, axis=-1, keepdims=True) + EPS) * g.astype(jnp.float32)
    return out.astype(x.dtype)


def swiglu(x, w_gate, w_up, w_down):
    return (jax.nn.silu(x @ w_gate) * (x @ w_up)) @ w_down


def rope(x, pos):
    half = QK_ROPE // 2
    inv = ROPE_THETA ** (-jnp.arange(half, dtype=jnp.float32) * 2.0 / QK_ROPE)
    ang = pos.astype(jnp.float32)[:, None] * inv[None, :]
    if x.ndim == 4:
        ang = ang[:, None, :]
    c, s = jnp.cos(ang), jnp.sin(ang)
    xf = x.astype(jnp.float32)
    x1, x2 = xf[..., :half], xf[..., half:]
    return jnp.concatenate([x1 * c - x2 * s, x2 * c + x1 * s], axis=-1).astype(x.dtype)


def pool_mix(z_hist, z, pos, w_pool, pool_scale):
    B, T, P = z.shape
    L = z_hist.shape[1]
    z_ext = jnp.concatenate([z_hist, z], axis=1)
    cs = jnp.cumsum(z_ext.astype(jnp.float32), axis=1)
    cs = jnp.pad(cs, ((0, 0), (1, 0), (0, 0)))
    means = []
    for g, w in enumerate(POOL_WINDOWS):
        sl = slice(g * POOL_GC, (g + 1) * POOL_GC)
        s = cs[:, L + 1:L + 1 + T, sl] - cs[:, L + 1 - w:L + 1 - w + T, sl]
        cnt = jnp.minimum(pos + 1, w).astype(jnp.float32)[None, :, None]
        means.append(s / cnt)
    pooled = jnp.concatenate(means, axis=-1) - z.astype(jnp.float32)
    pooled = pooled.astype(z.dtype).reshape(B, T, POOL_GROUPS, POOL_GC)
    mixed = jnp.einsum('btgc,gcd->btgd', pooled, w_pool).reshape(B, T, P)
    return mixed * pool_scale, z_ext[:, -POOL_HIST:]


def mla_attend(q_nope, q_rope, ckv, k_rope, q_pos, k_pos, w_uk, w_uv):
    B, Tq = q_nope.shape[:2]
    k_nope = jnp.einsum('bsc,chd->bshd', ckv, w_uk)
    v = jnp.einsum('bsc,chd->bshd', ckv, w_uv)
    k_chunk = k_pos // CHUNK

    def attend_block(args):
        qn, qr, qp = args
        s = (jnp.einsum('bqhd,bshd->bhqs', qn, k_nope)
             + jnp.einsum('bqhr,bsr->bhqs', qr, k_rope)).astype(jnp.float32) * SM_SCALE
        mask = k_chunk[None, :] <= (qp // CHUNK)[:, None]
        s = jnp.where(mask[None, None], s, NEG_INF)
        p = jax.nn.softmax(s, axis=-1).astype(v.dtype)
        return jnp.einsum('bhqs,bshd->bqhd', p, v)

    if Tq > Q_BLOCK and Tq % Q_BLOCK == 0:
        nb = Tq // Q_BLOCK
        qn_b = q_nope.reshape(B, nb, Q_BLOCK, N_HEADS, QK_NOPE).transpose(1, 0, 2, 3, 4)
        qr_b = q_rope.reshape(B, nb, Q_BLOCK, N_HEADS, QK_ROPE).transpose(1, 0, 2, 3, 4)
        qp_b = q_pos.reshape(nb, Q_BLOCK)
        out = lax.map(attend_block, (qn_b, qr_b, qp_b))
        return out.transpose(1, 0, 2, 3, 4).reshape(B, Tq, N_HEADS, V_HEAD)
    return attend_block((q_nope, q_rope, q_pos))


def token_mix(u, pos, past_pos, pool_hist, ckv_past, krope_past, p):
    B, T, _ = u.shape
    proj = u @ p['w_in']
    o1 = POOL_WIDTH
    o2 = o1 + Q_LORA
    o3 = o2 + KV_LORA
    o4 = o3 + QK_ROPE
    z_pool = proj[..., :o1]
    q_lat = rmsnorm(proj[..., o1:o2], p['g_q_lat'])
    ckv = rmsnorm(proj[..., o2:o3], p['g_kv_lat'])
    krope = rope(proj[..., o3:o4], pos)
    gates = jax.nn.sigmoid(proj[..., o4:].astype(jnp.float32)).astype(u.dtype).reshape(B, T, N_BRANCH, D_MODEL)

    pooled, pool_tail = pool_mix(pool_hist, z_pool, pos, p['w_pool'], p['pool_scale'])
    a = pooled @ p['w_pool_out']

    q = jnp.einsum('btc,chd->bthd', q_lat, p['w_uq'])
    q_nope = q[..., :QK_NOPE]
    q_rope = rope(q[..., QK_NOPE:], pos)
    ckv_all = jnp.concatenate([ckv_past, ckv], axis=1)
    kr_all = jnp.concatenate([krope_past, krope], axis=1)
    k_pos = jnp.concatenate([past_pos, pos])
    o = mla_attend(q_nope, q_rope, ckv_all, kr_all, pos, k_pos, p['w_uk'], p['w_uv'])
    b = o.reshape(B, T, N_HEADS * V_HEAD) @ p['w_o_attn']

    merged = gates[:, :, 0] * a + gates[:, :, 1] * b
    return merged @ p['w_out'], ckv, krope, pool_tail


def layer(x, pos, past_pos, pool_hist, ckv_past, krope_past, p):
    h = x + 0.5 * swiglu(rmsnorm(x, p['g_ffn1']), p['w1_gate'], p['w1_up'], p['w1_down'])
    mix, ckv_new, kr_new, pool_tail = token_mix(rmsnorm(h, p['g_mix']), pos, past_pos,
                                                pool_hist, ckv_past, krope_past, p)
    h = h + mix
    h = h + 0.5 * swiglu(rmsnorm(h, p['g_ffn2']), p['w2_gate'], p['w2_up'], p['w2_down'])
    return h, ckv_new, kr_new, pool_tail


def setup_inputs(seed: int = 0) -> dict:
    key = jax.random.key(seed)
    ks = iter(jax.random.split(key, 40))
    f32 = jnp.float32

    def nrm(shape, scale):
        return jax.random.normal(next(ks), shape, f32) * scale

    def gain(shape):
        return 1.0 + 0.01 * jax.random.normal(next(ks), shape, f32)

    return {
        'x_prompt': nrm((BATCH, SEQ, D_MODEL), 1.0),
        'x_sample': nrm((DEC_BATCH, DEC_SEQ, D_MODEL), 1.0),
        'cache_ckv': nrm((DEPTH, DEC_BATCH, PAST_LEN, KV_LORA), 1.0),
        'cache_krope': nrm((DEPTH, DEC_BATCH, PAST_LEN, QK_ROPE), 1.0),
        'state_pool': nrm((DEPTH, DEC_BATCH, POOL_HIST, POOL_WIDTH), 1.0),
        'g_ffn1': gain((DEPTH, D_MODEL)),
        'w1_gate': nrm((DEPTH, D_MODEL, D_FF), D_MODEL ** -0.5),
        'w1_up': nrm((DEPTH, D_MODEL, D_FF), D_MODEL ** -0.5),
        'w1_down': nrm((DEPTH, D_FF, D_MODEL), D_FF ** -0.5),
        'g_mix': gain((DEPTH, D_MODEL)),
        'w_in': nrm((DEPTH, D_MODEL, D_IN), D_MODEL ** -0.5),
        'g_q_lat': gain((DEPTH, Q_LORA)),
        'g_kv_lat': gain((DEPTH, KV_LORA)),
        'w_uq': nrm((DEPTH, Q_LORA, N_HEADS, QK_NOPE + QK_ROPE), Q_LORA ** -0.5),
        'w_uk': nrm((DEPTH, KV_LORA, N_HEADS, QK_NOPE), KV_LORA ** -0.5),
        'w_uv': nrm((DEPTH, KV_LORA, N_HEADS, V_HEAD), KV_LORA ** -0.5),
        'w_o_attn': nrm((DEPTH, N_HEADS * V_HEAD, D_MODEL), (N_HEADS * V_HEAD) ** -0.5),
        'w_pool': nrm((DEPTH, POOL_GROUPS, POOL_GC, POOL_GC), POOL_GC ** -0.5),
        'pool_scale': gain((DEPTH, POOL_WIDTH)),
        'w_pool_out': nrm((DEPTH, POOL_WIDTH, D_MODEL), POOL_WIDTH ** -0.5),
        'w_out': nrm((DEPTH, D_MODEL, D_MODEL), D_MODEL ** -0.5),
        'g_ffn2': gain((DEPTH, D_MODEL)),
        'w2_gate': nrm((DEPTH, D_MODEL, D_FF), D_MODEL ** -0.5),
        'w2_up': nrm((DEPTH, D_MODEL, D_FF), D_MODEL ** -0.5),
        'w2_down': nrm((DEPTH, D_FF, D_MODEL), D_FF ** -0.5),
        'g_final': gain((D_MODEL,)),
    }


def reference(x_prompt, x_sample, cache_ckv, cache_krope, state_pool,
              g_ffn1, w1_gate, w1_up, w1_down, g_mix, w_in, g_q_lat, g_kv_lat,
              w_uq, w_uk, w_uv, w_o_attn, w_pool, pool_scale, w_pool_out, w_out,
              g_ffn2, w2_gate, w2_up, w2_down, g_final):
    Bp, Tp, _ = x_prompt.shape
    Bs, Ts, _ = x_sample.shape
    past = cache_ckv.shape[2]
    dt = x_prompt.dtype

    pos_p = jnp.arange(Tp, dtype=jnp.int32)
    past_pos_p = jnp.zeros((0,), jnp.int32)
    pos_s = past + jnp.arange(Ts, dtype=jnp.int32)
    past_pos_s = jnp.arange(past, dtype=jnp.int32)

    hp, hs = x_prompt, x_sample
    ckv_p, kr_p, pool_p, ckv_s, kr_s, pool_s = [], [], [], [], [], []
    for l in range(DEPTH):
        lw = {
            'g_ffn1': g_ffn1[l], 'w1_gate': w1_gate[l], 'w1_up': w1_up[l], 'w1_down': w1_down[l],
            'g_mix': g_mix[l], 'w_in': w_in[l], 'g_q_lat': g_q_lat[l], 'g_kv_lat': g_kv_lat[l],
            'w_uq': w_uq[l], 'w_uk': w_uk[l], 'w_uv': w_uv[l], 'w_o_attn': w_o_attn[l],
            'w_pool': w_pool[l], 'pool_scale': pool_scale[l], 'w_pool_out': w_pool_out[l],
            'w_out': w_out[l], 'g_ffn2': g_ffn2[l], 'w2_gate': w2_gate[l], 'w2_up': w2_up[l],
            'w2_down': w2_down[l],
        }
        hp, c1, k1, t1 = layer(hp, pos_p, past_pos_p,
                               jnp.zeros((Bp, POOL_HIST, POOL_WIDTH), dt),
                               jnp.zeros((Bp, 0, KV_LORA), dt),
                               jnp.zeros((Bp, 0, QK_ROPE), dt), lw)
        hs, c2, k2, t2 = layer(hs, pos_s, past_pos_s, state_pool[l].astype(hs.dtype),
                               cache_ckv[l].astype(hs.dtype), cache_krope[l].astype(hs.dtype), lw)
        ckv_p.append(c1); kr_p.append(k1); pool_p.append(t1)
        ckv_s.append(c2); kr_s.append(k2); pool_s.append(t2)

    y_prompt = rmsnorm(hp, g_final)
    y_sample = rmsnorm(hs, g_final)
    return (y_prompt, y_sample,
            jnp.stack(ckv_p), jnp.stack(kr_p), jnp.stack(pool_p),
            jnp.stack(ckv_s), jnp.stack(kr_s), jnp.stack(pool_s))
```

```python
import numpy as np
from contextlib import ExitStack
import concourse.bass as bass
import concourse.mybir as mybir
from concourse.bass_utils import run_bass_kernel_spmd

F32 = mybir.dt.float32
BF16 = mybir.dt.bfloat16
AF = mybir.ActivationFunctionType
ALU = mybir.AluOpType

D = 2048
DFF = 5632
NOWN = 1024
NSMP = 128
NT = NOWN + NSMP
NPREV = 1024
EPS = 1e-6
SM_SCALE = 192 ** -0.5
ARENA = 211968
RING_SLOTS = 4
SLOT_B = 8192
RING_OFF = ARENA - RING_SLOTS * SLOT_B

ENGS = ["pe", "act", "dve", "pool", "sp"]


class Task:
    __slots__ = ("eng", "emit", "deps", "is_dma", "sem", "val", "prev_val", "has_dep", "ndma")

    def __init__(self, eng, emit, is_dma, ndma):
        self.eng = eng
        self.emit = emit
        self.deps = set()
        self.is_dma = is_dma
        self.ndma = ndma
        self.sem = None
        self.val = 0
        self.prev_val = 0
        self.has_dep = False


class RK:
    __slots__ = ("slot", "u", "state")

    def __init__(self, slot, u, state):
        self.slot, self.u, self.state = slot, u, state

    def __hash__(self):
        return hash(("ring", self.slot))

    def __eq__(self, o):
        return isinstance(o, RK) and o.slot == self.slot

    def check(self):
        assert self.state["u"] - self.u < RING_SLOTS, "stale ring slot use"


class Sched:
    def __init__(self):
        self.tasks = {e: [] for e in ENGS}
        self.lastw = {}
        self.readers = {}
        self.pending = {}
        self.dma_since = []

    def add(self, eng, emit, reads=(), writes=(), dma=0):
        t = Task(eng, emit, dma > 0, dma)
        deps = set()
        for r in reads:
            if isinstance(r, RK):
                r.check()
            w = self.lastw.get(r)
            if w is not None:
                deps.add(w)
        for k in writes:
            w = self.lastw.get(k)
            if w is not None:
                deps.add(w)
            rs = self.readers.get(k)
            if rs:
                deps.update(rs)
        for r in reads:
            self.readers.setdefault(r, []).append(t)
        for k in writes:
            self.lastw[k] = t
            self.readers[k] = []
        if eng in self.pending:
            deps |= self.pending.pop(eng)
        deps.discard(t)
        t.deps = deps
        for d in deps:
            d.has_dep = True
        self.tasks[eng].append(t)
        if t.is_dma:
            self.dma_since.append(t)
        return t

    def barrier(self):
        s = set(self.dma_since)
        for e in ENGS:
            if self.tasks[e]:
                s.add(self.tasks[e][-1])
        for t in s:
            t.has_dep = True
        for e in ENGS:
            self.pending[e] = self.pending.get(e, set()) | s
        self.dma_since = []
        self.lastw.clear()
        self.readers.clear()

    def emit_all(self, nc, stack):
        KD = 8
        csem = {}
        for e in ["pe", "act", "dve"]:
            n = sum(1 for t in self.tasks[e] if t.has_dep)
            ngen = n // 30000 + 1
            csem[e] = [stack.enter_context(nc.semaphore(f"c_{e}_{g}")) for g in range(ngen)]
            cnt = 0
            for t in self.tasks[e]:
                if t.has_dep:
                    t.sem = csem[e][cnt // 30000]
                    t.val = cnt % 30000 + 1
                    cnt += 1
        all_dma = []
        for e in ["pool", "sp"]:
            pool = [stack.enter_context(nc.semaphore(f"d_{e}_{k}")) for k in range(KD)]
            vals = [0] * KD
            for i, t in enumerate(self.tasks[e]):
                assert t.is_dma
                k = i % KD
                t.sem = pool[k]
                t.prev_val = vals[k]
                vals[k] += 16 * t.ndma
                t.val = vals[k]
            all_dma += [(pool[k], vals[k]) for k in range(KD) if vals[k] > 0]

        block = stack.enter_context(nc.Block())

        def run(ename, eng):
            waited = {}

            def wait(sem, val):
                key = id(sem)
                if waited.get(key, 0) < val:
                    eng.wait_ge(sem, val)
                    waited[key] = val

            for t in self.tasks[ename]:
                for d in t.deps:
                    if ename == "pe" and d.eng == "pe":
                        continue
                    wait(d.sem, d.val)
                if t.is_dma and t.prev_val > 0:
                    wait(t.sem, t.prev_val)
                r = t.emit(eng)
                if t.is_dma:
                    assert len(r) == t.ndma, (len(r), t.ndma)
                    for ins in r:
                        ins.then_inc(t.sem, 16)
                elif t.has_dep:
                    r.then_inc(t.sem, 1)
            if ename == "sp":
                for sem, val in all_dma:
                    wait(sem, val)

        @block.tensor
        def _(pe):
            run("pe", pe)

        @block.scalar
        def _(a):
            run("act", a)

        @block.vector
        def _(v):
            run("dve", v)

        @block.gpsimd
        def _(g):
            run("pool", g)

        @block.sync
        def _(sp):
            run("sp", sp)


def prep_cols(W):
    K, N = W.shape
    kc, ncn = K // 128, N // 128
    return np.ascontiguousarray(W.reshape(kc, 128, ncn, 128).transpose(2, 1, 0, 3)).reshape(ncn, 128, kc * 128)


def prep_ffn(wg, wu, wd):
    g = prep_cols(wg)
    u = prep_cols(wu)
    gu = np.ascontiguousarray(np.stack([g, u], axis=2)).reshape(44, 128, 4096)
    wds = []
    for q in range(4):
        c = prep_cols(wd[q * 1408:(q + 1) * 1408, :])
        c = c.reshape(8, 2, 128, 1408).transpose(0, 2, 1, 3).reshape(8, 128, 2816)
        wds.append(c)
    return gu, np.ascontiguousarray(np.concatenate(wds, axis=0))


def swap_half(W):
    return np.concatenate([W[..., 32:64], W[..., 0:32]], axis=-1)


def rope_tables(pos):
    half = 32
    inv = 10000.0 ** (-np.arange(half, dtype=np.float64) * 2.0 / 64)
    ang = pos.astype(np.float64)[:, None] * inv[None, :]
    c, s = np.cos(ang).astype(np.float32), np.sin(ang).astype(np.float32)
    C = np.concatenate([c, c], axis=1).T
    S = np.concatenate([-s, s], axis=1).T
    return np.ascontiguousarray(np.stack([C, S], axis=0)).astype(np.float32)


def build_program():
    nc = bass.Bass("TRN2", target_bir_lowering=False)
    S = Sched()

    def din(name, shape, dt=F32):
        return nc.dram_tensor(name, list(shape), dt, kind="ExternalInput").ap()

    def dout(name, shape, dt=F32):
        return nc.dram_tensor(name, list(shape), dt, kind="ExternalOutput").ap()

    x_own = din("x_own", [D, NOWN])
    x_prev = din("x_prev", [D, NPREV])
    x_smp = din("x_smp", [D, NSMP])
    cache_ckv = din("cache_ckv", [4, 1024, 512])
    cache_ckvT = din("cache_ckvT", [4, 512, 1024])
    cache_krT = din("cache_krT", [4, 64, 1024])
    state_pool = din("state_pool", [1024, 60])
    wgu1 = din("wgu1", [44, 128, 4096])
    wd1 = din("wd1", [32, 128, 2816])
    wgu2 = din("wgu2", [44, 128, 4096])
    wd2 = din("wd2", [32, 128, 2816])
    wkvq = din("wkvq", [4, 128, 4096])
    wkr_d = din("wkr", [1, 128, 2048])
    wz_d = din("wz", [4, 128, 4096])
    wgate = din("wgate", [16, 128, 4096])
    wpool = din("wpool", [1, 128, 2048])
    watt = din("watt", [16, 128, 2048])
    watt_s = din("watt_s", [16, 128, 1536])
    wuv_s = din("wuv_s", [2, 128, 4096])
    wmrg = din("wmrg", [16, 128, 3072])
    wout = din("wout", [8, 128, 4096])
    c_ident = din("c_ident", [128, 128])
    c_gains = din("c_gains", [128, 80])
    c_cs_main = din("c_cs_main", [2, 64, NT])
    c_cs_prev = din("c_cs_prev", [2, 64, NPREV])
    c_rc16 = din("c_rc16", [128, 64])
    c_prevb = din("c_prevb", [128, 1])

    y_own = dout("y_own", [D, NOWN])
    y_smp = dout("y_smp", [D, NSMP])
    ckvT_out = dout("ckvT_out", [512, NT])
    krT_out = dout("krT_out", [64, NT])
    poolT_own = dout("poolT_own", [1024, 15])
    poolT_smp = dout("poolT_smp", [1024, 60])

    hs = nc.dram_tensor("hs_scratch", [128, 16 * NT], F32).ap()
    import os
    DBG = os.environ.get("KDBG", "0") == "1"
    if DBG:
        dbg_mixed = dout("dbg_mixed", [128, 8 * NT], BF16)
        dbg_o = dout("dbg_o", [128, 16 * NT], BF16)
        dbg_m = dout("dbg_m", [128, 16 * NT], BF16)
        dbg_x1 = dout("dbg_x1", [128, 16 * NT], F32)
        dbg_pooled = dout("dbg_pooled", [128, 8 * NT], BF16)

    stack = ExitStack()
    arena_t = stack.enter_context(nc.sbuf_tensor("arena", [128, ARENA // 4], F32))
    psum_t = stack.enter_context(nc.psum_tensor("psum", [128, 4096], F32))

    def carve(off, shape, dt):
        esz = 4 if dt == F32 else 2
        P = shape[0]
        n = 1
        for s_ in shape[1:]:
            n *= s_
        nb = n * esz
        assert off % 4 == 0 and nb % 4 == 0 and off + nb <= ARENA, (off, shape)
        ap = arena_t[0:P, off // 4:(off + nb) // 4]
        if dt != F32:
            ap = ap.bitcast(dt)
        if len(shape) == 3:
            ap = ap.rearrange("p (a b) -> p a b", b=shape[2])
        elif len(shape) == 4:
            ap = ap.rearrange("p (a b c) -> p a b c", b=shape[2], c=shape[3])
        return ap

    class Alloc:
        def __init__(self, lo, hi):
            self.lo, self.hi, self.cur = lo, hi, lo

        def get(self, shape, dt):
            esz = 4 if dt == F32 else 2
            n = 1
            for s_ in shape[1:]:
                n *= s_
            nb = (n * esz + 31) // 32 * 32
            off = self.cur
            self.cur += nb
            assert self.cur <= self.hi, ("alloc overflow", shape, self.cur, self.hi)
            return carve(off, shape, dt)

    def psb(b, n=512, parts=128):
        return psum_t[0:parts, b * 512:b * 512 + n]

    def psb_bf(b, n, parts=128):
        return psum_t[0:parts, b * 512:b * 512 + 512].bitcast(BF16)[:, 0:n]

    ca = Alloc(0, 2176)
    ident_f = ca.get([128, 128], F32)
    ident_b = ca.get([128, 128], BF16)
    onesD = ca.get([128, 128], BF16)
    ones512 = ca.get([128, 128], BF16)
    ones1 = ca.get([128, 128], BF16)
    gains = ca.get([128, 80], F32)
    prevb = ca.get([128, 1], F32)
    epsc = ca.get([128, 1], F32)
    rc16 = ca.get([128, 4, 16], F32)
    G_FFN1, G_MIX, G_FFN2, G_FIN, G_Q, G_KV, G_PS = 0, 16, 32, 48, 64, 68, 72

    pa = Alloc(2176, 12928)
    ckvp_bf = pa.get([128, 4, 1024], BF16)
    krp_full = pa.get([128, 1024], BF16)
    krp_bf = krp_full[0:64, :]
    zhist = pa.get([128, 8, 16], F32)
    XN_OFF = 12928
    R0_OFF = XN_OFF + 36864
    E_OFF = R0_OFF + 73728
    assert E_OFF == 123520

    def dma(eng, out, in_, reads=(), writes=()):
        return S.add(eng, lambda e, o=out, i=in_: [e.dma_start(out=o, in_=i)], reads, writes, dma=1)

    dma("sp", ident_f, c_ident, (), ["ident_f"])
    dma("sp", gains, c_gains, (), ["gains"])
    dma("sp", prevb, c_prevb, (), ["prevb"])
    dma("sp", rc16.rearrange("p a b -> p (a b)"), c_rc16, (), ["rc16"])
    S.add("dve", lambda v: v.tensor_copy(out=ident_b, in_=ident_f), ["ident_f"], ["ident_b"])
    S.add("dve", lambda v: v.memset(onesD, 1.0 / 2048), (), ["onesD"])
    S.add("dve", lambda v: v.memset(ones512, 1.0 / 512), (), ["ones512"])
    S.add("dve", lambda v: v.memset(ones1, 1.0), (), ["ones1"])
    S.add("dve", lambda v: v.memset(epsc, EPS), (), ["epsc"])

    ring_state = {"u": 0}

    def wload(src_ap, ncols):
        u = ring_state["u"]
        ring_state["u"] += 1
        slot = u % RING_SLOTS
        assert ncols * 2 <= SLOT_B
        view = carve(RING_OFF + slot * SLOT_B, [128, ncols], BF16)
        key = RK(slot, u + 1, ring_state)
        dma("pool", view, src_ap, (), [key])
        return view, key

    psrot = {"i": 0}

    def bank(group):
        i = psrot.setdefault(id(group), 0)
        psrot[id(group)] = i + 1
        return group[i % len(group)]

    PS_A = [0, 1, 2, 3]
    PS_B = [4, 5]
    PS_C = [6, 7]

    def rms_stats(src_fn, nch, ones_ap, n, rstd_out, keyr, sq_bufs, sqt_buf, tag):
        b = bank(PS_C)
        for kc in range(nch):
            ap, keys = src_fn(kc)
            sq = sq_bufs[kc % 2]
            S.add("act", lambda a, o=sq[:, 0:n], i=ap: a.activation(out=o, in_=i, func=AF.Square),
                  list(keys), [("sq", tag, kc % 2)])
            S.add("pe", lambda pe, o=psb(b, n), r=sq[:, 0:n], st=(kc == 0), sp_=(kc == nch - 1):
                  pe.matmul(o, ones_ap, r, start=st, stop=sp_),
                  [("sq", tag, kc % 2), "onesD", "ones512"], [("ps", b)])
        S.add("act", lambda a, o=sqt_buf[:, 0:n], i=psb(b, n): a.activation(out=o, in_=i, func=AF.Sqrt, bias=epsc[:, 0:1], scale=1.0),
              [("ps", b), "epsc"], [("sqt", tag)])
        S.add("dve", lambda v, o=rstd_out, i=sqt_buf[:, 0:n]: v.reciprocal(out=o, in_=i),
              [("sqt", tag)], [keyr])

    def load_xT(x_dram, ntok, xT, col0, stage_bufs, tag):
        src3 = x_dram.rearrange("(k p) t -> p k t", p=128)
        for c0 in range(0, ntok, 512):
            n = min(512, ntok - c0)
            S.add("sp", lambda e, c0=c0, n=n: [e.dma_start(out=xT[:, g * 4:(g + 1) * 4, col0 + c0:col0 + c0 + n], in_=src3[:, g * 4:(g + 1) * 4, c0:c0 + n]) for g in range(4)],
                  (), [(tag, kc, c) for kc in range(16) for c in range(col0 + c0, col0 + c0 + n, 128)], dma=4)

    def xkeys(tag, kc, c0, n):
        return [(tag, kc, c) for c in range(c0 - c0 % 128, c0 + n, 128)]

    def norm_to_bf(xT, xtag, tiles, gcol, xn, xntag, ea):
        sq_bufs = [ea["sq0"], ea["sq1"]]
        for (c0, n) in tiles:
            rstd = ea["rstd"][:, 0:n]
            rms_stats(lambda kc: (xT[:, kc, c0:c0 + n], xkeys(xtag, kc, c0, n)), 16, onesD, n, rstd, ("rstd", xntag), sq_bufs, ea["sqt"], xntag)
            for kc in range(16):
                S.add("dve", lambda v, o=xn[:, kc, c0:c0 + n], i=xT[:, kc, c0:c0 + n], s=gains[:, gcol + kc:gcol + kc + 1], r=rstd:
                      v.scalar_tensor_tensor(out=o, in0=i, scalar=s, in1=r, op0=ALU.mult, op1=ALU.mult),
                      [("rstd", xntag), "gains"] + xkeys(xtag, kc, c0, n), xkeys(xntag, kc, c0, n))

    def ffn(xT, xtag, xn, xntag, tiles, wgu, wd, ea):
        hT = ea["hT"]
        for q in range(4):
            for fl in range(11):
                fc = q * 11 + fl
                wv, wk = wload(wgu[fc], 4096)
                wv4 = wv.rearrange("p (a b c) -> p a b c", a=2, b=16, c=128)
                for (c0, n) in tiles:
                    bg = bank(PS_A)
                    bu = bank(PS_A)
                    for which, b in ((0, bg), (1, bu)):
                        def mm(pe, which=which, b=b, c0=c0, n=n, wv4=wv4):
                            r = None
                            for kc in range(16):
                                r = pe.matmul(psb(b, n), wv4[:, which, kc, :], xn[:, kc, c0:c0 + n], start=(kc == 0), stop=(kc == 15))
                            return r
                        S.add("pe", mm, [wk] + [k for kc in range(16) for k in xkeys(xntag, kc, c0, n)], [("ps", b)])
                    sg = ea["sg"][bg % 2 if False else (psrot.setdefault("sg", 0) % 2)]
                    sgi = psrot["sg"] % 2
                    psrot["sg"] += 1
                    S.add("act", lambda a, o=sg[:, 0:n], i=psb(bg, n): a.activation(out=o, in_=i, func=AF.Silu),
                          [("ps", bg)], [("sg", sgi)])
                    S.add("dve", lambda v, o=hT[:, fl, c0:c0 + n], i0=sg[:, 0:n], i1=psb(bu, n): v.tensor_tensor(out=o, in0=i0, in1=i1, op=ALU.mult),
                          [("sg", sgi), ("ps", bu)], [("hT", fl, c0)])
            for dcp in range(8):
                wv, wk = wload(wd[q * 8 + dcp], 2816)
                wv4 = wv.rearrange("p (a b c) -> p a b c", a=2, b=11, c=128)
                for d2 in range(2):
                    dc = dcp * 2 + d2
                    for (c0, n) in tiles:
                        b = bank(PS_B)

                        def mm(pe, d2=d2, b=b, c0=c0, n=n, wv4=wv4):
                            r = None
                            for fl in range(11):
                                r = pe.matmul(psb(b, n), wv4[:, d2, fl, :], hT[:, fl, c0:c0 + n], start=(fl == 0), stop=(fl == 10))
                            return r
                        S.add("pe", mm, [wk] + [("hT", fl, c0) for fl in range(11)], [("ps", b)])
                        S.add("dve", lambda v, o=xT[:, dc, c0:c0 + n], i0=psb(b, n): v.scalar_tensor_tensor(out=o, in0=i0, scalar=0.5, in1=o, op0=ALU.mult, op1=ALU.add),
                              [("ps", b)] + xkeys(xtag, dc, c0, n), xkeys(xtag, dc, c0, n))

    def proj(wview, KC, xin, xintag, c0, n, b, M=128, mcol0=0, kc_keys=None):
        def mm(pe):
            r = None
            for kc in range(KC):
                r = pe.matmul(psb(b, n, M), wview[:, kc, mcol0:mcol0 + M], xin[:, kc, c0:c0 + n], start=(kc == 0), stop=(kc == KC - 1))
            return r
        return mm

    def feat_norm512(raw, rawtag, n, gcol, ea, outs, tagn):
        rstd = ea["rstd"][:, 0:n]
        rk_ = (lambda kc: (rawtag + (kc,)) if isinstance(rawtag, tuple) else (rawtag, kc))
        rms_stats(lambda kc: (raw[:, kc, 0:n], [rk_(kc)]), 4, ones512, n, rstd, ("rstd", tagn), [ea["sq0"], ea["sq1"]], ea["sqt"], tagn)
        for kc in range(4):
            for (o_ap, okey) in outs:
                S.add("dve", lambda v, o=o_ap(kc), i=raw[:, kc, 0:n], s=gains[:, gcol + kc:gcol + kc + 1], r=rstd:
                      v.scalar_tensor_tensor(out=o, in0=i, scalar=s, in1=r, op0=ALU.mult, op1=ALU.mult),
                      [("rstd", tagn), rk_(kc), "gains"], okey(kc))

    def dup_rows(dupI, full, c0, n, key):
        b = bank(PS_A)
        S.add("pe", lambda pe, o=psb(b, n), l=dupI, r=full[0:64, c0:c0 + n]: pe.matmul(o, l, r, start=True, stop=True),
              [key, "dupI"], [("ps", b)])
        S.add("act", lambda a, o=full[64:128, c0:c0 + n], i=psum_t[64:128, b * 512:b * 512 + n]: a.copy(out=o, in_=i),
              [("ps", b)], [(key, "dup")])

    def make_dupI(dupI):
        S.add("dve", lambda v, o=dupI[:, 0:64], i=ident_b[0:64, 0:64]: v.tensor_copy(out=o, in_=i), ["ident_b"], ["dupI0"])
        S.add("dve", lambda v, o=dupI[:, 64:128], i=ident_b[0:64, 0:64]: v.tensor_copy(out=o, in_=i), ["ident_b", "dupI0"], ["dupI"])

    def rope_from_ps(b0, b1, n, cs, ccol0, t1, t2, outs):
        S.add("dve", lambda v, o=t1[:, 0:n], i0=psb(b0, n, 64), i1=cs[:, 0, ccol0:ccol0 + n]: v.tensor_tensor(out=o, in0=i0, in1=i1, op=ALU.mult),
              [("ps", b0), "cs"], ["rt1"])
        S.add("dve", lambda v, o=t2[:, 0:n], i0=psb(b1, n, 64), i1=cs[:, 1, ccol0:ccol0 + n]: v.tensor_tensor(out=o, in0=i0, in1=i1, op=ALU.mult),
              [("ps", b1), "cs"], ["rt2"])
        for (o_ap, okeys) in outs:
            S.add("dve", lambda v, o=o_ap, i0=t1[:, 0:n], i1=t2[:, 0:n]: v.tensor_tensor(out=o, in0=i0, in1=i1, op=ALU.add),
                  ["rt1", "rt2"], okeys)

    def ffn_alloc(ea_alloc, ntok):
        ea = {}
        ea["hT"] = ea_alloc.get([128, 11, ntok], BF16)
        ea["sg"] = [ea_alloc.get([128, 512], F32) for _ in range(2)]
        ea["sq0"] = ea_alloc.get([128, 512], BF16)
        ea["sq1"] = ea_alloc.get([128, 512], BF16)
        ea["rstd"] = ea_alloc.get([128, 512], F32)
        ea["sqt"] = ea_alloc.get([128, 512], F32)
        return ea

    MAIN_TILES = [(0, 512), (512, 512), (1024, 128)]
    LIN_TILES = [(0, 384), (384, 384), (768, 384)]
    PREV_TILES = [(0, 512), (512, 512)]

    xn_prev = carve(XN_OFF, [128, 16, NPREV], BF16)
    xT_prev = carve(R0_OFF, [128, 16, NPREV], F32)
    eal = Alloc(E_OFF, RING_OFF)
    ea = ffn_alloc(eal, NT)
    xstage = [eal.get([128, D], F32) for _ in range(2)]

    load_xT(x_prev, NPREV, xT_prev, 0, xstage, "xp")
    norm_to_bf(xT_prev, "xp", PREV_TILES, G_FFN1, xn_prev, "xnp", ea)
    ffn(xT_prev, "xp", xn_prev, "xnp", PREV_TILES, wgu1, wd1, ea)
    norm_to_bf(xT_prev, "xp", PREV_TILES, G_MIX, xn_prev, "xnp", ea)
    S.barrier()
    xn = carve(XN_OFF, [128, 16, NT], BF16)
    xT = carve(R0_OFF, [128, 16, NT], F32)
    pal = Alloc(E_OFF, RING_OFF)
    p_raw = pal.get([128, 4, 512], F32)
    p_cs = pal.get([64, 2, NPREV], F32)
    p_t1 = pal.get([64, 512], F32)
    p_t2 = pal.get([64, 512], F32)
    p_ea = {"sq0": pal.get([128, 512], BF16), "sq1": pal.get([128, 512], BF16),
            "rstd": pal.get([128, 512], F32), "sqt": pal.get([128, 512], F32)}
    p_dupI = pal.get([64, 128], BF16)
    make_dupI(p_dupI)
    dma("sp", p_cs, c_cs_prev.rearrange("a p t -> p a t"), (), ["cs"])
    load_xT(x_own, NOWN, xT, 0, xstage, "x")
    load_xT(x_smp, NSMP, xT, NOWN, xstage, "x")
    wkv = []
    for u_ in range(2):
        wv, wk = wload(wkvq[u_], 4096)
        w4_ = wv.rearrange("p (a b c) -> p a b c", a=2, b=16, c=128)
        wkv += [(w4_[:, 0], wk), (w4_[:, 1], wk)]
    def prev_kv(ti, c0, n):
        for ch in range(4):
            b = bank(PS_A)
            S.add("pe", proj(wkv[ch][0], 16, xn_prev, "xnp", c0, n, b),
                  [wkv[ch][1]] + [k for kc in range(16) for k in xkeys("xnp", kc, c0, n)], [("ps", b)])
            S.add("act", lambda a, o=p_raw[:, ch, 0:n], i=psb(b, n): a.copy(out=o, in_=i), [("ps", b)], [("praw", ch)])
        feat_norm512(p_raw, "praw", n, G_KV, p_ea,
                     [(lambda kc, c0=c0, n=n: ckvp_bf[:, kc, c0:c0 + n], lambda kc, c0=c0: [("ckvp", kc, c0)])], "pkv")
    prev_kv(0, *PREV_TILES[0])
    wv, wk = wload(wkr_d[0], 2048)
    wkr = wv.rearrange("p (a b) -> p a b", b=128)
    for ti, (c0, n) in enumerate(PREV_TILES):
        b0 = bank(PS_A)
        b1 = bank(PS_A)
        rk = [wk] + [k for kc in range(16) for k in xkeys("xnp", kc, c0, n)]
        S.add("pe", proj(wkr, 16, xn_prev, "xnp", c0, n, b0, M=64, mcol0=0), rk, [("ps", b0)])
        S.add("pe", proj(wkr, 16, xn_prev, "xnp", c0, n, b1, M=64, mcol0=64), rk, [("ps", b1)])
        rope_from_ps(b0, b1, n, p_cs, c0, p_t1, p_t2, [(krp_bf[:, c0:c0 + n], [("krp", c0)])])
        dup_rows(p_dupI, krp_full, c0, n, ("krp", c0))
    prev_kv(1, *PREV_TILES[1])
    for zc in range(8):
        if zc % 2 == 0:
            wv, wk = wload(wz_d[zc // 2], 4096)
            wz4 = wv.rearrange("p (a b c) -> p a b c", a=2, b=16, c=128)
        wz = wz4[:, zc % 2]
        b = bank(PS_A)
        S.add("pe", proj(wz, 16, xn_prev, "xnp", NPREV - 16, 16, b),
              [wk] + [k for kc in range(16) for k in xkeys("xnp", kc, NPREV - 16, 16)], [("ps", b)])
        S.add("act", lambda a, o=zhist[:, zc, :], i=psb(b, 16): a.copy(out=o, in_=i), [("ps", b)], [("zhist", zc)])
    rstd3 = carve(E_OFF + 36864, [128, NT], F32)
    for (c0, n) in LIN_TILES:
        rms_stats(lambda kc: (xT[:, kc, c0:c0 + n], xkeys("x", kc, c0, n)), 16, onesD, n, rstd3[:, c0:c0 + n], ("rstd3", c0),
                  [ea["sq0"], ea["sq1"]], ea["sqt"], "xn")
    S.barrier()

    for (c0, n) in LIN_TILES:
        for kc in range(16):
            S.add("dve", lambda v, o=xn[:, kc, c0:c0 + n], i=xT[:, kc, c0:c0 + n], s=gains[:, G_FFN1 + kc:G_FFN1 + kc + 1], r=rstd3[:, c0:c0 + n]:
                  v.scalar_tensor_tensor(out=o, in0=i, scalar=s, in1=r, op0=ALU.mult, op1=ALU.mult),
                  ["gains"], xkeys("xn", kc, c0, n))
    ffn(xT, "x", xn, "xn", LIN_TILES, wgu1, wd1, ea)
    norm_to_bf(xT, "x", LIN_TILES, G_MIX, xn, "xn", ea)
    S.barrier()
    spill_t = S.add("sp", lambda e: [e.dma_start(out=hs[:, kc * NT:(kc + 1) * NT], in_=xT[:, kc, :]) for kc in range(16)], (), ["hs"], dma=16)
    spill_t.has_dep = True
    S.pending["dve"] = S.pending.get("dve", set()) | {spill_t}

    r0 = Alloc(R0_OFF, E_OFF)
    oT = r0.get([128, 16, NT], BF16)
    mixedT = r0.get([128, 8, NT], BF16)
    qln = r0.get([128, 4, NT], BF16)
    ckvbf = r0.get([128, 4, NT], BF16)
    pooledT = carve(R0_OFF, [128, 8, NT], BF16)
    me = Alloc(E_OFF, RING_OFF)
    krbf_full = me.get([128, NT], BF16)
    krbf = krbf_full[0:64, :]
    cs = me.get([64, 2, NT], F32)
    m_dupI = me.get([64, 128], BF16)
    make_dupI(m_dupI)
    T_OFF = me.cur
    dma("sp", cs, c_cs_main.rearrange("a p t -> p a t"), (), ["cs"])

    ta = Alloc(T_OFF, RING_OFF)
    raw2 = [ta.get([128, 4, 512], F32) for _ in range(2)]
    kvn2 = [ta.get([128, 4, 512], F32)] * 2
    t1 = ta.get([64, 512], F32)
    t2 = ta.get([64, 512], F32)
    krn2 = [ta.get([64, 512], F32) for _ in range(2)]
    ckvT_out3 = ckvT_out.rearrange("(k p) t -> p k t", p=128)
    m_ea = {"sq0": ta.get([128, 512], BF16), "sq1": ta.get([128, 512], BF16),
            "rstd": ta.get([128, 512], F32), "sqt": ta.get([128, 512], F32)}

    wkv = []
    for u_ in range(2):
        wv, wk = wload(wkvq[u_], 4096)
        w4_ = wv.rearrange("p (a b c) -> p a b c", a=2, b=16, c=128)
        wkv += [(w4_[:, 0], wk), (w4_[:, 1], wk)]
    wv, wk = wload(wkr_d[0], 2048)
    wkr = (wv.rearrange("p (a b) -> p a b", b=128), wk)
    for ti, (c0, n) in enumerate(MAIN_TILES):
        for ch in range(4):
            b = bank(PS_A)
            S.add("pe", proj(wkv[ch][0], 16, xn, "xn", c0, n, b),
                  [wkv[ch][1]] + [k for kc in range(16) for k in xkeys("xn", kc, c0, n)], [("ps", b)])
            S.add("act", lambda a, o=raw2[ti % 2][:, ch, 0:n], i=psb(b, n): a.copy(out=o, in_=i), [("ps", b)], [("rawkv", ti % 2, ch)])
        kvn = kvn2[ti % 2]
        krn = krn2[ti % 2]
        feat_norm512(raw2[ti % 2], ("rawkv", ti % 2), n, G_KV, m_ea,
                     [(lambda kc, n=n, kvn=kvn: kvn[:, kc, 0:n], lambda kc, ti=ti: [("kvn", 0, kc)])], "kv")
        for kc in range(4):
            S.add("act", lambda a, o=ckvbf[:, kc, c0:c0 + n], i=kvn[:, kc, 0:n]: a.copy(out=o, in_=i),
                  [("kvn", 0, kc)], [("ckvbf", kc, c0)])
        b0 = bank(PS_A)
        b1 = bank(PS_A)
        rk = [wkr[1]] + [k for kc in range(16) for k in xkeys("xn", kc, c0, n)]
        S.add("pe", proj(wkr[0], 16, xn, "xn", c0, n, b0, M=64, mcol0=0), rk, [("ps", b0)])
        S.add("pe", proj(wkr[0], 16, xn, "xn", c0, n, b1, M=64, mcol0=64), rk, [("ps", b1)])
        rope_from_ps(b0, b1, n, cs, c0, t1, t2, [(krn[:, 0:n], [("krn", ti % 2)]), (krbf[:, c0:c0 + n], [("krbf", c0)])])
        if c0 < NOWN:
            dup_rows(m_dupI, krbf_full, c0, n, ("krbf", c0))
        dma("sp", ckvT_out3[:, :, c0:c0 + n], kvn[:, :, 0:n], [("kvn", 0, kc) for kc in range(4)], [])
        dma("sp", krT_out[:, c0:c0 + n], krn[:, 0:n], [("krn", ti % 2)], [])
    wq = []
    for u_ in range(2):
        wv, wk = wload(wkvq[2 + u_], 4096)
        w4_ = wv.rearrange("p (a b c) -> p a b c", a=2, b=16, c=128)
        wq += [(w4_[:, 0], wk), (w4_[:, 1], wk)]
    for ti, (c0, n) in enumerate(MAIN_TILES):
        for ch in range(4):
            b = bank(PS_A)
            S.add("pe", proj(wq[ch][0], 16, xn, "xn", c0, n, b),
                  [wq[ch][1]] + [k for kc in range(16) for k in xkeys("xn", kc, c0, n)], [("ps", b)])
            S.add("act", lambda a, o=raw2[(ti + 1) % 2][:, ch, 0:n], i=psb(b, n): a.copy(out=o, in_=i), [("ps", b)], [("rawq", ti % 2, ch)])
        feat_norm512(raw2[(ti + 1) % 2], ("rawq", ti % 2), n, G_Q, m_ea,
                     [(lambda kc, c0=c0, n=n: qln[:, kc, c0:c0 + n], lambda kc, c0=c0: [("qln", kc, c0)])], "ql")

    S.barrier()
    tb_ = Alloc(T_OFF, RING_OFF)
    zxb = [tb_.get([128, 1232], F32) for _ in range(2)]
    ppa = tb_.get([128, 1232], F32)
    ppb = tb_.get([128, 1232], F32)
    fx16 = tb_.get([128, 16], F32)
    SEG = [(0, 1024, 0)] + [(1040 + 48 * j, 32, NOWN + 32 * j) for j in range(4)]
    for zc in range(8):
        g = zc // 2
        w = 2 << g
        if zc % 2 == 0:
            wv, wk = wload(wz_d[zc // 2], 4096)
            wz4 = wv.rearrange("p (a b c) -> p a b c", a=2, b=16, c=128)
        wz = wz4[:, zc % 2]
        zx = zxb[zc % 2]
        zk = ("zx", zc % 2)
        S.add("dve", lambda v, o=zx[:, 0:16], i=zhist[:, zc, :]: v.tensor_copy(out=o, in_=i), [("zhist", zc)], [(zk, "h0")])
        dma("sp", zx[:, 1040:1232].rearrange("p (j t) -> p j t", t=48)[:, :, 1:16],
            state_pool[zc * 128:(zc + 1) * 128, :].rearrange("p (j t) -> p j t", t=15), (), [(zk, "h", j) for j in range(4)])
        for (c0, n) in MAIN_TILES:
            b = bank(PS_A)
            S.add("pe", proj(wz, 16, xn, "xn", c0, n, b),
                  [wk] + [k for kc in range(16) for k in xkeys("xn", kc, c0, n)], [("ps", b)])
            if c0 < NOWN:
                S.add("act", lambda a, o=zx[:, 16 + c0:16 + c0 + n], i=psb(b, n): a.copy(out=o, in_=i), [("ps", b)], [(zk, "z", c0)])
            else:
                dst = zx[:, 1040:1232].rearrange("p (j t) -> p j t", t=48)[:, :, 16:48]
                src = psb(b, n).rearrange("p (j t) -> p j t", t=32)
                S.add("act", lambda a, o=dst, i=src: a.copy(out=o, in_=i), [("ps", b)], [(zk, "z", c0)])
        allz = [(zk, "h0")] + [(zk, "h", j) for j in range(4)] + [(zk, "z", c0) for (c0, n) in MAIN_TILES]
        dma("sp", poolT_own[zc * 128:(zc + 1) * 128, :], zx[:, 1025:1040], allz, [])
        dma("sp", poolT_smp[zc * 128:(zc + 1) * 128, :].rearrange("p (j t) -> p j t", t=15),
            zx[:, 1040:1232].rearrange("p (j t) -> p j t", t=48)[:, :, 33:48], allz, [])
        cur, curk = zx, allz
        k = 1
        tog = 0
        while k < w:
            nxt = ppa if tog == 0 else ppb
            nk = ["ppa"] if tog == 0 else ["ppb"]
            S.add("dve", lambda v, o=nxt[:, k:1232], i0=cur[:, k:1232], i1=cur[:, 0:1232 - k]: v.tensor_tensor(out=o, in0=i0, in1=i1, op=ALU.add),
                  curk, nk)
            S.add("dve", lambda v, o=nxt[:, 0:k], i=cur[:, 0:k]: v.tensor_copy(out=o, in_=i), curk, [nk[0] + "h"])
            cur, curk = nxt, nk + [nk[0] + "h"]
            tog ^= 1
            k *= 2
        for si_, (h0, nt, tc) in enumerate(SEG):
            S.add("dve", lambda v, o=pooledT[:, zc, tc:tc + nt], i0=cur[:, h0 + 16:h0 + 16 + nt], i1=zx[:, h0 + 16:h0 + 16 + nt], sc=1.0 / w:
                  v.scalar_tensor_tensor(out=o, in0=i0, scalar=sc, in1=i1, op0=ALU.mult, op1=ALU.subtract),
                  curk + allz, [("pooled", zc, tc)])
        S.add("dve", lambda v, o=fx16, i0=cur[:, 16:32], i1=rc16[:, g, :]:
              v.tensor_tensor(out=o, in0=i0, in1=i1, op=ALU.mult), curk + ["rc16"], ["fix16"])
        S.add("dve", lambda v, o=pooledT[:, zc, 0:16], i0=fx16, i1=zx[:, 16:32]: v.tensor_tensor(out=o, in0=i0, in1=i1, op=ALU.subtract),
              ["fix16"] + allz + [("pooled", zc, 0)], [("pooled", zc, 0)])
    wv, wpk = wload(wpool[0], 2048)
    wp4 = wv.rearrange("p (a b c) -> p a b c", a=4, b=2, c=256)
    for (c0, n) in MAIN_TILES:
        for g in range(4):
            for dd in range(2):
                b = bank(PS_A)

                def mm(pe, g=g, dd=dd, b=b, c0=c0, n=n):
                    r = None
                    for cc in range(2):
                        r = pe.matmul(psb(b, n), wp4[:, g, cc, dd * 128:(dd + 1) * 128], pooledT[:, 2 * g + cc, c0:c0 + n], start=(cc == 0), stop=(cc == 1))
                    return r
                rk = [wpk] + [("pooled", 2 * g + cc, tc) for cc in range(2) for tc in ([0, 512] if c0 < NOWN else [NOWN + 32 * j for j in range(4)])]
                S.add("pe", mm, rk, [("ps", b)])
                S.add("dve", lambda v, o=mixedT[:, 2 * g + dd, c0:c0 + n], i=psb(b, n), s=gains[:, G_PS + 2 * g + dd:G_PS + 2 * g + dd + 1]:
                      v.tensor_scalar_mul(out=o, in0=i, scalar1=s), [("ps", b), "gains"], [("mixed", 2 * g + dd, c0)])
    S.barrier()
    if DBG:
        dma("sp", dbg_pooled, pooledT.rearrange("p a b -> p (a b)"), (), [])
        dma("sp", dbg_mixed, mixedT.rearrange("p a b -> p (a b)"), (), [])
        S.barrier()

    aa = Alloc(T_OFF, RING_OFF)
    WS = []
    for _ in range(2):
        WS.append({"qn": aa.get([128, 1024], BF16), "qr": aa.get([128, 1024], BF16),
                   "kn": aa.get([128, 2048], BF16), "v": aa.get([128, 16, 128], BF16)})
    pts = [aa.get([128, 512], BF16) for _ in range(4)]
    rs = aa.get([128, 512], F32)
    cs2 = aa.get([128, 1024], F32)
    dma("sp", cs2[0:64, :], c_cs_main[0, :, 0:1024], (), ["cs2a"])
    dma("sp", cs2[64:128, :], c_cs_main[1, :, 0:1024], (), ["cs2b"])
    A2_OFF = aa.cur
    PS_S = [0, 1, 2]
    PS_O = [3]
    PS_SUM = [4]
    PS_P = [5, 6, 7]
    bo_sb = [aa.get([128, 512], F32) for _ in range(2)]
    bs_sb = [aa.get([128, 512], F32) for _ in range(2)]
    fin_i = [0]
    pt_i = [0]

    def attend(ws, wsk, h, qcol0, nq, ocol0, blocks):
        LOOK = 2
        bo = bank(PS_O)
        bs = bank(PS_SUM)
        nb = len(blocks)
        ptinfo = {}
        for it in range(nb + LOOK):
            if it < nb:
                (kn_ap, kr_ap, v_ap, nk, bias_ap, qoff, mask, keys) = blocks[it]
                nqq = nq - qoff
                b = bank(PS_S)

                def mm(pe, b=b, kn_ap=kn_ap, kr_ap=kr_ap, nk=nk, qoff=qoff, nqq=nqq):
                    pe.matmul(psb(b, nqq, nk), kn_ap, ws["qn"][:, qcol0 + qoff:qcol0 + qoff + nqq], start=True, stop=False)
                    return pe.matmul(psb(b, nqq, nk), kr_ap, ws["qr"][:, qcol0 + qoff:qcol0 + qoff + nqq], start=False, stop=True)
                S.add("pe", mm, keys + [(wsk, "qn"), (wsk, "qr")], [("ps", b)])
                pi = pt_i[0] % len(pts)
                pt_i[0] += 1
                pt = pts[pi]
                ptinfo[it] = (pi, pt)
                if bias_ap is None:
                    S.add("act", lambda a, o=pt[0:nk, 0:nqq], i=psb(b, nqq, nk): a.activation(out=o, in_=i, func=AF.Exp, scale=SM_SCALE),
                          [("ps", b)], [("pt", pi)])
                else:
                    S.add("act", lambda a, o=pt[0:nk, 0:nqq], i=psb(b, nqq, nk), bb=bias_ap: a.activation(out=o, in_=i, func=AF.Exp, bias=bb, scale=SM_SCALE),
                          [("ps", b), "prevb"], [("pt", pi)])
                if mask:
                    S.add("dve", lambda v, o=pt[64:128, 0:64]: v.memset(o, 0.0), [("pt", pi)], [("pt", pi)])
            bi = it - LOOK
            if bi >= 0:
                (kn_ap, kr_ap, v_ap, nk, bias_ap, qoff, mask, keys) = blocks[bi]
                nqq = nq - qoff
                pi, pt = ptinfo[bi]
                S.add("pe", lambda pe, o=psb(bo, nq)[:, qoff:nq], l=v_ap, r=pt[0:nk, 0:nqq], st=(bi == 0), sp_=(bi == nb - 1):
                      pe.matmul(o, l, r, start=st, stop=sp_), [("pt", pi), (wsk, "v")] + keys, [("ps", bo)])
                S.add("pe", lambda pe, o=psb(bs, nq)[:, qoff:nq], l=ones1[0:nk, :], r=pt[0:nk, 0:nqq], st=(bi == 0), sp_=(bi == nb - 1):
                      pe.matmul(o, l, r, start=st, stop=sp_), [("pt", pi), "ones1"], [("ps", bs)])
        ai = fin_i[0] % 2
        fin_i[0] += 1
        S.add("act", lambda a, o=bs_sb[ai][:, 0:nq], i=psb(bs, nq): a.copy(out=o, in_=i), [("ps", bs)], [("bs_sb", ai)])
        S.add("act", lambda a, o=bo_sb[ai][:, 0:nq], i=psb(bo, nq): a.copy(out=o, in_=i), [("ps", bo)], [("bo_sb", ai)])
        def fin():
            S.add("dve", lambda v, o=rs[:, 0:nq], i=bs_sb[ai][:, 0:nq]: v.reciprocal(out=o, in_=i), [("bs_sb", ai)], ["rs"])
            S.add("dve", lambda v, o=oT[:, h, ocol0:ocol0 + nq], i0=bo_sb[ai][:, 0:nq], i1=rs[:, 0:nq]: v.tensor_tensor(out=o, in0=i0, in1=i1, op=ALU.mult),
                  [("bo_sb", ai), "rs"], [("oT", h, ocol0)])
        return fin

    head_full = {}

    def head_weights(h):
        wv, wk = wload(watt[h], 2048)
        w4 = wv.rearrange("p (a b c) -> p a b c", a=4, b=4, c=128)
        return w4, wk

    def head_weights_s(h):
        wv, wk = wload(watt_s[h], 1536)
        head_full[h] = wv
        w4 = wv[:, 0:1024].rearrange("p (a b c) -> p a b c", a=2, b=4, c=128)
        return w4, wk

    def q_proj(w4, wk, ws, wsk, c0, n, dcol0, cscol0):
        rk = [wk] + [("qln", kc, c) for kc in range(4) for c in range(c0 - c0 % 128, c0 + n, 128)] + \
             [("qln", kc, cc) for kc in range(4) for cc in (0, 512, 1024)]
        b = bank(PS_P)
        S.add("pe", proj(w4[:, 0], 4, qln, "qln", c0, n, b), rk, [("ps", b)])
        S.add("act", lambda a, o=ws["qn"][:, dcol0:dcol0 + n], i=psb(b, n): a.copy(out=o, in_=i), [("ps", b)], [(wsk, "qn")])
        b0 = bank(PS_P)
        S.add("pe", proj(w4[:, 1], 4, qln, "qln", c0, n, b0, M=128, mcol0=0), rk, [("ps", b0)])
        S.add("dve", lambda v, o=ws["qr"][:, dcol0:dcol0 + n], i0=psb(b0, n), i1=cs2[:, cscol0:cscol0 + n]: v.tensor_tensor(out=o, in0=i0, in1=i1, op=ALU.mult),
              [("ps", b0), "cs2a", "cs2b"], [(wsk, "qr")])

    def k_proj(w4, wk, ws, wsk, src, srckeys, c0, n, dcol0):
        b = bank(PS_P)
        S.add("pe", proj(w4[:, 2], 4, src, None, c0, n, b), [wk] + srckeys, [("ps", b)])
        S.add("dve", lambda v, o=ws["kn"][:, dcol0:dcol0 + n], i=psb(b, n): v.tensor_copy(out=o, in_=i), [("ps", b)], [(wsk, "kn")])

    def v_proj(w4, wk, ws, wsk, src, srckeys, c0, nk, blk):
        b = bank(PS_P)

        def mm(pe):
            r = None
            for kc in range(4):
                r = pe.matmul(psb(b, 128, nk), src[:, kc, c0:c0 + nk], w4[:, 3, kc, :], start=(kc == 0), stop=(kc == 3))
            return r
        S.add("pe", mm, [wk] + srckeys, [("ps", b)])
        S.add("act", lambda a, o=ws["v"][0:nk, blk, :], i=psb(b, 128, nk): a.copy(out=o, in_=i), [("ps", b)], [(wsk, "v")])

    def v_proj4(w4, wk, ws, wsk, src, c0, blk0):
        b = bank(PS_P)

        def mm(pe):
            r = None
            for j in range(4):
                for kc in range(4):
                    r = pe.matmul(psb(b)[:, j * 128:(j + 1) * 128], src[:, kc, c0 + j * 128:c0 + (j + 1) * 128], w4[:, 3, kc, :],
                                  start=(kc == 0), stop=(kc == 3))
            return r
        S.add("pe", mm, [wk], [("ps", b)])
        S.add("act", lambda a, o=ws["v"][:, blk0:blk0 + 4, :], i=psb(b).rearrange("p (a b) -> p a b", b=128): a.copy(out=o, in_=i),
              [("ps", b)], [(wsk, "v")])

    ckvp_keys = []
    own_keys = []
    def head_proj(h):
        ws = WS[h % 2]
        wsk = ("ws", h % 2)
        w4, wk = head_weights(h)
        for (c0, n) in [(0, 512), (512, 512)]:
            q_proj(w4, wk, ws, wsk, c0, n, c0, c0)
        for (c0, n) in [(0, 512), (512, 512)]:
            k_proj(w4, wk, ws, wsk, ckvp_bf, [], c0, n, c0)
            k_proj(w4, wk, ws, wsk, ckvbf, [], c0, n, 1024 + c0)
        for g4 in range(2):
            v_proj4(w4, wk, ws, wsk, ckvp_bf, g4 * 512, g4 * 4)
        for g4 in range(2):
            v_proj4(w4, wk, ws, wsk, ckvbf, g4 * 512, 8 + g4 * 4)

    def head_attend(h, qt):
        ws = WS[h % 2]
        wsk = ("ws", h % 2)
        q0 = qt * 512
        blocks = []
        for blk in range(8):
            blocks.append((ws["kn"][:, blk * 128:(blk + 1) * 128], krp_full[:, blk * 128:(blk + 1) * 128], ws["v"][:, blk, :], 128,
                           prevb[:, 0:1], 0, False, [(wsk, "kn")]))
        for blk in range(4 * qt):
            blocks.append((ws["kn"][:, 1024 + blk * 128:1024 + (blk + 1) * 128], krbf_full[:, blk * 128:(blk + 1) * 128], ws["v"][:, 8 + blk, :], 128,
                           None, 0, False, [(wsk, "kn")]))
        for j in range(4):
            blk = 4 * qt + j
            blocks.append((ws["kn"][:, 1024 + blk * 128:1024 + (blk + 1) * 128], krbf_full[:, blk * 128:(blk + 1) * 128], ws["v"][:, 8 + blk, :], 128,
                           None, j * 128, True, [(wsk, "kn")]))
        return attend(ws, wsk, h, q0, 512, q0, blocks)

    head_proj(0)
    for h in range(16):
        fin0 = head_attend(h, 0)
        if h + 1 < 16:
            head_proj(h + 1)
        fin0()
        fin1 = head_attend(h, 1)
        fin1()
    S.barrier()

    SC0 = NOWN
    sa = Alloc(E_OFF + 2304 + 9216, RING_OFF)
    qabs = sa.get([128, 4, 4, 512], BF16)
    qrs = sa.get([64, 4, 512], BF16)
    qn_s = [sa.get([128, 128], BF16) for _ in range(2)]
    s_t1 = sa.get([64, 128], F32)
    s_t2 = sa.get([64, 128], F32)
    S2_OFF = sa.cur
    PS_Q = [0, 1, 2, 3]
    s1 = {}

    def s1_a(h):
        w4, wk = head_weights_s(h)
        wukT = head_full[h][:, 1024:1536]
        rk = [wk] + [("qln", kc, SC0) for kc in range(4)]
        b = bank(PS_Q)
        S.add("pe", proj(w4[:, 0], 4, qln, "qln", SC0, 128, b), rk, [("ps", b)])
        qs = qn_s[h % 2]
        S.add("act", lambda a_, o=qs, i=psb(b, 128): a_.copy(out=o, in_=i), [("ps", b)], [("qn_s", h % 2)])
        b0 = bank(PS_Q)
        S.add("pe", proj(w4[:, 1], 4, qln, "qln", SC0, 128, b0, M=64, mcol0=0), rk, [("ps", b0)])
        b1 = bank(PS_Q)
        S.add("pe", proj(w4[:, 1], 4, qln, "qln", SC0, 128, b1, M=64, mcol0=64), rk, [("ps", b1)])
        S.add("dve", lambda v, o=s_t1, i0=psb(b0, 128, 64), i1=cs[:, 0, SC0:SC0 + 128]: v.tensor_tensor(out=o, in0=i0, in1=i1, op=ALU.mult),
              [("ps", b0), "cs"], ["s_t1"])
        S.add("dve", lambda v, o=s_t2, i0=psb(b1, 128, 64), i1=cs[:, 1, SC0:SC0 + 128]: v.tensor_tensor(out=o, in0=i0, in1=i1, op=ALU.mult),
              [("ps", b1), "cs"], ["s_t2"])
        S.add("dve", lambda v, o=qrs[:, :, h * 32:(h + 1) * 32], i0=s_t1.rearrange("p (b q) -> p b q", q=32), i1=s_t2.rearrange("p (b q) -> p b q", q=32):
              v.tensor_tensor(out=o, in0=i0, in1=i1, op=ALU.add), ["s_t1", "s_t2"], [("qrs", h)])
        s1[h] = (wukT, wk, qs)

    def s1_b(h):
        wukT, wk, qs = s1[h]
        b2 = bank(PS_Q)

        def mmq(pe, b2=b2, wukT=wukT, qs=qs):
            r = None
            for c in range(4):
                r = pe.matmul(psb(b2)[:, c * 128:(c + 1) * 128], wukT[:, c * 128:(c + 1) * 128], qs, start=True, stop=True)
            return r
        S.add("pe", mmq, [wk, ("qn_s", h % 2)], [("ps", b2)])
        S.add("act", lambda a_, o=qabs[:, :, :, h * 32:(h + 1) * 32], i=psb(b2).rearrange("p (c b q) -> p c b q", c=4, b=4): a_.copy(out=o, in_=i),
              [("ps", b2)], [("qabs", h)])

    s1_a(0)
    for h in range(16):
        if h + 1 < 16:
            s1_a(h + 1)
        s1_b(h)
    S.barrier()

    sb_ = Alloc(E_OFF + 2304, S2_OFF if False else RING_OFF)
    sb_ = Alloc(E_OFF + 2304, E_OFF + 2304 + 9216)
    sc_ = Alloc(S2_OFF, RING_OFF)

    def sget(shape, dt):
        try:
            return sb_.get(shape, dt)
        except AssertionError:
            return sc_.get(shape, dt)
    ctok2 = [sget([128, 4, 512], BF16) for _ in range(2)]
    cacheT2 = [sget([128, 4, 512], BF16) for _ in range(2)]
    ckrT2 = [sget([64, 512], BF16) for _ in range(2)]
    new_tok = sget([32, 512], BF16)
    pts = [sget([128, 512], BF16) for _ in range(3)]
    rs = sget([128, 512], F32)
    PS_S2 = [0, 1]
    PS_PV = [2, 3, 4, 5]
    PS_SM = 6
    PS_T = [7]
    pt_j = [0]

    def cache_load(u):
        bb, hf = u // 2, u % 2
        dma("pool", ctok2[u % 2], cache_ckv[bb, hf * 512:(hf + 1) * 512, :].rearrange("(k p) c -> p k c", p=128), (), [("ctok", u % 2)])
        dma("pool", cacheT2[u % 2], cache_ckvT[bb].rearrange("(k p) t -> p k t", p=128)[:, :, hf * 512:(hf + 1) * 512], (), [("cacheT", u % 2)])
        dma("pool", ckrT2[u % 2], cache_krT[bb][:, hf * 512:(hf + 1) * 512], (), [("ckrT", u % 2)])

    def cache_transposes(u):
        pass

    cache_load(0)
    cache_load(1)
    cache_transposes(0)
    for bi_ in range(4):
        tc = NOWN + 32 * bi_
        b = bank(PS_T)
        for c in range(4):
            S.add("pe", lambda pe, o=psb_bf(b, 1024, 32)[:, c * 128:(c + 1) * 128], i=ckvbf[:, c, tc:tc + 32]: pe.transpose(o, i, ident_b),
                  ["ident_b"], [("ps", b)])
        S.add("dve", lambda v, o=new_tok, i=psb_bf(b, 512, 32): v.tensor_copy(out=o, in_=i), [("ps", b)], ["new_tok"])
        nb = 9
        info = {}
        for it in range(nb + 1):
            if it < nb:
                nk = 128 if it < 8 else 32
                u = 2 * bi_ + it // 4
                ub, lb = u % 2, it % 4
                if it == 2:
                    cache_transposes(2 * bi_ + 1)
                if it == 6 and bi_ < 3:
                    cache_transposes(2 * bi_ + 2)
                bsx = bank(PS_S2)

                def mms(pe, it=it, nk=nk, bsx=bsx, bi_=bi_, tc=tc, ub=ub, lb=lb):
                    for c in range(4):
                        l = cacheT2[ub][:, c, lb * 128:(lb + 1) * 128] if it < 8 else ckvbf[:, c, tc:tc + 32]
                        pe.matmul(psb(bsx, 512, nk), l, qabs[:, c, bi_, :], start=(c == 0), stop=False)
                    l = ckrT2[ub][:, lb * 128:(lb + 1) * 128] if it < 8 else krbf[:, tc:tc + 32]
                    return pe.matmul(psb(bsx, 512, nk), l, qrs[:, bi_, :], start=False, stop=True)
                rkeys = ([("cacheT", ub), ("ckrT", ub)] if it < 8 else []) + [("qabsb", bi_)]
                S.add("pe", mms, rkeys, [("ps", bsx)])
                pi = pt_j[0] % 3
                pt_j[0] += 1
                pt = pts[pi]
                info[it] = (pi, pt, nk, ub, lb)
                S.add("act", lambda a_, o=pt[0:nk, :], i=psb(bsx, 512, nk): a_.activation(out=o, in_=i, func=AF.Exp, scale=SM_SCALE),
                      [("ps", bsx)], [("pt", pi)])
            j = it - 1
            if j >= 0:
                pi, pt, nk, ub, lb = info[j]

                def mmpv(pe, j=j, nk=nk, pt=pt, ub=ub, lb=lb):
                    for c in range(4):
                        l = ctok2[ub][:, lb, c * 128:(c + 1) * 128] if j < 8 else new_tok[:, c * 128:(c + 1) * 128]
                        pe.matmul(psb(PS_PV[c]), l, pt[0:nk, :], start=(j == 0), stop=(j == nb - 1))
                    return pe.matmul(psb(PS_SM), ones1[0:nk, :], pt[0:nk, :], start=(j == 0), stop=(j == nb - 1))
                S.add("pe", mmpv, [("pt", pi), ("ctok", ub), "new_tok", "ones1"], [("ps", PS_PV[c]) for c in range(4)] + [("ps", PS_SM)])
                if j in (3, 7):
                    uu = 2 * bi_ + j // 4
                    if uu + 2 < 8:
                        cache_load(uu + 2)
        S.add("dve", lambda v, o=rs, i=psb(PS_SM): v.reciprocal(out=o, in_=i), [("ps", PS_SM)], ["rs"])
        for c in range(4):
            S.add("dve", lambda v, o=qabs[:, c, bi_, :], i0=psb(PS_PV[c]), i1=rs: v.tensor_tensor(out=o, in0=i0, in1=i1, op=ALU.mult),
                  [("ps", PS_PV[c]), "rs"], [("olat", bi_, c), ("qabsb", bi_)])
    for h in range(16):
        if h % 8 == 0:
            wv_, wk = wload(wuv_s[h // 8], 4096)
            wuv8 = wv_.rearrange("p (a b c) -> p a b c", a=8, b=4, c=128)
        b = bank(PS_S2)

        def mmo(pe, b=b, wuv8=wuv8, h=h):
            r = None
            for c in range(4):
                r = pe.matmul(psb(b, 128), wuv8[:, h % 8, c, :], qabs[:, c, :, h * 32:(h + 1) * 32], start=(c == 0), stop=(c == 3))
            return r
        S.add("pe", mmo, [wk] + [("olat", bb, c) for bb in range(4) for c in range(4)], [("ps", b)])
        S.add("act", lambda a_, o=oT[:, h, SC0:SC0 + 128], i=psb(b, 128): a_.copy(out=o, in_=i), [("ps", b)], [("oT", h, SC0)])
    S.barrier()

    if DBG:
        dma("sp", dbg_o, oT.rearrange("p a b -> p (a b)"), (), [])
        S.barrier()
    ma = Alloc(E_OFF, RING_OFF)
    mT = ma.get([128, 16, NT], BF16)
    sga = [ma.get([128, 512], F32) for _ in range(2)]
    sgb = [ma.get([128, 512], F32) for _ in range(2)]
    mt1 = [ma.get([128, 512], F32) for _ in range(2)]
    mi = [0]
    for dc in range(16):
        wv, wgk = wload(wgate[dc], 4096)
        wg4 = wv.rearrange("p (a b c) -> p a b c", a=2, b=16, c=128)
        wv, wmk = wload(wmrg[dc], 3072)
        wo3 = wv[:, 0:2048].rearrange("p (b c) -> p b c", c=128)
        wpo3 = wv[:, 2048:3072].rearrange("p (b c) -> p b c", c=128)
        for (c0, n) in LIN_TILES:
            i2 = mi[0] % 2
            mi[0] += 1
            bga = bank(PS_A)
            S.add("pe", proj(wg4[:, 0], 16, xn, None, c0, n, bga), [wgk], [("ps", bga)])
            bgb = bank(PS_A)
            S.add("pe", proj(wg4[:, 1], 16, xn, None, c0, n, bgb), [wgk], [("ps", bgb)])
            ba = bank(PS_B)
            S.add("pe", proj(wpo3, 8, mixedT, None, c0, n, ba), [wmk], [("ps", ba)])
            bb = bank(PS_C)
            S.add("pe", proj(wo3, 16, oT, None, c0, n, bb), [wmk], [("ps", bb)])
            S.add("act", lambda a, o=sga[i2][:, 0:n], i=psb(bga, n): a.activation(out=o, in_=i, func=AF.Sigmoid), [("ps", bga)], [("sga", i2)])
            S.add("act", lambda a, o=sgb[i2][:, 0:n], i=psb(bgb, n): a.activation(out=o, in_=i, func=AF.Sigmoid), [("ps", bgb)], [("sgb", i2)])
            S.add("dve", lambda v, o=mt1[i2][:, 0:n], i0=sga[i2][:, 0:n], i1=psb(ba, n): v.tensor_tensor(out=o, in0=i0, in1=i1, op=ALU.mult),
                  [("sga", i2), ("ps", ba)], [("mt1", i2)])
            S.add("dve", lambda v, o=sgb[i2][:, 0:n], i0=sgb[i2][:, 0:n], i1=psb(bb, n): v.tensor_tensor(out=o, in0=i0, in1=i1, op=ALU.mult),
                  [("sgb", i2), ("ps", bb)], [("sgb", i2)])
            S.add("dve", lambda v, o=mT[:, dc, c0:c0 + n], i0=mt1[i2][:, 0:n], i1=sgb[i2][:, 0:n]: v.tensor_tensor(out=o, in0=i0, in1=i1, op=ALU.add),
                  [("mt1", i2), ("sgb", i2)], [("mT", dc, c0)])
    S.barrier()
    if DBG:
        dma("sp", dbg_m, mT.rearrange("p a b -> p (a b)"), (), [])
        S.barrier()
    for kc in range(16):
        dma("sp", xT[:, kc, :], hs[:, kc * NT:(kc + 1) * NT], (), xkeys("x", kc, 0, NT))
    for up in range(8):
        wv, wk = wload(wout[up], 4096)
        w4o = wv.rearrange("p (a b c) -> p a b c", a=2, b=16, c=128)
        for d2 in range(2):
            dc = up * 2 + d2
            for (c0, n) in LIN_TILES:
                b = bank(PS_A)
                S.add("pe", proj(w4o[:, d2], 16, mT, None, c0, n, b), [wk], [("ps", b)])
                S.add("dve", lambda v, o=xT[:, dc, c0:c0 + n], i0=psb(b, n): v.tensor_tensor(out=o, in0=i0, in1=o, op=ALU.add),
                      [("ps", b)] + xkeys("x", dc, c0, n), xkeys("x", dc, c0, n))
    S.barrier()

    if DBG:
        dma("sp", dbg_x1, xT.rearrange("p a b -> p (a b)"), (), [])
        S.barrier()
    eal2 = Alloc(E_OFF, RING_OFF)
    ea2 = ffn_alloc(eal2, NT)
    ystage = [eal2.get([128, D], F32) for _ in range(2)]
    norm_to_bf(xT, "x", LIN_TILES, G_FFN2, xn, "xn", ea2)
    ffn(xT, "x", xn, "xn", LIN_TILES, wgu2, wd2, ea2)
    S.barrier()

    fa = Alloc(E_OFF, RING_OFF)
    f_ea = {"sq0": fa.get([128, 512], BF16), "sq1": fa.get([128, 512], BF16),
            "rstd": fa.get([128, 512], F32), "sqt": fa.get([128, 512], F32)}
    yT = [fa.get([128, 16, 384], F32) for _ in range(2)]
    y_own3 = y_own.rearrange("(k p) t -> p k t", p=128)
    y_smp3 = y_smp.rearrange("(k p) t -> p k t", p=128)
    FIN_TILES = [(0, 384), (384, 384), (768, 256), (1024, 128)]
    for ti, (c0, n) in enumerate(FIN_TILES):
        rstd = f_ea["rstd"][:, 0:n]
        rms_stats(lambda kc: (xT[:, kc, c0:c0 + n], []), 16, onesD, n, rstd, ("rstd", "fin"), [f_ea["sq0"], f_ea["sq1"]], f_ea["sqt"], "fin")
        yt = yT[ti % 2]
        for kc in range(16):
            S.add("dve", lambda v, o=yt[:, kc, 0:n], i=xT[:, kc, c0:c0 + n], s=gains[:, G_FIN + kc:G_FIN + kc + 1], r=rstd:
                  v.scalar_tensor_tensor(out=o, in0=i, scalar=s, in1=r, op0=ALU.mult, op1=ALU.mult),
                  [("rstd", "fin"), "gains"], [("yT", ti % 2, kc // 4)])
        for g in range(4):
            n_own = max(0, min(c0 + n, NOWN) - c0)
            lst = []
            if n_own > 0:
                lst.append((y_own3[:, g * 4:(g + 1) * 4, c0:c0 + n_own], yt[:, g * 4:(g + 1) * 4, 0:n_own]))
            if n_own < n:
                lst.append((y_smp3[:, g * 4:(g + 1) * 4, c0 + n_own - NOWN:c0 + n - NOWN], yt[:, g * 4:(g + 1) * 4, n_own:n]))
            S.add("sp", lambda e, lst=lst: [e.dma_start(out=o, in_=i) for (o, i) in lst], [("yT", ti % 2, g)], [], dma=len(lst))

    S.emit_all(nc, stack)
    stack.close()
    return nc


_CACHE = {}


def kernel(x_prompt, x_sample, cache_ckv, cache_krope, state_pool,
           g_ffn1, w1_gate, w1_up, w1_down, g_mix, w_in, g_q_lat, g_kv_lat,
           w_uq, w_uk, w_uv, w_o_attn, w_pool, pool_scale, w_pool_out, w_out,
           g_ffn2, w2_gate, w2_up, w2_down, g_final):
    f = np.float32
    A = lambda a: np.ascontiguousarray(np.asarray(a, dtype=f))
    x_prompt, x_sample = A(x_prompt), A(x_sample)
    cache_ckv, cache_krope, state_pool = A(cache_ckv)[0], A(cache_krope)[0], A(state_pool)[0]
    wgu1, wd1 = prep_ffn(A(w1_gate)[0], A(w1_up)[0], A(w1_down)[0])
    wgu2, wd2 = prep_ffn(A(w2_gate)[0], A(w2_up)[0], A(w2_down)[0])
    win = A(w_in)[0]
    z_c, ql_c, kv_c, kr_c = win[:, 0:1024], win[:, 1024:1536], win[:, 1536:2048], win[:, 2048:2112]
    gA_c, gB_c = win[:, 2112:2112 + 2048], win[:, 2112 + 2048:2112 + 4096]
    kr2 = np.concatenate([kr_c, swap_half(kr_c)], axis=1)
    def pair_units(pc):
        n_ = pc.shape[0]
        return np.ascontiguousarray(pc.reshape(n_ // 2, 2, 128, 2048).transpose(0, 2, 1, 3)).reshape(n_ // 2, 128, 4096)
    wkvq = pair_units(prep_cols(np.concatenate([kv_c, ql_c], axis=1)))
    wkr_h = prep_cols(kr2)
    wz_h = pair_units(prep_cols(z_c))
    wgate = np.ascontiguousarray(np.stack([prep_cols(gA_c), prep_cols(gB_c)], axis=2)).reshape(16, 128, 4096)
    wp = A(w_pool)[0]
    wpool = np.ascontiguousarray(wp.reshape(4, 2, 128, 256).transpose(2, 0, 1, 3)).reshape(1, 128, 2048)
    uq = A(w_uq)[0]
    uk = A(w_uk)[0]
    uv = A(w_uv)[0]
    uqn = uq[:, :, 0:128]
    uqr = uq[:, :, 128:192]
    uqr2 = np.concatenate([uqr, swap_half(uqr)], axis=2)

    def per_head(Wh):
        return np.ascontiguousarray(Wh.reshape(4, 128, 16, 128).transpose(2, 1, 0, 3)).reshape(16, 128, 512)
    watt4 = np.ascontiguousarray(np.stack([per_head(uqn), per_head(uqr2), per_head(uk), per_head(uv)], axis=2)).reshape(16, 128, 2048)
    ukT = np.ascontiguousarray(uk.transpose(1, 2, 0))
    watt = watt4
    watt_s = np.ascontiguousarray(np.concatenate([per_head(uqn), per_head(uqr2), ukT], axis=2))
    wuv_s = np.ascontiguousarray(per_head(uv).reshape(2, 8, 128, 512).transpose(0, 2, 1, 3)).reshape(2, 128, 4096)
    wo_c = prep_cols(A(w_o_attn)[0])
    wpo_c = prep_cols(A(w_pool_out)[0])
    wmrg = np.ascontiguousarray(np.concatenate([wo_c, wpo_c], axis=2))
    wout_c = prep_cols(A(w_out)[0])
    wout = np.ascontiguousarray(wout_c.reshape(8, 2, 128, 2048).transpose(0, 2, 1, 3)).reshape(8, 128, 4096)

    def gcol(g, n):
        return np.asarray(g, dtype=f).reshape(n, 128).T
    gains = np.ascontiguousarray(np.concatenate([
        gcol(A(g_ffn1)[0], 16), gcol(A(g_mix)[0], 16), gcol(A(g_ffn2)[0], 16), gcol(A(g_final), 16),
        gcol(A(g_q_lat)[0], 4), gcol(A(g_kv_lat)[0], 4), gcol(A(pool_scale)[0], 8)], axis=1))
    ident = np.eye(128, dtype=f)
    cs_prev = rope_tables(np.arange(1024))
    pos_s = 1024 + (np.arange(128) % 32)

    shared = dict(wgu1=wgu1, wd1=wd1, wgu2=wgu2, wd2=wd2, wkvq=wkvq, wkr=wkr_h, wz=wz_h, wgate=wgate, wpool=wpool, watt=watt, watt_s=watt_s, wuv_s=wuv_s,
                  wmrg=wmrg, wout=wout, c_ident=ident, c_gains=gains, c_cs_prev=cs_prev)
    in_maps = []
    zeros_prev = np.zeros((D, NPREV), dtype=f)
    for c in range(8):
        b, half = c // 2, c % 2
        pos_o = half * 1024 + np.arange(1024)
        cs_main = np.ascontiguousarray(np.concatenate([rope_tables(pos_o), rope_tables(pos_s)], axis=2))
        rc = np.zeros((4, 16), dtype=f)
        for g, w in enumerate((2, 4, 8, 16)):
            rc[g] = 1.0 / np.minimum(pos_o[:16] + 1, w)
        m = dict(shared)
        m.update(
            x_own=np.ascontiguousarray(x_prompt[b, half * 1024:(half + 1) * 1024].T),
            x_prev=np.ascontiguousarray(x_prompt[b, 0:1024].T) if half == 1 else zeros_prev,
            x_smp=np.ascontiguousarray(x_sample[4 * c:4 * c + 4].reshape(128, D).T),
            cache_ckv=np.ascontiguousarray(cache_ckv[4 * c:4 * c + 4]),
            cache_ckvT=np.ascontiguousarray(cache_ckv[4 * c:4 * c + 4].transpose(0, 2, 1)),
            cache_krT=np.ascontiguousarray(cache_krope[4 * c:4 * c + 4].transpose(0, 2, 1)),
            state_pool=np.ascontiguousarray(state_pool[4 * c:4 * c + 4].reshape(60, 1024).T),
            c_cs_main=cs_main,
            c_rc16=np.ascontiguousarray(np.broadcast_to(rc.reshape(1, 64), (128, 64))).astype(f),
            c_prevb=np.full((128, 1), 0.0 if half == 1 else -1e30, dtype=f),
        )
        in_maps.append(m)

    if "nc" not in _CACHE:
        _CACHE["nc"] = build_program()
    nc = _CACHE["nc"]
    res = run_bass_kernel_spmd(nc, in_maps, core_ids=list(range(8)))
    R = res.results
    _CACHE["last"] = R

    y_prompt = np.zeros((4, 2048, D), f)
    y_sample = np.zeros((32, 32, D), f)
    ckv_p = np.zeros((1, 4, 2048, 512), f)
    kr_p = np.zeros((1, 4, 2048, 64), f)
    pool_p = np.zeros((1, 4, 15, 1024), f)
    ckv_s = np.zeros((1, 32, 32, 512), f)
    kr_s = np.zeros((1, 32, 32, 64), f)
    pool_s = np.zeros((1, 32, 15, 1024), f)
    for c in range(8):
        b, half = c // 2, c % 2
        r = R[c]
        sl = slice(half * 1024, (half + 1) * 1024)
        y_prompt[b, sl] = np.asarray(r["y_own"]).T
        ckvT = np.asarray(r["ckvT_out"])
        krT = np.asarray(r["krT_out"])
        ckv_p[0, b, sl] = ckvT[:, :NOWN].T
        kr_p[0, b, sl] = krT[:, :NOWN].T
        if half == 1:
            pool_p[0, b] = np.asarray(r["poolT_own"]).T
        y_sample[4 * c:4 * c + 4] = np.asarray(r["y_smp"]).T.reshape(4, 32, D)
        ckv_s[0, 4 * c:4 * c + 4] = ckvT[:, NOWN:].T.reshape(4, 32, 512)
        kr_s[0, 4 * c:4 * c + 4] = krT[:, NOWN:].T.reshape(4, 32, 64)
        pool_s[0, 4 * c:4 * c + 4] = np.asarray(r["poolT_smp"]).reshape(1024, 4, 15).transpose(1, 2, 0)
    return (y_prompt, y_sample, ckv_p, kr_p, pool_p, ckv_s, kr_s, pool_s)
```

```python
import numpy as np
from contextlib import ExitStack
import concourse.bass as bass
import concourse.mybir as mybir
from concourse.bass_utils import run_bass_kernel_spmd

F32 = mybir.dt.float32
BF16 = mybir.dt.bfloat16
AF = mybir.ActivationFunctionType
ALU = mybir.AluOpType

D = 2048
DFF = 5632
NOWN = 1024
NSMP = 128
NT = NOWN + NSMP
NPREV = 1024
EPS = 1e-6
SM_SCALE = 192 ** -0.5
ARENA = 211968
RING_SLOTS = 4
SLOT_B = 8192
RING_OFF = ARENA - RING_SLOTS * SLOT_B

ENGS = ["pe", "act", "dve", "pool", "sp"]


class Task:
    __slots__ = ("eng", "emit", "deps", "is_dma", "sem", "val", "prev_val", "has_dep", "ndma")

    def __init__(self, eng, emit, is_dma, ndma):
        self.eng = eng
        self.emit = emit
        self.deps = set()
        self.is_dma = is_dma
        self.ndma = ndma
        self.sem = None
        self.val = 0
        self.prev_val = 0
        self.has_dep = False


class RK:
    __slots__ = ("slot", "u", "state")

    def __init__(self, slot, u, state):
        self.slot, self.u, self.state = slot, u, state

    def __hash__(self):
        return hash(("ring", self.slot))

    def __eq__(self, o):
        return isinstance(o, RK) and o.slot == self.slot

    def check(self):
        assert self.state["u"] - self.u < RING_SLOTS, "stale ring slot use"


class Sched:
    def __init__(self):
        self.tasks = {e: [] for e in ENGS}
        self.lastw = {}
        self.readers = {}
        self.pending = {}
        self.dma_since = []

    def add(self, eng, emit, reads=(), writes=(), dma=0):
        t = Task(eng, emit, dma > 0, dma)
        deps = set()
        for r in reads:
            if isinstance(r, RK):
                r.check()
            w = self.lastw.get(r)
            if w is not None:
                deps.add(w)
        for k in writes:
            w = self.lastw.get(k)
            if w is not None:
                deps.add(w)
            rs = self.readers.get(k)
            if rs:
                deps.update(rs)
        for r in reads:
            self.readers.setdefault(r, []).append(t)
        for k in writes:
            self.lastw[k] = t
            self.readers[k] = []
        if eng in self.pending:
            deps |= self.pending.pop(eng)
        deps.discard(t)
        t.deps = deps
        for d in deps:
            d.has_dep = True
        self.tasks[eng].append(t)
        if t.is_dma:
            self.dma_since.append(t)
        return t

    def barrier(self):
        s = set(self.dma_since)
        for e in ENGS:
            if self.tasks[e]:
                s.add(self.tasks[e][-1])
        for t in s:
            t.has_dep = True
        for e in ENGS:
            self.pending[e] = self.pending.get(e, set()) | s
        self.dma_since = []
        self.lastw.clear()
        self.readers.clear()

    def emit_all(self, nc, stack):
        KD = 8
        csem = {}
        for e in ["pe", "act", "dve"]:
            n = sum(1 for t in self.tasks[e] if t.has_dep)
            ngen = n // 30000 + 1
            csem[e] = [stack.enter_context(nc.semaphore(f"c_{e}_{g}")) for g in range(ngen)]
            cnt = 0
            for t in self.tasks[e]:
                if t.has_dep:
                    t.sem = csem[e][cnt // 30000]
                    t.val = cnt % 30000 + 1
                    cnt += 1
        all_dma = []
        for e in ["pool", "sp"]:
            pool = [stack.enter_context(nc.semaphore(f"d_{e}_{k}")) for k in range(KD)]
            vals = [0] * KD
            for i, t in enumerate(self.tasks[e]):
                assert t.is_dma
                k = i % KD
                t.sem = pool[k]
                t.prev_val = vals[k]
                vals[k] += 16 * t.ndma
                t.val = vals[k]
            all_dma += [(pool[k], vals[k]) for k in range(KD) if vals[k] > 0]

        block = stack.enter_context(nc.Block())

        def run(ename, eng):
            waited = {}

            def wait(sem, val):
                key = id(sem)
                if waited.get(key, 0) < val:
                    eng.wait_ge(sem, val)
                    waited[key] = val

            for t in self.tasks[ename]:
                for d in t.deps:
                    if ename == "pe" and d.eng == "pe":
                        continue
                    wait(d.sem, d.val)
                if t.is_dma and t.prev_val > 0:
                    wait(t.sem, t.prev_val)
                r = t.emit(eng)
                if t.is_dma:
                    assert len(r) == t.ndma, (len(r), t.ndma)
                    for ins in r:
                        ins.then_inc(t.sem, 16)
                elif t.has_dep:
                    r.then_inc(t.sem, 1)
            if ename == "sp":
                for sem, val in all_dma:
                    wait(sem, val)

        @block.tensor
        def _(pe):
            run("pe", pe)

        @block.scalar
        def _(a):
            run("act", a)

        @block.vector
        def _(v):
            run("dve", v)

        @block.gpsimd
        def _(g):
            run("pool", g)

        @block.sync
        def _(sp):
            run("sp", sp)


def prep_cols(W):
    K, N = W.shape
    kc, ncn = K // 128, N // 128
    return np.ascontiguousarray(W.reshape(kc, 128, ncn, 128).transpose(2, 1, 0, 3)).reshape(ncn, 128, kc * 128)


def prep_ffn(wg, wu, wd):
    g = prep_cols(wg)
    u = prep_cols(wu)
    gu = np.ascontiguousarray(np.stack([g, u], axis=2)).reshape(44, 128, 4096)
    wds = []
    for q in range(4):
        c = prep_cols(wd[q * 1408:(q + 1) * 1408, :])
        c = c.reshape(8, 2, 128, 1408).transpose(0, 2, 1, 3).reshape(8, 128, 2816)
        wds.append(c)
    return gu, np.ascontiguousarray(np.concatenate(wds, axis=0))


def swap_half(W):
    return np.concatenate([W[..., 32:64], W[..., 0:32]], axis=-1)


def rope_tables(pos):
    half = 32
    inv = 10000.0 ** (-np.arange(half, dtype=np.float64) * 2.0 / 64)
    ang = pos.astype(np.float64)[:, None] * inv[None, :]
    c, s = np.cos(ang).astype(np.float32), np.sin(ang).astype(np.float32)
    C = np.concatenate([c, c], axis=1).T
    S = np.concatenate([-s, s], axis=1).T
    return np.ascontiguousarray(np.stack([C, S], axis=0)).astype(np.float32)


def build_program():
    nc = bass.Bass("TRN2", target_bir_lowering=False)
    S = Sched()

    def din(name, shape, dt=F32):
        return nc.dram_tensor(name, list(shape), dt, kind="ExternalInput").ap()

    def dout(name, shape, dt=F32):
        return nc.dram_tensor(name, list(shape), dt, kind="ExternalOutput").ap()

    x_own = din("x_own", [D, NOWN])
    x_prev = din("x_prev", [D, NPREV])
    x_smp = din("x_smp", [D, NSMP])
    cache_ckv = din("cache_ckv", [4, 1024, 512])
    cache_ckvT = din("cache_ckvT", [4, 512, 1024])
    cache_krT = din("cache_krT", [4, 64, 1024])
    state_pool = din("state_pool", [1024, 60])
    wgu1 = din("wgu1", [44, 128, 4096])
    wd1 = din("wd1", [32, 128, 2816])
    wgu2 = din("wgu2", [44, 128, 4096])
    wd2 = din("wd2", [32, 128, 2816])
    wkvq = din("wkvq", [4, 128, 4096])
    wkr_d = din("wkr", [1, 128, 2048])
    wz_d = din("wz", [4, 128, 4096])
    wgate = din("wgate", [16, 128, 4096])
    wpool = din("wpool", [1, 128, 2048])
    watt = din("watt", [16, 128, 2048])
    watt_s = din("watt_s", [16, 128, 1536])
    wuv_s = din("wuv_s", [2, 128, 4096])
    wmrg = din("wmrg", [16, 128, 3072])
    wout = din("wout", [8, 128, 4096])
    c_ident = din("c_ident", [128, 128])
    c_gains = din("c_gains", [128, 80])
    c_cs_main = din("c_cs_main", [2, 64, NT])
    c_cs_prev = din("c_cs_prev", [2, 64, NPREV])
    c_rc16 = din("c_rc16", [128, 64])
    c_prevb = din("c_prevb", [128, 1])

    y_own = dout("y_own", [D, NOWN])
    y_smp = dout("y_smp", [D, NSMP])
    ckvT_out = dout("ckvT_out", [512, NT])
    krT_out = dout("krT_out", [64, NT])
    poolT_own = dout("poolT_own", [1024, 15])
    poolT_smp = dout("poolT_smp", [1024, 60])

    hs = nc.dram_tensor("hs_scratch", [128, 16 * NT], F32).ap()
    import os
    DBG = os.environ.get("KDBG", "0") == "1"
    if DBG:
        dbg_mixed = dout("dbg_mixed", [128, 8 * NT], BF16)
        dbg_o = dout("dbg_o", [128, 16 * NT], BF16)
        dbg_m = dout("dbg_m", [128, 16 * NT], BF16)
        dbg_x1 = dout("dbg_x1", [128, 16 * NT], F32)
        dbg_pooled = dout("dbg_pooled", [128, 8 * NT], BF16)

    stack = ExitStack()
    arena_t = stack.enter_context(nc.sbuf_tensor("arena", [128, ARENA // 4], F32))
    psum_t = stack.enter_context(nc.psum_tensor("psum", [128, 4096], F32))

    def carve(off, shape, dt):
        esz = 4 if dt == F32 else 2
        P = shape[0]
        n = 1
        for s_ in shape[1:]:
            n *= s_
        nb = n * esz
        assert off % 4 == 0 and nb % 4 == 0 and off + nb <= ARENA, (off, shape)
        ap = arena_t[0:P, off // 4:(off + nb) // 4]
        if dt != F32:
            ap = ap.bitcast(dt)
        if len(shape) == 3:
            ap = ap.rearrange("p (a b) -> p a b", b=shape[2])
        elif len(shape) == 4:
            ap = ap.rearrange("p (a b c) -> p a b c", b=shape[2], c=shape[3])
        return ap

    class Alloc:
        def __init__(self, lo, hi):
            self.lo, self.hi, self.cur = lo, hi, lo

        def get(self, shape, dt):
            esz = 4 if dt == F32 else 2
            n = 1
            for s_ in shape[1:]:
                n *= s_
            nb = (n * esz + 31) // 32 * 32
            off = self.cur
            self.cur += nb
            assert self.cur <= self.hi, ("alloc overflow", shape, self.cur, self.hi)
            return carve(off, shape, dt)

    def psb(b, n=512, parts=128):
        return psum_t[0:parts, b * 512:b * 512 + n]

    def psb_bf(b, n, parts=128):
        return psum_t[0:parts, b * 512:b * 512 + 512].bitcast(BF16)[:, 0:n]

    ca = Alloc(0, 2176)
    ident_f = ca.get([128, 128], F32)
    ident_b = ca.get([128, 128], BF16)
    onesD = ca.get([128, 128], BF16)
    ones512 = ca.get([128, 128], BF16)
    ones1 = ca.get([128, 128], BF16)
    gains = ca.get([128, 80], F32)
    prevb = ca.get([128, 1], F32)
    epsc = ca.get([128, 1], F32)
    rc16 = ca.get([128, 4, 16], F32)
    G_FFN1, G_MIX, G_FFN2, G_FIN, G_Q, G_KV, G_PS = 0, 16, 32, 48, 64, 68, 72

    pa = Alloc(2176, 12928)
    ckvp_bf = pa.get([128, 4, 1024], BF16)
    krp_full = pa.get([128, 1024], BF16)
    krp_bf = krp_full[0:64, :]
    zhist = pa.get([128, 8, 16], F32)
    XN_OFF = 12928
    R0_OFF = XN_OFF + 36864
    E_OFF = R0_OFF + 73728
    assert E_OFF == 123520

    def dma(eng, out, in_, reads=(), writes=()):
        return S.add(eng, lambda e, o=out, i=in_: [e.dma_start(out=o, in_=i)], reads, writes, dma=1)

    dma("sp", ident_f, c_ident, (), ["ident_f"])
    dma("sp", gains, c_gains, (), ["gains"])
    dma("sp", prevb, c_prevb, (), ["prevb"])
    dma("sp", rc16.rearrange("p a b -> p (a b)"), c_rc16, (), ["rc16"])
    S.add("dve", lambda v: v.tensor_copy(out=ident_b, in_=ident_f), ["ident_f"], ["ident_b"])
    S.add("dve", lambda v: v.memset(onesD, 1.0 / 2048), (), ["onesD"])
    S.add("dve", lambda v: v.memset(ones512, 1.0 / 512), (), ["ones512"])
    S.add("dve", lambda v: v.memset(ones1, 1.0), (), ["ones1"])
    S.add("dve", lambda v: v.memset(epsc, EPS), (), ["epsc"])

    ring_state = {"u": 0}

    def wload(src_ap, ncols):
        u = ring_state["u"]
        ring_state["u"] += 1
        slot = u % RING_SLOTS
        assert ncols * 2 <= SLOT_B
        view = carve(RING_OFF + slot * SLOT_B, [128, ncols], BF16)
        key = RK(slot, u + 1, ring_state)
        dma("pool", view, src_ap, (), [key])
        return view, key

    prefetched = {}

    def wl(src, idx, ncols):
        k = (src.tensor.name, idx)
        if k in prefetched:
            return prefetched.pop(k)
        return wload(src[idx], ncols)

    def pf(src, idx, ncols):
        prefetched[(src.tensor.name, idx)] = wload(src[idx], ncols)

    psrot = {"i": 0}

    def bank(group):
        i = psrot.setdefault(id(group), 0)
        psrot[id(group)] = i + 1
        return group[i % len(group)]

    PS_A = [0, 1, 2, 3]
    PS_B = [4, 5]
    PS_C = [6, 7]

    def rms_stats(src_fn, nch, ones_ap, n, rstd_out, keyr, sq_bufs, sqt_buf, tag):
        b = bank(PS_C)
        for kc in range(nch):
            ap, keys = src_fn(kc)
            sq = sq_bufs[kc % 2]
            S.add("act", lambda a, o=sq[:, 0:n], i=ap: a.activation(out=o, in_=i, func=AF.Square),
                  list(keys), [("sq", tag, kc % 2)])
            S.add("pe", lambda pe, o=psb(b, n), r=sq[:, 0:n], st=(kc == 0), sp_=(kc == nch - 1):
                  pe.matmul(o, ones_ap, r, start=st, stop=sp_),
                  [("sq", tag, kc % 2), "onesD", "ones512"], [("ps", b)])
        S.add("act", lambda a, o=sqt_buf[:, 0:n], i=psb(b, n): a.activation(out=o, in_=i, func=AF.Sqrt, bias=epsc[:, 0:1], scale=1.0),
              [("ps", b), "epsc"], [("sqt", tag)])
        S.add("dve", lambda v, o=rstd_out, i=sqt_buf[:, 0:n]: v.reciprocal(out=o, in_=i),
              [("sqt", tag)], [keyr])

    def load_xT(x_dram, ntok, xT, col0, stage_bufs, tag):
        src3 = x_dram.rearrange("(k p) t -> p k t", p=128)
        for c0 in range(0, ntok, 512):
            n = min(512, ntok - c0)
            S.add("sp", lambda e, c0=c0, n=n: [e.dma_start(out=xT[:, g * 4:(g + 1) * 4, col0 + c0:col0 + c0 + n], in_=src3[:, g * 4:(g + 1) * 4, c0:c0 + n]) for g in range(4)],
                  (), [(tag, kc, c) for kc in range(16) for c in range(col0 + c0, col0 + c0 + n, 128)], dma=4)

    def xkeys(tag, kc, c0, n):
        return [(tag, kc, c) for c in range(c0 - c0 % 128, c0 + n, 128)]

    def norm_to_bf(xT, xtag, tiles, gcol, xn, xntag, ea):
        sq_bufs = [ea["sq0"], ea["sq1"]]
        for (c0, n) in tiles:
            rstd = ea["rstd"][:, 0:n]
            rms_stats(lambda kc: (xT[:, kc, c0:c0 + n], xkeys(xtag, kc, c0, n)), 16, onesD, n, rstd, ("rstd", xntag), sq_bufs, ea["sqt"], xntag)
            for kc in range(16):
                S.add("dve", lambda v, o=xn[:, kc, c0:c0 + n], i=xT[:, kc, c0:c0 + n], s=gains[:, gcol + kc:gcol + kc + 1], r=rstd:
                      v.scalar_tensor_tensor(out=o, in0=i, scalar=s, in1=r, op0=ALU.mult, op1=ALU.mult),
                      [("rstd", xntag), "gains"] + xkeys(xtag, kc, c0, n), xkeys(xntag, kc, c0, n))

    def ffn(xT, xtag, xn, xntag, tiles, wgu, wd, ea):
        hT = ea["hT"]
        for q in range(4):
            for fl in range(11):
                fc = q * 11 + fl
                wv, wk = wl(wgu, fc, 4096)
                wv4 = wv.rearrange("p (a b c) -> p a b c", a=2, b=16, c=128)
                for (c0, n) in tiles:
                    bg = bank(PS_A)
                    bu = bank(PS_A)
                    for which, b in ((0, bg), (1, bu)):
                        def mm(pe, which=which, b=b, c0=c0, n=n, wv4=wv4):
                            r = None
                            for kc in range(16):
                                r = pe.matmul(psb(b, n), wv4[:, which, kc, :], xn[:, kc, c0:c0 + n], start=(kc == 0), stop=(kc == 15))
                            return r
                        S.add("pe", mm, [wk] + [k for kc in range(16) for k in xkeys(xntag, kc, c0, n)], [("ps", b)])
                    sg = ea["sg"][bg % 2 if False else (psrot.setdefault("sg", 0) % 2)]
                    sgi = psrot["sg"] % 2
                    psrot["sg"] += 1
                    S.add("act", lambda a, o=sg[:, 0:n], i=psb(bg, n): a.activation(out=o, in_=i, func=AF.Silu),
                          [("ps", bg)], [("sg", sgi)])
                    S.add("dve", lambda v, o=hT[:, fl, c0:c0 + n], i0=sg[:, 0:n], i1=psb(bu, n): v.tensor_tensor(out=o, in0=i0, in1=i1, op=ALU.mult),
                          [("sg", sgi), ("ps", bu)], [("hT", fl, c0)])
            for dcp in range(8):
                wv, wk = wl(wd, q * 8 + dcp, 2816)
                wv4 = wv.rearrange("p (a b c) -> p a b c", a=2, b=11, c=128)
                for d2 in range(2):
                    dc = dcp * 2 + d2
                    for (c0, n) in tiles:
                        b = bank(PS_B)

                        def mm(pe, d2=d2, b=b, c0=c0, n=n, wv4=wv4):
                            r = None
                            for fl in range(11):
                                r = pe.matmul(psb(b, n), wv4[:, d2, fl, :], hT[:, fl, c0:c0 + n], start=(fl == 0), stop=(fl == 10))
                            return r
                        S.add("pe", mm, [wk] + [("hT", fl, c0) for fl in range(11)], [("ps", b)])
                        S.add("dve", lambda v, o=xT[:, dc, c0:c0 + n], i0=psb(b, n): v.scalar_tensor_tensor(out=o, in0=i0, scalar=0.5, in1=o, op0=ALU.mult, op1=ALU.add),
                              [("ps", b)] + xkeys(xtag, dc, c0, n), xkeys(xtag, dc, c0, n))

    def proj(wview, KC, xin, xintag, c0, n, b, M=128, mcol0=0, kc_keys=None):
        def mm(pe):
            r = None
            for kc in range(KC):
                r = pe.matmul(psb(b, n, M), wview[:, kc, mcol0:mcol0 + M], xin[:, kc, c0:c0 + n], start=(kc == 0), stop=(kc == KC - 1))
            return r
        return mm

    def feat_norm512(raw, rawtag, n, gcol, ea, outs, tagn):
        rstd = ea["rstd"][:, 0:n]
        rk_ = (lambda kc: (rawtag + (kc,)) if isinstance(rawtag, tuple) else (rawtag, kc))
        rms_stats(lambda kc: (raw[:, kc, 0:n], [rk_(kc)]), 4, ones512, n, rstd, ("rstd", tagn), [ea["sq0"], ea["sq1"]], ea["sqt"], tagn)
        for kc in range(4):
            for (o_ap, okey) in outs:
                S.add("dve", lambda v, o=o_ap(kc), i=raw[:, kc, 0:n], s=gains[:, gcol + kc:gcol + kc + 1], r=rstd:
                      v.scalar_tensor_tensor(out=o, in0=i, scalar=s, in1=r, op0=ALU.mult, op1=ALU.mult),
                      [("rstd", tagn), rk_(kc), "gains"], okey(kc))

    def dup_rows(dupI, full, c0, n, key):
        b = bank(PS_A)
        S.add("pe", lambda pe, o=psb(b, n), l=dupI, r=full[0:64, c0:c0 + n]: pe.matmul(o, l, r, start=True, stop=True),
              [key, "dupI"], [("ps", b)])
        S.add("act", lambda a, o=full[64:128, c0:c0 + n], i=psum_t[64:128, b * 512:b * 512 + n]: a.copy(out=o, in_=i),
              [("ps", b)], [(key, "dup")])

    def make_dupI(dupI):
        S.add("dve", lambda v, o=dupI[:, 0:64], i=ident_b[0:64, 0:64]: v.tensor_copy(out=o, in_=i), ["ident_b"], ["dupI0"])
        S.add("dve", lambda v, o=dupI[:, 64:128], i=ident_b[0:64, 0:64]: v.tensor_copy(out=o, in_=i), ["ident_b", "dupI0"], ["dupI"])

    def rope_from_ps(b0, b1, n, cs, ccol0, t1, t2, outs):
        S.add("dve", lambda v, o=t1[:, 0:n], i0=psb(b0, n, 64), i1=cs[:, 0, ccol0:ccol0 + n]: v.tensor_tensor(out=o, in0=i0, in1=i1, op=ALU.mult),
              [("ps", b0), "cs"], ["rt1"])
        S.add("dve", lambda v, o=t2[:, 0:n], i0=psb(b1, n, 64), i1=cs[:, 1, ccol0:ccol0 + n]: v.tensor_tensor(out=o, in0=i0, in1=i1, op=ALU.mult),
              [("ps", b1), "cs"], ["rt2"])
        for (o_ap, okeys) in outs:
            S.add("dve", lambda v, o=o_ap, i0=t1[:, 0:n], i1=t2[:, 0:n]: v.tensor_tensor(out=o, in0=i0, in1=i1, op=ALU.add),
                  ["rt1", "rt2"], okeys)

    def ffn_alloc(ea_alloc, ntok):
        ea = {}
        ea["hT"] = ea_alloc.get([128, 11, ntok], BF16)
        ea["sg"] = [ea_alloc.get([128, 512], F32) for _ in range(2)]
        ea["sq0"] = ea_alloc.get([128, 512], BF16)
        ea["sq1"] = ea_alloc.get([128, 512], BF16)
        ea["rstd"] = ea_alloc.get([128, 512], F32)
        ea["sqt"] = ea_alloc.get([128, 512], F32)
        return ea

    MAIN_TILES = [(0, 512), (512, 512), (1024, 128)]
    LIN_TILES = [(0, 384), (384, 384), (768, 384)]
    PREV_TILES = [(0, 512), (512, 512)]

    xn_prev = carve(XN_OFF, [128, 16, NPREV], BF16)
    xT_prev = carve(R0_OFF, [128, 16, NPREV], F32)
    eal = Alloc(E_OFF, RING_OFF)
    ea = ffn_alloc(eal, NT)
    xstage = [eal.get([128, D], F32) for _ in range(2)]

    load_xT(x_prev, NPREV, xT_prev, 0, xstage, "xp")
    norm_to_bf(xT_prev, "xp", PREV_TILES, G_FFN1, xn_prev, "xnp", ea)
    ffn(xT_prev, "xp", xn_prev, "xnp", PREV_TILES, wgu1, wd1, ea)
    norm_to_bf(xT_prev, "xp", PREV_TILES, G_MIX, xn_prev, "xnp", ea)
    pf(wkvq, 0, 4096)
    pf(wkvq, 1, 4096)
    pf(wkr_d, 0, 2048)
    S.barrier()
    xn = carve(XN_OFF, [128, 16, NT], BF16)
    xT = carve(R0_OFF, [128, 16, NT], F32)
    pal = Alloc(E_OFF, RING_OFF)
    p_raw = pal.get([128, 4, 512], F32)
    p_cs = pal.get([64, 2, NPREV], F32)
    p_t1 = pal.get([64, 512], F32)
    p_t2 = pal.get([64, 512], F32)
    p_ea = {"sq0": pal.get([128, 512], BF16), "sq1": pal.get([128, 512], BF16),
            "rstd": pal.get([128, 512], F32), "sqt": pal.get([128, 512], F32)}
    p_dupI = pal.get([64, 128], BF16)
    make_dupI(p_dupI)
    dma("sp", p_cs, c_cs_prev.rearrange("a p t -> p a t"), (), ["cs"])
    load_xT(x_own, NOWN, xT, 0, xstage, "x")
    load_xT(x_smp, NSMP, xT, NOWN, xstage, "x")
    wkv = []
    for u_ in range(2):
        wv, wk = wl(wkvq, u_, 4096)
        w4_ = wv.rearrange("p (a b c) -> p a b c", a=2, b=16, c=128)
        wkv += [(w4_[:, 0], wk), (w4_[:, 1], wk)]
    def prev_kv(ti, c0, n):
        for ch in range(4):
            b = bank(PS_A)
            S.add("pe", proj(wkv[ch][0], 16, xn_prev, "xnp", c0, n, b),
                  [wkv[ch][1]] + [k for kc in range(16) for k in xkeys("xnp", kc, c0, n)], [("ps", b)])
            S.add("act", lambda a, o=p_raw[:, ch, 0:n], i=psb(b, n): a.copy(out=o, in_=i), [("ps", b)], [("praw", ch)])
        feat_norm512(p_raw, "praw", n, G_KV, p_ea,
                     [(lambda kc, c0=c0, n=n: ckvp_bf[:, kc, c0:c0 + n], lambda kc, c0=c0: [("ckvp", kc, c0)])], "pkv")
    prev_kv(0, *PREV_TILES[0])
    wv, wk = wl(wkr_d, 0, 2048)
    wkr = wv.rearrange("p (a b) -> p a b", b=128)
    for ti, (c0, n) in enumerate(PREV_TILES):
        b0 = bank(PS_A)
        b1 = bank(PS_A)
        rk = [wk] + [k for kc in range(16) for k in xkeys("xnp", kc, c0, n)]
        S.add("pe", proj(wkr, 16, xn_prev, "xnp", c0, n, b0, M=64, mcol0=0), rk, [("ps", b0)])
        S.add("pe", proj(wkr, 16, xn_prev, "xnp", c0, n, b1, M=64, mcol0=64), rk, [("ps", b1)])
        rope_from_ps(b0, b1, n, p_cs, c0, p_t1, p_t2, [(krp_bf[:, c0:c0 + n], [("krp", c0)])])
        dup_rows(p_dupI, krp_full, c0, n, ("krp", c0))
    prev_kv(1, *PREV_TILES[1])
    for zc in range(8):
        if zc % 2 == 0:
            wv, wk = wl(wz_d, zc // 2, 4096)
            wz4 = wv.rearrange("p (a b c) -> p a b c", a=2, b=16, c=128)
        wz = wz4[:, zc % 2]
        b = bank(PS_A)
        S.add("pe", proj(wz, 16, xn_prev, "xnp", NPREV - 16, 16, b),
              [wk] + [k for kc in range(16) for k in xkeys("xnp", kc, NPREV - 16, 16)], [("ps", b)])
        S.add("act", lambda a, o=zhist[:, zc, :], i=psb(b, 16): a.copy(out=o, in_=i), [("ps", b)], [("zhist", zc)])
    rstd3 = carve(E_OFF + 36864, [128, NT], F32)
    for (c0, n) in LIN_TILES:
        rms_stats(lambda kc: (xT[:, kc, c0:c0 + n], xkeys("x", kc, c0, n)), 16, onesD, n, rstd3[:, c0:c0 + n], ("rstd3", c0),
                  [ea["sq0"], ea["sq1"]], ea["sqt"], "xn")
    S.barrier()

    for (c0, n) in LIN_TILES:
        for kc in range(16):
            S.add("dve", lambda v, o=xn[:, kc, c0:c0 + n], i=xT[:, kc, c0:c0 + n], s=gains[:, G_FFN1 + kc:G_FFN1 + kc + 1], r=rstd3[:, c0:c0 + n]:
                  v.scalar_tensor_tensor(out=o, in0=i, scalar=s, in1=r, op0=ALU.mult, op1=ALU.mult),
                  ["gains"], xkeys("xn", kc, c0, n))
    ffn(xT, "x", xn, "xn", LIN_TILES, wgu1, wd1, ea)
    norm_to_bf(xT, "x", LIN_TILES, G_MIX, xn, "xn", ea)
    pf(wkvq, 0, 4096)
    pf(wkvq, 1, 4096)
    pf(wkr_d, 0, 2048)
    S.barrier()
    spill_t = S.add("sp", lambda e: [e.dma_start(out=hs[:, kc * NT:(kc + 1) * NT], in_=xT[:, kc, :]) for kc in range(16)], (), ["hs"], dma=16)
    spill_t.has_dep = True
    S.pending["dve"] = S.pending.get("dve", set()) | {spill_t}

    r0 = Alloc(R0_OFF, E_OFF)
    oT = r0.get([128, 16, NT], BF16)
    mixedT = r0.get([128, 8, NT], BF16)
    qln = r0.get([128, 4, NT], BF16)
    ckvbf = r0.get([128, 4, NT], BF16)
    pooledT = carve(R0_OFF, [128, 8, NT], BF16)
    me = Alloc(E_OFF, RING_OFF)
    krbf_full = me.get([128, NT], BF16)
    krbf = krbf_full[0:64, :]
    cs = me.get([64, 2, NT], F32)
    m_dupI = me.get([64, 128], BF16)
    make_dupI(m_dupI)
    T_OFF = me.cur
    dma("sp", cs, c_cs_main.rearrange("a p t -> p a t"), (), ["cs"])

    ta = Alloc(T_OFF, RING_OFF)
    raw2 = [ta.get([128, 4, 512], F32) for _ in range(2)]
    kvn2 = [ta.get([128, 4, 512], F32)] * 2
    t1 = ta.get([64, 512], F32)
    t2 = ta.get([64, 512], F32)
    krn2 = [ta.get([64, 512], F32) for _ in range(2)]
    ckvT_out3 = ckvT_out.rearrange("(k p) t -> p k t", p=128)
    m_ea = {"sq0": ta.get([128, 512], BF16), "sq1": ta.get([128, 512], BF16),
            "rstd": ta.get([128, 512], F32), "sqt": ta.get([128, 512], F32)}

    wkv = []
    for u_ in range(2):
        wv, wk = wl(wkvq, u_, 4096)
        w4_ = wv.rearrange("p (a b c) -> p a b c", a=2, b=16, c=128)
        wkv += [(w4_[:, 0], wk), (w4_[:, 1], wk)]
    wv, wk = wl(wkr_d, 0, 2048)
    wkr = (wv.rearrange("p (a b) -> p a b", b=128), wk)
    for ti, (c0, n) in enumerate(MAIN_TILES):
        for ch in range(4):
            b = bank(PS_A)
            S.add("pe", proj(wkv[ch][0], 16, xn, "xn", c0, n, b),
                  [wkv[ch][1]] + [k for kc in range(16) for k in xkeys("xn", kc, c0, n)], [("ps", b)])
            S.add("act", lambda a, o=raw2[ti % 2][:, ch, 0:n], i=psb(b, n): a.copy(out=o, in_=i), [("ps", b)], [("rawkv", ti % 2, ch)])
        kvn = kvn2[ti % 2]
        krn = krn2[ti % 2]
        feat_norm512(raw2[ti % 2], ("rawkv", ti % 2), n, G_KV, m_ea,
                     [(lambda kc, n=n, kvn=kvn: kvn[:, kc, 0:n], lambda kc, ti=ti: [("kvn", 0, kc)])], "kv")
        for kc in range(4):
            S.add("act", lambda a, o=ckvbf[:, kc, c0:c0 + n], i=kvn[:, kc, 0:n]: a.copy(out=o, in_=i),
                  [("kvn", 0, kc)], [("ckvbf", kc, c0)])
        b0 = bank(PS_A)
        b1 = bank(PS_A)
        rk = [wkr[1]] + [k for kc in range(16) for k in xkeys("xn", kc, c0, n)]
        S.add("pe", proj(wkr[0], 16, xn, "xn", c0, n, b0, M=64, mcol0=0), rk, [("ps", b0)])
        S.add("pe", proj(wkr[0], 16, xn, "xn", c0, n, b1, M=64, mcol0=64), rk, [("ps", b1)])
        rope_from_ps(b0, b1, n, cs, c0, t1, t2, [(krn[:, 0:n], [("krn", ti % 2)]), (krbf[:, c0:c0 + n], [("krbf", c0)])])
        if c0 < NOWN:
            dup_rows(m_dupI, krbf_full, c0, n, ("krbf", c0))
        dma("sp", ckvT_out3[:, :, c0:c0 + n], kvn[:, :, 0:n], [("kvn", 0, kc) for kc in range(4)], [])
        dma("sp", krT_out[:, c0:c0 + n], krn[:, 0:n], [("krn", ti % 2)], [])
    wq = []
    for u_ in range(2):
        wv, wk = wl(wkvq, 2 + u_, 4096)
        w4_ = wv.rearrange("p (a b c) -> p a b c", a=2, b=16, c=128)
        wq += [(w4_[:, 0], wk), (w4_[:, 1], wk)]
    for ti, (c0, n) in enumerate(MAIN_TILES):
        for ch in range(4):
            b = bank(PS_A)
            S.add("pe", proj(wq[ch][0], 16, xn, "xn", c0, n, b),
                  [wq[ch][1]] + [k for kc in range(16) for k in xkeys("xn", kc, c0, n)], [("ps", b)])
            S.add("act", lambda a, o=raw2[(ti + 1) % 2][:, ch, 0:n], i=psb(b, n): a.copy(out=o, in_=i), [("ps", b)], [("rawq", ti % 2, ch)])
        feat_norm512(raw2[(ti + 1) % 2], ("rawq", ti % 2), n, G_Q, m_ea,
                     [(lambda kc, c0=c0, n=n: qln[:, kc, c0:c0 + n], lambda kc, c0=c0: [("qln", kc, c0)])], "ql")

    pf(wz_d, 0, 4096)
    pf(wz_d, 1, 4096)
    S.barrier()
    tb_ = Alloc(T_OFF, RING_OFF)
    zxb = [tb_.get([128, 1232], F32) for _ in range(2)]
    ppa = tb_.get([128, 1232], F32)
    ppb = tb_.get([128, 1232], F32)
    fx16 = tb_.get([128, 16], F32)
    SEG = [(0, 1024, 0)] + [(1040 + 48 * j, 32, NOWN + 32 * j) for j in range(4)]
    for zc in range(8):
        g = zc // 2
        w = 2 << g
        if zc % 2 == 0:
            wv, wk = wl(wz_d, zc // 2, 4096)
            wz4 = wv.rearrange("p (a b c) -> p a b c", a=2, b=16, c=128)
        wz = wz4[:, zc % 2]
        zx = zxb[zc % 2]
        zk = ("zx", zc % 2)
        S.add("dve", lambda v, o=zx[:, 0:16], i=zhist[:, zc, :]: v.tensor_copy(out=o, in_=i), [("zhist", zc)], [(zk, "h0")])
        dma("sp", zx[:, 1040:1232].rearrange("p (j t) -> p j t", t=48)[:, :, 1:16],
            state_pool[zc * 128:(zc + 1) * 128, :].rearrange("p (j t) -> p j t", t=15), (), [(zk, "h", j) for j in range(4)])
        for (c0, n) in MAIN_TILES:
            b = bank(PS_A)
            S.add("pe", proj(wz, 16, xn, "xn", c0, n, b),
                  [wk] + [k for kc in range(16) for k in xkeys("xn", kc, c0, n)], [("ps", b)])
            if c0 < NOWN:
                S.add("act", lambda a, o=zx[:, 16 + c0:16 + c0 + n], i=psb(b, n): a.copy(out=o, in_=i), [("ps", b)], [(zk, "z", c0)])
            else:
                dst = zx[:, 1040:1232].rearrange("p (j t) -> p j t", t=48)[:, :, 16:48]
                src = psb(b, n).rearrange("p (j t) -> p j t", t=32)
                S.add("act", lambda a, o=dst, i=src: a.copy(out=o, in_=i), [("ps", b)], [(zk, "z", c0)])
        allz = [(zk, "h0")] + [(zk, "h", j) for j in range(4)] + [(zk, "z", c0) for (c0, n) in MAIN_TILES]
        dma("sp", poolT_own[zc * 128:(zc + 1) * 128, :], zx[:, 1025:1040], allz, [])
        dma("sp", poolT_smp[zc * 128:(zc + 1) * 128, :].rearrange("p (j t) -> p j t", t=15),
            zx[:, 1040:1232].rearrange("p (j t) -> p j t", t=48)[:, :, 33:48], allz, [])
        cur, curk = zx, allz
        k = 1
        tog = 0
        while k < w:
            nxt = ppa if tog == 0 else ppb
            nk = ["ppa"] if tog == 0 else ["ppb"]
            S.add("dve", lambda v, o=nxt[:, k:1232], i0=cur[:, k:1232], i1=cur[:, 0:1232 - k]: v.tensor_tensor(out=o, in0=i0, in1=i1, op=ALU.add),
                  curk, nk)
            S.add("dve", lambda v, o=nxt[:, 0:k], i=cur[:, 0:k]: v.tensor_copy(out=o, in_=i), curk, [nk[0] + "h"])
            cur, curk = nxt, nk + [nk[0] + "h"]
            tog ^= 1
            k *= 2
        for si_, (h0, nt, tc) in enumerate(SEG):
            S.add("dve", lambda v, o=pooledT[:, zc, tc:tc + nt], i0=cur[:, h0 + 16:h0 + 16 + nt], i1=zx[:, h0 + 16:h0 + 16 + nt], sc=1.0 / w:
                  v.scalar_tensor_tensor(out=o, in0=i0, scalar=sc, in1=i1, op0=ALU.mult, op1=ALU.subtract),
                  curk + allz, [("pooled", zc, tc)])
        S.add("dve", lambda v, o=fx16, i0=cur[:, 16:32], i1=rc16[:, g, :]:
              v.tensor_tensor(out=o, in0=i0, in1=i1, op=ALU.mult), curk + ["rc16"], ["fix16"])
        S.add("dve", lambda v, o=pooledT[:, zc, 0:16], i0=fx16, i1=zx[:, 16:32]: v.tensor_tensor(out=o, in0=i0, in1=i1, op=ALU.subtract),
              ["fix16"] + allz + [("pooled", zc, 0)], [("pooled", zc, 0)])
    wv, wpk = wl(wpool, 0, 2048)
    wp4 = wv.rearrange("p (a b c) -> p a b c", a=4, b=2, c=256)
    for (c0, n) in MAIN_TILES:
        for g in range(4):
            for dd in range(2):
                b = bank(PS_A)

                def mm(pe, g=g, dd=dd, b=b, c0=c0, n=n):
                    r = None
                    for cc in range(2):
                        r = pe.matmul(psb(b, n), wp4[:, g, cc, dd * 128:(dd + 1) * 128], pooledT[:, 2 * g + cc, c0:c0 + n], start=(cc == 0), stop=(cc == 1))
                    return r
                rk = [wpk] + [("pooled", 2 * g + cc, tc) for cc in range(2) for tc in ([0, 512] if c0 < NOWN else [NOWN + 32 * j for j in range(4)])]
                S.add("pe", mm, rk, [("ps", b)])
                S.add("dve", lambda v, o=mixedT[:, 2 * g + dd, c0:c0 + n], i=psb(b, n), s=gains[:, G_PS + 2 * g + dd:G_PS + 2 * g + dd + 1]:
                      v.tensor_scalar_mul(out=o, in0=i, scalar1=s), [("ps", b), "gains"], [("mixed", 2 * g + dd, c0)])
    pf(watt, 0, 2048)
    pf(watt, 1, 2048)
    S.barrier()
    if DBG:
        dma("sp", dbg_pooled, pooledT.rearrange("p a b -> p (a b)"), (), [])
        dma("sp", dbg_mixed, mixedT.rearrange("p a b -> p (a b)"), (), [])
        S.barrier()

    aa = Alloc(T_OFF, RING_OFF)
    WS = []
    for _ in range(2):
        WS.append({"qn": aa.get([128, 1024], BF16), "qr": aa.get([128, 1024], BF16),
                   "kn": aa.get([128, 2048], BF16), "v": aa.get([128, 16, 128], BF16)})
    pts = [aa.get([128, 512], BF16) for _ in range(4)]
    rs = aa.get([128, 512], F32)
    cs2 = aa.get([128, 1024], F32)
    dma("sp", cs2[0:64, :], c_cs_main[0, :, 0:1024], (), ["cs2a"])
    dma("sp", cs2[64:128, :], c_cs_main[1, :, 0:1024], (), ["cs2b"])
    A2_OFF = aa.cur
    PS_S = [0, 1, 2]
    PS_O = [3]
    PS_SUM = [4]
    PS_P = [5, 6, 7]
    bo_sb = [aa.get([128, 512], F32) for _ in range(2)]
    bs_sb = [aa.get([128, 512], F32) for _ in range(2)]
    fin_i = [0]
    pt_i = [0]

    def attend(ws, wsk, h, qcol0, nq, ocol0, blocks):
        LOOK = 2
        bo = bank(PS_O)
        bs = bank(PS_SUM)
        nb = len(blocks)
        ptinfo = {}
        for it in range(nb + LOOK):
            if it < nb:
                (kn_ap, kr_ap, v_ap, nk, bias_ap, qoff, mask, keys) = blocks[it]
                nqq = nq - qoff
                b = bank(PS_S)

                def mm(pe, b=b, kn_ap=kn_ap, kr_ap=kr_ap, nk=nk, qoff=qoff, nqq=nqq):
                    pe.matmul(psb(b, nqq, nk), kn_ap, ws["qn"][:, qcol0 + qoff:qcol0 + qoff + nqq], start=True, stop=False)
                    return pe.matmul(psb(b, nqq, nk), kr_ap, ws["qr"][:, qcol0 + qoff:qcol0 + qoff + nqq], start=False, stop=True)
                S.add("pe", mm, keys + [(wsk, "qn"), (wsk, "qr")], [("ps", b)])
                pi = pt_i[0] % len(pts)
                pt_i[0] += 1
                pt = pts[pi]
                ptinfo[it] = (pi, pt)
                if bias_ap is None:
                    S.add("act", lambda a, o=pt[0:nk, 0:nqq], i=psb(b, nqq, nk): a.activation(out=o, in_=i, func=AF.Exp, scale=SM_SCALE),
                          [("ps", b)], [("pt", pi)])
                else:
                    S.add("act", lambda a, o=pt[0:nk, 0:nqq], i=psb(b, nqq, nk), bb=bias_ap: a.activation(out=o, in_=i, func=AF.Exp, bias=bb, scale=SM_SCALE),
                          [("ps", b), "prevb"], [("pt", pi)])
                if mask:
                    S.add("dve", lambda v, o=pt[64:128, 0:64]: v.memset(o, 0.0), [("pt", pi)], [("pt", pi)])
            bi = it - LOOK
            if bi >= 0:
                (kn_ap, kr_ap, v_ap, nk, bias_ap, qoff, mask, keys) = blocks[bi]
                nqq = nq - qoff
                pi, pt = ptinfo[bi]
                S.add("pe", lambda pe, o=psb(bo, nq)[:, qoff:nq], l=v_ap, r=pt[0:nk, 0:nqq], st=(bi == 0), sp_=(bi == nb - 1):
                      pe.matmul(o, l, r, start=st, stop=sp_), [("pt", pi), (wsk, "v")] + keys, [("ps", bo)])
                S.add("pe", lambda pe, o=psb(bs, nq)[:, qoff:nq], l=ones1[0:nk, :], r=pt[0:nk, 0:nqq], st=(bi == 0), sp_=(bi == nb - 1):
                      pe.matmul(o, l, r, start=st, stop=sp_), [("pt", pi), "ones1"], [("ps", bs)])
        ai = fin_i[0] % 2
        fin_i[0] += 1
        S.add("act", lambda a, o=bs_sb[ai][:, 0:nq], i=psb(bs, nq): a.copy(out=o, in_=i), [("ps", bs)], [("bs_sb", ai)])
        S.add("act", lambda a, o=bo_sb[ai][:, 0:nq], i=psb(bo, nq): a.copy(out=o, in_=i), [("ps", bo)], [("bo_sb", ai)])
        def fin():
            S.add("dve", lambda v, o=rs[:, 0:nq], i=bs_sb[ai][:, 0:nq]: v.reciprocal(out=o, in_=i), [("bs_sb", ai)], ["rs"])
            S.add("dve", lambda v, o=oT[:, h, ocol0:ocol0 + nq], i0=bo_sb[ai][:, 0:nq], i1=rs[:, 0:nq]: v.tensor_tensor(out=o, in0=i0, in1=i1, op=ALU.mult),
                  [("bo_sb", ai), "rs"], [("oT", h, ocol0)])
        return fin

    head_full = {}

    def head_weights(h):
        wv, wk = wl(watt, h, 2048)
        w4 = wv.rearrange("p (a b c) -> p a b c", a=4, b=4, c=128)
        return w4, wk

    def head_weights_s(h):
        wv, wk = wl(watt_s, h, 1536)
        head_full[h] = wv
        w4 = wv[:, 0:1024].rearrange("p (a b c) -> p a b c", a=2, b=4, c=128)
        return w4, wk

    def q_proj(w4, wk, ws, wsk, c0, n, dcol0, cscol0):
        rk = [wk] + [("qln", kc, c) for kc in range(4) for c in range(c0 - c0 % 128, c0 + n, 128)] + \
             [("qln", kc, cc) for kc in range(4) for cc in (0, 512, 1024)]
        b = bank(PS_P)
        S.add("pe", proj(w4[:, 0], 4, qln, "qln", c0, n, b), rk, [("ps", b)])
        S.add("act", lambda a, o=ws["qn"][:, dcol0:dcol0 + n], i=psb(b, n): a.copy(out=o, in_=i), [("ps", b)], [(wsk, "qn")])
        b0 = bank(PS_P)
        S.add("pe", proj(w4[:, 1], 4, qln, "qln", c0, n, b0, M=128, mcol0=0), rk, [("ps", b0)])
        S.add("dve", lambda v, o=ws["qr"][:, dcol0:dcol0 + n], i0=psb(b0, n), i1=cs2[:, cscol0:cscol0 + n]: v.tensor_tensor(out=o, in0=i0, in1=i1, op=ALU.mult),
              [("ps", b0), "cs2a", "cs2b"], [(wsk, "qr")])

    def k_proj(w4, wk, ws, wsk, src, srckeys, c0, n, dcol0):
        b = bank(PS_P)
        S.add("pe", proj(w4[:, 2], 4, src, None, c0, n, b), [wk] + srckeys, [("ps", b)])
        S.add("dve", lambda v, o=ws["kn"][:, dcol0:dcol0 + n], i=psb(b, n): v.tensor_copy(out=o, in_=i), [("ps", b)], [(wsk, "kn")])

    def v_proj(w4, wk, ws, wsk, src, srckeys, c0, nk, blk):
        b = bank(PS_P)

        def mm(pe):
            r = None
            for kc in range(4):
                r = pe.matmul(psb(b, 128, nk), src[:, kc, c0:c0 + nk], w4[:, 3, kc, :], start=(kc == 0), stop=(kc == 3))
            return r
        S.add("pe", mm, [wk] + srckeys, [("ps", b)])
        S.add("act", lambda a, o=ws["v"][0:nk, blk, :], i=psb(b, 128, nk): a.copy(out=o, in_=i), [("ps", b)], [(wsk, "v")])

    def v_proj4(w4, wk, ws, wsk, src, c0, blk0):
        b = bank(PS_P)

        def mm(pe):
            r = None
            for j in range(4):
                for kc in range(4):
                    r = pe.matmul(psb(b)[:, j * 128:(j + 1) * 128], src[:, kc, c0 + j * 128:c0 + (j + 1) * 128], w4[:, 3, kc, :],
                                  start=(kc == 0), stop=(kc == 3))
            return r
        S.add("pe", mm, [wk], [("ps", b)])
        S.add("act", lambda a, o=ws["v"][:, blk0:blk0 + 4, :], i=psb(b).rearrange("p (a b) -> p a b", b=128): a.copy(out=o, in_=i),
              [("ps", b)], [(wsk, "v")])

    ckvp_keys = []
    own_keys = []
    def head_proj(h):
        ws = WS[h % 2]
        wsk = ("ws", h % 2)
        w4, wk = head_weights(h)
        for (c0, n) in [(0, 512), (512, 512)]:
            q_proj(w4, wk, ws, wsk, c0, n, c0, c0)
        for (c0, n) in [(0, 512), (512, 512)]:
            k_proj(w4, wk, ws, wsk, ckvp_bf, [], c0, n, c0)
            k_proj(w4, wk, ws, wsk, ckvbf, [], c0, n, 1024 + c0)
        for g4 in range(2):
            v_proj4(w4, wk, ws, wsk, ckvp_bf, g4 * 512, g4 * 4)
        for g4 in range(2):
            v_proj4(w4, wk, ws, wsk, ckvbf, g4 * 512, 8 + g4 * 4)

    def head_attend(h, qt):
        ws = WS[h % 2]
        wsk = ("ws", h % 2)
        q0 = qt * 512
        blocks = []
        for blk in range(8):
            blocks.append((ws["kn"][:, blk * 128:(blk + 1) * 128], krp_full[:, blk * 128:(blk + 1) * 128], ws["v"][:, blk, :], 128,
                           prevb[:, 0:1], 0, False, [(wsk, "kn")]))
        for blk in range(4 * qt):
            blocks.append((ws["kn"][:, 1024 + blk * 128:1024 + (blk + 1) * 128], krbf_full[:, blk * 128:(blk + 1) * 128], ws["v"][:, 8 + blk, :], 128,
                           None, 0, False, [(wsk, "kn")]))
        for j in range(4):
            blk = 4 * qt + j
            blocks.append((ws["kn"][:, 1024 + blk * 128:1024 + (blk + 1) * 128], krbf_full[:, blk * 128:(blk + 1) * 128], ws["v"][:, 8 + blk, :], 128,
                           None, j * 128, True, [(wsk, "kn")]))
        return attend(ws, wsk, h, q0, 512, q0, blocks)

    head_proj(0)
    for h in range(16):
        fin0 = head_attend(h, 0)
        if h + 1 < 16:
            head_proj(h + 1)
        fin0()
        fin1 = head_attend(h, 1)
        fin1()
    pf(watt_s, 0, 1536)
    pf(watt_s, 1, 1536)
    S.barrier()

    SC0 = NOWN
    sa = Alloc(E_OFF + 2304 + 9216, RING_OFF)
    qabs = sa.get([128, 4, 4, 512], BF16)
    qrs = sa.get([64, 4, 512], BF16)
    qn_s = [sa.get([128, 128], BF16) for _ in range(2)]
    s_t1 = sa.get([64, 128], F32)
    s_t2 = sa.get([64, 128], F32)
    S2_OFF = sa.cur
    PS_Q = [0, 1, 2, 3]
    sb_ = Alloc(E_OFF + 2304, E_OFF + 2304 + 9216)
    sc_ = Alloc(S2_OFF, RING_OFF)
    ctok2 = [sc_.get([128, 4, 512], BF16), sb_.get([128, 4, 512], BF16)]
    cacheT2 = [sc_.get([128, 4, 512], BF16), sb_.get([128, 4, 512], BF16)]
    ckrT2 = [sc_.get([64, 512], BF16), sb_.get([64, 512], BF16)]
    new_tok = sc_.get([32, 512], BF16)
    pts = [sc_.get([128, 512], BF16) for _ in range(3)]
    rs = sc_.get([128, 512], F32)
    PS_S2 = [0, 1]
    PS_PV = [2, 3, 4, 5]
    PS_SM = 6
    PS_T = [7]
    pt_j = [0]

    def cache_load(u):
        bb, hf = u // 2, u % 2
        dma("pool", ctok2[u % 2], cache_ckv[bb, hf * 512:(hf + 1) * 512, :].rearrange("(k p) c -> p k c", p=128), (), [("ctok", u % 2)])
        dma("pool", cacheT2[u % 2], cache_ckvT[bb].rearrange("(k p) t -> p k t", p=128)[:, :, hf * 512:(hf + 1) * 512], (), [("cacheT", u % 2)])
        dma("pool", ckrT2[u % 2], cache_krT[bb][:, hf * 512:(hf + 1) * 512], (), [("ckrT", u % 2)])

    cache_load(0)
    s1 = {}

    def s1_a(h):
        w4, wk = head_weights_s(h)
        wukT = head_full[h][:, 1024:1536]
        rk = [wk] + [("qln", kc, SC0) for kc in range(4)]
        b = bank(PS_Q)
        S.add("pe", proj(w4[:, 0], 4, qln, "qln", SC0, 128, b), rk, [("ps", b)])
        qs = qn_s[h % 2]
        S.add("act", lambda a_, o=qs, i=psb(b, 128): a_.copy(out=o, in_=i), [("ps", b)], [("qn_s", h % 2)])
        b0 = bank(PS_Q)
        S.add("pe", proj(w4[:, 1], 4, qln, "qln", SC0, 128, b0, M=64, mcol0=0), rk, [("ps", b0)])
        b1 = bank(PS_Q)
        S.add("pe", proj(w4[:, 1], 4, qln, "qln", SC0, 128, b1, M=64, mcol0=64), rk, [("ps", b1)])
        S.add("dve", lambda v, o=s_t1, i0=psb(b0, 128, 64), i1=cs[:, 0, SC0:SC0 + 128]: v.tensor_tensor(out=o, in0=i0, in1=i1, op=ALU.mult),
              [("ps", b0), "cs"], ["s_t1"])
        S.add("dve", lambda v, o=s_t2, i0=psb(b1, 128, 64), i1=cs[:, 1, SC0:SC0 + 128]: v.tensor_tensor(out=o, in0=i0, in1=i1, op=ALU.mult),
              [("ps", b1), "cs"], ["s_t2"])
        S.add("dve", lambda v, o=qrs[:, :, h * 32:(h + 1) * 32], i0=s_t1.rearrange("p (b q) -> p b q", q=32), i1=s_t2.rearrange("p (b q) -> p b q", q=32):
              v.tensor_tensor(out=o, in0=i0, in1=i1, op=ALU.add), ["s_t1", "s_t2"], [("qrs", h)])
        s1[h] = (wukT, wk, qs)

    def s1_b(h):
        wukT, wk, qs = s1[h]
        b2 = bank(PS_Q)

        def mmq(pe, b2=b2, wukT=wukT, qs=qs):
            r = None
            for c in range(4):
                r = pe.matmul(psb(b2)[:, c * 128:(c + 1) * 128], wukT[:, c * 128:(c + 1) * 128], qs, start=True, stop=True)
            return r
        S.add("pe", mmq, [wk, ("qn_s", h % 2)], [("ps", b2)])
        S.add("act", lambda a_, o=qabs[:, :, :, h * 32:(h + 1) * 32], i=psb(b2).rearrange("p (c b q) -> p c b q", c=4, b=4): a_.copy(out=o, in_=i),
              [("ps", b2)], [("qabs", h)])

    s1_a(0)
    for h in range(16):
        if h + 1 < 16:
            s1_a(h + 1)
        s1_b(h)
    S.barrier()

    def cache_transposes(u):
        pass

    cache_load(1)
    cache_transposes(0)
    for bi_ in range(4):
        tc = NOWN + 32 * bi_
        b = bank(PS_T)
        for c in range(4):
            S.add("pe", lambda pe, o=psb_bf(b, 1024, 32)[:, c * 128:(c + 1) * 128], i=ckvbf[:, c, tc:tc + 32]: pe.transpose(o, i, ident_b),
                  ["ident_b"], [("ps", b)])
        S.add("dve", lambda v, o=new_tok, i=psb_bf(b, 512, 32): v.tensor_copy(out=o, in_=i), [("ps", b)], ["new_tok"])
        nb = 9
        info = {}
        for it in range(nb + 1):
            if it < nb:
                nk = 128 if it < 8 else 32
                u = 2 * bi_ + it // 4
                ub, lb = u % 2, it % 4
                if it == 2:
                    cache_transposes(2 * bi_ + 1)
                if it == 6 and bi_ < 3:
                    cache_transposes(2 * bi_ + 2)
                bsx = bank(PS_S2)

                def mms(pe, it=it, nk=nk, bsx=bsx, bi_=bi_, tc=tc, ub=ub, lb=lb):
                    for c in range(4):
                        l = cacheT2[ub][:, c, lb * 128:(lb + 1) * 128] if it < 8 else ckvbf[:, c, tc:tc + 32]
                        pe.matmul(psb(bsx, 512, nk), l, qabs[:, c, bi_, :], start=(c == 0), stop=False)
                    l = ckrT2[ub][:, lb * 128:(lb + 1) * 128] if it < 8 else krbf[:, tc:tc + 32]
                    return pe.matmul(psb(bsx, 512, nk), l, qrs[:, bi_, :], start=False, stop=True)
                rkeys = ([("cacheT", ub), ("ckrT", ub)] if it < 8 else []) + [("qabsb", bi_)]
                S.add("pe", mms, rkeys, [("ps", bsx)])
                pi = pt_j[0] % 3
                pt_j[0] += 1
                pt = pts[pi]
                info[it] = (pi, pt, nk, ub, lb)
                S.add("act", lambda a_, o=pt[0:nk, :], i=psb(bsx, 512, nk): a_.activation(out=o, in_=i, func=AF.Exp, scale=SM_SCALE),
                      [("ps", bsx)], [("pt", pi)])
            j = it - 1
            if j >= 0:
                pi, pt, nk, ub, lb = info[j]

                def mmpv(pe, j=j, nk=nk, pt=pt, ub=ub, lb=lb):
                    for c in range(4):
                        l = ctok2[ub][:, lb, c * 128:(c + 1) * 128] if j < 8 else new_tok[:, c * 128:(c + 1) * 128]
                        pe.matmul(psb(PS_PV[c]), l, pt[0:nk, :], start=(j == 0), stop=(j == nb - 1))
                    return pe.matmul(psb(PS_SM), ones1[0:nk, :], pt[0:nk, :], start=(j == 0), stop=(j == nb - 1))
                S.add("pe", mmpv, [("pt", pi), ("ctok", ub), "new_tok", "ones1"], [("ps", PS_PV[c]) for c in range(4)] + [("ps", PS_SM)])
                if j in (3, 7):
                    uu = 2 * bi_ + j // 4
                    if uu + 2 < 8:
                        cache_load(uu + 2)
        S.add("dve", lambda v, o=rs, i=psb(PS_SM): v.reciprocal(out=o, in_=i), [("ps", PS_SM)], ["rs"])
        for c in range(4):
            S.add("dve", lambda v, o=qabs[:, c, bi_, :], i0=psb(PS_PV[c]), i1=rs: v.tensor_tensor(out=o, in0=i0, in1=i1, op=ALU.mult),
                  [("ps", PS_PV[c]), "rs"], [("olat", bi_, c), ("qabsb", bi_)])
    for h in range(16):
        if h % 8 == 0:
            wv_, wk = wl(wuv_s, h // 8, 4096)
            wuv8 = wv_.rearrange("p (a b c) -> p a b c", a=8, b=4, c=128)
        b = bank(PS_S2)

        def mmo(pe, b=b, wuv8=wuv8, h=h):
            r = None
            for c in range(4):
                r = pe.matmul(psb(b, 128), wuv8[:, h % 8, c, :], qabs[:, c, :, h * 32:(h + 1) * 32], start=(c == 0), stop=(c == 3))
            return r
        S.add("pe", mmo, [wk] + [("olat", bb, c) for bb in range(4) for c in range(4)], [("ps", b)])
        S.add("act", lambda a_, o=oT[:, h, SC0:SC0 + 128], i=psb(b, 128): a_.copy(out=o, in_=i), [("ps", b)], [("oT", h, SC0)])
    pf(wgate, 0, 4096)
    pf(wmrg, 0, 3072)
    S.barrier()

    if DBG:
        dma("sp", dbg_o, oT.rearrange("p a b -> p (a b)"), (), [])
        S.barrier()
    ma = Alloc(E_OFF, RING_OFF)
    mT = ma.get([128, 16, NT], BF16)
    sga = [ma.get([128, 512], F32) for _ in range(2)]
    sgb = [ma.get([128, 512], F32) for _ in range(2)]
    mt1 = [ma.get([128, 512], F32) for _ in range(2)]
    mi = [0]
    for dc in range(16):
        wv, wgk = wl(wgate, dc, 4096)
        wg4 = wv.rearrange("p (a b c) -> p a b c", a=2, b=16, c=128)
        wv, wmk = wl(wmrg, dc, 3072)
        wo3 = wv[:, 0:2048].rearrange("p (b c) -> p b c", c=128)
        wpo3 = wv[:, 2048:3072].rearrange("p (b c) -> p b c", c=128)
        for (c0, n) in LIN_TILES:
            i2 = mi[0] % 2
            mi[0] += 1
            bga = bank(PS_A)
            S.add("pe", proj(wg4[:, 0], 16, xn, None, c0, n, bga), [wgk], [("ps", bga)])
            bgb = bank(PS_A)
            S.add("pe", proj(wg4[:, 1], 16, xn, None, c0, n, bgb), [wgk], [("ps", bgb)])
            ba = bank(PS_B)
            S.add("pe", proj(wpo3, 8, mixedT, None, c0, n, ba), [wmk], [("ps", ba)])
            bb = bank(PS_C)
            S.add("pe", proj(wo3, 16, oT, None, c0, n, bb), [wmk], [("ps", bb)])
            S.add("act", lambda a, o=sga[i2][:, 0:n], i=psb(bga, n): a.activation(out=o, in_=i, func=AF.Sigmoid), [("ps", bga)], [("sga", i2)])
            S.add("act", lambda a, o=sgb[i2][:, 0:n], i=psb(bgb, n): a.activation(out=o, in_=i, func=AF.Sigmoid), [("ps", bgb)], [("sgb", i2)])
            S.add("dve", lambda v, o=mt1[i2][:, 0:n], i0=sga[i2][:, 0:n], i1=psb(ba, n): v.tensor_tensor(out=o, in0=i0, in1=i1, op=ALU.mult),
                  [("sga", i2), ("ps", ba)], [("mt1", i2)])
            S.add("dve", lambda v, o=sgb[i2][:, 0:n], i0=sgb[i2][:, 0:n], i1=psb(bb, n): v.tensor_tensor(out=o, in0=i0, in1=i1, op=ALU.mult),
                  [("sgb", i2), ("ps", bb)], [("sgb", i2)])
            S.add("dve", lambda v, o=mT[:, dc, c0:c0 + n], i0=mt1[i2][:, 0:n], i1=sgb[i2][:, 0:n]: v.tensor_tensor(out=o, in0=i0, in1=i1, op=ALU.add),
                  [("mt1", i2), ("sgb", i2)], [("mT", dc, c0)])
    pf(wout, 0, 4096)
    S.barrier()
    if DBG:
        dma("sp", dbg_m, mT.rearrange("p a b -> p (a b)"), (), [])
        S.barrier()
    for kc in range(16):
        dma("sp", xT[:, kc, :], hs[:, kc * NT:(kc + 1) * NT], (), xkeys("x", kc, 0, NT))
    for up in range(8):
        wv, wk = wl(wout, up, 4096)
        w4o = wv.rearrange("p (a b c) -> p a b c", a=2, b=16, c=128)
        for d2 in range(2):
            dc = up * 2 + d2
            for (c0, n) in LIN_TILES:
                b = bank(PS_A)
                S.add("pe", proj(w4o[:, d2], 16, mT, None, c0, n, b), [wk], [("ps", b)])
                S.add("dve", lambda v, o=xT[:, dc, c0:c0 + n], i0=psb(b, n): v.tensor_tensor(out=o, in0=i0, in1=o, op=ALU.add),
                      [("ps", b)] + xkeys("x", dc, c0, n), xkeys("x", dc, c0, n))
    S.barrier()

    if DBG:
        dma("sp", dbg_x1, xT.rearrange("p a b -> p (a b)"), (), [])
        S.barrier()
    eal2 = Alloc(E_OFF, RING_OFF)
    ea2 = ffn_alloc(eal2, NT)
    ystage = [eal2.get([128, D], F32) for _ in range(2)]
    norm_to_bf(xT, "x", LIN_TILES, G_FFN2, xn, "xn", ea2)
    ffn(xT, "x", xn, "xn", LIN_TILES, wgu2, wd2, ea2)
    S.barrier()

    fa = Alloc(E_OFF, RING_OFF)
    f_ea = {"sq0": fa.get([128, 512], BF16), "sq1": fa.get([128, 512], BF16),
            "rstd": fa.get([128, 512], F32), "sqt": fa.get([128, 512], F32)}
    yT = [fa.get([128, 16, 384], F32) for _ in range(2)]
    y_own3 = y_own.rearrange("(k p) t -> p k t", p=128)
    y_smp3 = y_smp.rearrange("(k p) t -> p k t", p=128)
    FIN_TILES = [(0, 384), (384, 384), (768, 256), (1024, 128)]
    for ti, (c0, n) in enumerate(FIN_TILES):
        rstd = f_ea["rstd"][:, 0:n]
        rms_stats(lambda kc: (xT[:, kc, c0:c0 + n], []), 16, onesD, n, rstd, ("rstd", "fin"), [f_ea["sq0"], f_ea["sq1"]], f_ea["sqt"], "fin")
        yt = yT[ti % 2]
        for kc in range(16):
            S.add("dve", lambda v, o=yt[:, kc, 0:n], i=xT[:, kc, c0:c0 + n], s=gains[:, G_FIN + kc:G_FIN + kc + 1], r=rstd:
                  v.scalar_tensor_tensor(out=o, in0=i, scalar=s, in1=r, op0=ALU.mult, op1=ALU.mult),
                  [("rstd", "fin"), "gains"], [("yT", ti % 2, kc // 4)])
        for g in range(4):
            n_own = max(0, min(c0 + n, NOWN) - c0)
            lst = []
            if n_own > 0:
                lst.append((y_own3[:, g * 4:(g + 1) * 4, c0:c0 + n_own], yt[:, g * 4:(g + 1) * 4, 0:n_own]))
            if n_own < n:
                lst.append((y_smp3[:, g * 4:(g + 1) * 4, c0 + n_own - NOWN:c0 + n - NOWN], yt[:, g * 4:(g + 1) * 4, n_own:n]))
            S.add("sp", lambda e, lst=lst: [e.dma_start(out=o, in_=i) for (o, i) in lst], [("yT", ti % 2, g)], [], dma=len(lst))

    assert not prefetched, list(prefetched)
    S.emit_all(nc, stack)
    stack.close()
    return nc


_CACHE = {}


def kernel(x_prompt, x_sample, cache_ckv, cache_krope, state_pool,
           g_ffn1, w1_gate, w1_up, w1_down, g_mix, w_in, g_q_lat, g_kv_lat,
           w_uq, w_uk, w_uv, w_o_attn, w_pool, pool_scale, w_pool_out, w_out,
           g_ffn2, w2_gate, w2_up, w2_down, g_final):
    f = np.float32
    A = lambda a: np.ascontiguousarray(np.asarray(a, dtype=f))
    x_prompt, x_sample = A(x_prompt), A(x_sample)
    cache_ckv, cache_krope, state_pool = A(cache_ckv)[0], A(cache_krope)[0], A(state_pool)[0]
    wgu1, wd1 = prep_ffn(A(w1_gate)[0], A(w1_up)[0], A(w1_down)[0])
    wgu2, wd2 = prep_ffn(A(w2_gate)[0], A(w2_up)[0], A(w2_down)[0])
    win = A(w_in)[0]
    z_c, ql_c, kv_c, kr_c = win[:, 0:1024], win[:, 1024:1536], win[:, 1536:2048], win[:, 2048:2112]
    gA_c, gB_c = win[:, 2112:2112 + 2048], win[:, 2112 + 2048:2112 + 4096]
    kr2 = np.concatenate([kr_c, swap_half(kr_c)], axis=1)
    def pair_units(pc):
        n_ = pc.shape[0]
        return np.ascontiguousarray(pc.reshape(n_ // 2, 2, 128, 2048).transpose(0, 2, 1, 3)).reshape(n_ // 2, 128, 4096)
    wkvq = pair_units(prep_cols(np.concatenate([kv_c, ql_c], axis=1)))
    wkr_h = prep_cols(kr2)
    wz_h = pair_units(prep_cols(z_c))
    wgate = np.ascontiguousarray(np.stack([prep_cols(gA_c), prep_cols(gB_c)], axis=2)).reshape(16, 128, 4096)
    wp = A(w_pool)[0]
    wpool = np.ascontiguousarray(wp.reshape(4, 2, 128, 256).transpose(2, 0, 1, 3)).reshape(1, 128, 2048)
    uq = A(w_uq)[0]
    uk = A(w_uk)[0]
    uv = A(w_uv)[0]
    uqn = uq[:, :, 0:128]
    uqr = uq[:, :, 128:192]
    uqr2 = np.concatenate([uqr, swap_half(uqr)], axis=2)

    def per_head(Wh):
        return np.ascontiguousarray(Wh.reshape(4, 128, 16, 128).transpose(2, 1, 0, 3)).reshape(16, 128, 512)
    watt4 = np.ascontiguousarray(np.stack([per_head(uqn), per_head(uqr2), per_head(uk), per_head(uv)], axis=2)).reshape(16, 128, 2048)
    ukT = np.ascontiguousarray(uk.transpose(1, 2, 0))
    watt = watt4
    watt_s = np.ascontiguousarray(np.concatenate([per_head(uqn), per_head(uqr2), ukT], axis=2))
    wuv_s = np.ascontiguousarray(per_head(uv).reshape(2, 8, 128, 512).transpose(0, 2, 1, 3)).reshape(2, 128, 4096)
    wo_c = prep_cols(A(w_o_attn)[0])
    wpo_c = prep_cols(A(w_pool_out)[0])
    wmrg = np.ascontiguousarray(np.concatenate([wo_c, wpo_c], axis=2))
    wout_c = prep_cols(A(w_out)[0])
    wout = np.ascontiguousarray(wout_c.reshape(8, 2, 128, 2048).transpose(0, 2, 1, 3)).reshape(8, 128, 4096)

    def gcol(g, n):
        return np.asarray(g, dtype=f).reshape(n, 128).T
    gains = np.ascontiguousarray(np.concatenate([
        gcol(A(g_ffn1)[0], 16), gcol(A(g_mix)[0], 16), gcol(A(g_ffn2)[0], 16), gcol(A(g_final), 16),
        gcol(A(g_q_lat)[0], 4), gcol(A(g_kv_lat)[0], 4), gcol(A(pool_scale)[0], 8)], axis=1))
    ident = np.eye(128, dtype=f)
    cs_prev = rope_tables(np.arange(1024))
    pos_s = 1024 + (np.arange(128) % 32)

    shared = dict(wgu1=wgu1, wd1=wd1, wgu2=wgu2, wd2=wd2, wkvq=wkvq, wkr=wkr_h, wz=wz_h, wgate=wgate, wpool=wpool, watt=watt, watt_s=watt_s, wuv_s=wuv_s,
                  wmrg=wmrg, wout=wout, c_ident=ident, c_gains=gains, c_cs_prev=cs_prev)
    in_maps = []
    zeros_prev = np.zeros((D, NPREV), dtype=f)
    for c in range(8):
        b, half = c // 2, c % 2
        pos_o = half * 1024 + np.arange(1024)
        cs_main = np.ascontiguousarray(np.concatenate([rope_tables(pos_o), rope_tables(pos_s)], axis=2))
        rc = np.zeros((4, 16), dtype=f)
        for g, w in enumerate((2, 4, 8, 16)):
            rc[g] = 1.0 / np.minimum(pos_o[:16] + 1, w)
        m = dict(shared)
        m.update(
            x_own=np.ascontiguousarray(x_prompt[b, half * 1024:(half + 1) * 1024].T),
            x_prev=np.ascontiguousarray(x_prompt[b, 0:1024].T) if half == 1 else zeros_prev,
            x_smp=np.ascontiguousarray(x_sample[4 * c:4 * c + 4].reshape(128, D).T),
            cache_ckv=np.ascontiguousarray(cache_ckv[4 * c:4 * c + 4]),
            cache_ckvT=np.ascontiguousarray(cache_ckv[4 * c:4 * c + 4].transpose(0, 2, 1)),
            cache_krT=np.ascontiguousarray(cache_krope[4 * c:4 * c + 4].transpose(0, 2, 1)),
            state_pool=np.ascontiguousarray(state_pool[4 * c:4 * c + 4].reshape(60, 1024).T),
            c_cs_main=cs_main,
            c_rc16=np.ascontiguousarray(np.broadcast_to(rc.reshape(1, 64), (128, 64))).astype(f),
            c_prevb=np.full((128, 1), 0.0 if half == 1 else -1e30, dtype=f),
        )
        in_maps.append(m)

    if "nc" not in _CACHE:
        _CACHE["nc"] = build_program()
    nc = _CACHE["nc"]
    res = run_bass_kernel_spmd(nc, in_maps, core_ids=list(range(8)))
    R = res.results
    _CACHE["last"] = R

    y_prompt = np.zeros((4, 2048, D), f)
    y_sample = np.zeros((32, 32, D), f)
    ckv_p = np.zeros((1, 4, 2048, 512), f)
    kr_p = np.zeros((1, 4, 2048, 64), f)
    pool_p = np.zeros((1, 4, 15, 1024), f)
    ckv_s = np.zeros((1, 32, 32, 512), f)
    kr_s = np.zeros((1, 32, 32, 64), f)
    pool_s = np.zeros((1, 32, 15, 1024), f)
    for c in range(8):
        b, half = c // 2, c % 2
        r = R[c]
        sl = slice(half * 1024, (half + 1) * 1024)
        y_prompt[b, sl] = np.asarray(r["y_own"]).T
        ckvT = np.asarray(r["ckvT_out"])
        krT = np.asarray(r["krT_out"])
        ckv_p[0, b, sl] = ckvT[:, :NOWN].T
        kr_p[0, b, sl] = krT[:, :NOWN].T
        if half == 1:
            pool_p[0, b] = np.asarray(r["poolT_own"]).T
        y_sample[4 * c:4 * c + 4] = np.asarray(r["y_smp"]).T.reshape(4, 32, D)
        ckv_s[0, 4 * c:4 * c + 4] = ckvT[:, NOWN:].T.reshape(4, 32, 512)
        kr_s[0, 4 * c:4 * c + 4] = krT[:, NOWN:].T.reshape(4, 32, 64)
        pool_s[0, 4 * c:4 * c + 4] = np.asarray(r["poolT_smp"]).reshape(1024, 4, 15).transpose(1, 2, 0)
    return (y_prompt, y_sample, ckv_p, kr_p, pool_p, ckv_s, kr_s, pool_s)
```

```python
import numpy as np
from contextlib import ExitStack
import concourse.bass as bass
import concourse.mybir as mybir
from concourse.bass_utils import run_bass_kernel_spmd

F32 = mybir.dt.float32
BF16 = mybir.dt.bfloat16
AF = mybir.ActivationFunctionType
ALU = mybir.AluOpType

D = 2048
DFF = 5632
NOWN = 1024
NSMP = 128
NT = NOWN + NSMP
NPREV = 1024
EPS = 1e-6
SM_SCALE = 192 ** -0.5
ARENA = 211968
RING_SLOTS = 4
SLOT_B = 8192
RING_OFF = ARENA - RING_SLOTS * SLOT_B

ENGS = ["pe", "act", "dve", "pool", "sp"]


class Task:
    __slots__ = ("eng", "emit", "deps", "is_dma", "sem", "val", "prev_val", "has_dep", "ndma")

    def __init__(self, eng, emit, is_dma, ndma):
        self.eng = eng
        self.emit = emit
        self.deps = set()
        self.is_dma = is_dma
        self.ndma = ndma
        self.sem = None
        self.val = 0
        self.prev_val = 0
        self.has_dep = False


class RK:
    __slots__ = ("slot", "u", "state")

    def __init__(self, slot, u, state):
        self.slot, self.u, self.state = slot, u, state

    def __hash__(self):
        return hash(("ring", self.slot))

    def __eq__(self, o):
        return isinstance(o, RK) and o.slot == self.slot

    def check(self):
        assert self.state["u"] - self.u < RING_SLOTS, "stale ring slot use"


class Sched:
    def __init__(self):
        self.tasks = {e: [] for e in ENGS}
        self.lastw = {}
        self.readers = {}
        self.pending = {}
        self.dma_since = []

    def add(self, eng, emit, reads=(), writes=(), dma=0):
        t = Task(eng, emit, dma > 0, dma)
        deps = set()
        for r in reads:
            if isinstance(r, RK):
                r.check()
            w = self.lastw.get(r)
            if w is not None:
                deps.add(w)
        for k in writes:
            w = self.lastw.get(k)
            if w is not None:
                deps.add(w)
            rs = self.readers.get(k)
            if rs:
                deps.update(rs)
        for r in reads:
            self.readers.setdefault(r, []).append(t)
        for k in writes:
            self.lastw[k] = t
            self.readers[k] = []
        if eng in self.pending:
            deps |= self.pending.pop(eng)
        deps.discard(t)
        t.deps = deps
        for d in deps:
            d.has_dep = True
        self.tasks[eng].append(t)
        if t.is_dma:
            self.dma_since.append(t)
        return t

    def barrier(self):
        s = set(self.dma_since)
        for e in ENGS:
            if self.tasks[e]:
                s.add(self.tasks[e][-1])
        for t in s:
            t.has_dep = True
        for e in ENGS:
            self.pending[e] = self.pending.get(e, set()) | s
        self.dma_since = []
        self.lastw.clear()
        self.readers.clear()

    def emit_all(self, nc, stack):
        KD = 8
        csem = {}
        for e in ["pe", "act", "dve"]:
            n = sum(1 for t in self.tasks[e] if t.has_dep)
            ngen = n // 30000 + 1
            csem[e] = [stack.enter_context(nc.semaphore(f"c_{e}_{g}")) for g in range(ngen)]
            cnt = 0
            for t in self.tasks[e]:
                if t.has_dep:
                    t.sem = csem[e][cnt // 30000]
                    t.val = cnt % 30000 + 1
                    cnt += 1
        all_dma = []
        for e in ["pool", "sp"]:
            pool = [stack.enter_context(nc.semaphore(f"d_{e}_{k}")) for k in range(KD)]
            vals = [0] * KD
            for i, t in enumerate(self.tasks[e]):
                assert t.is_dma
                k = i % KD
                t.sem = pool[k]
                t.prev_val = vals[k]
                vals[k] += 16 * t.ndma
                t.val = vals[k]
            all_dma += [(pool[k], vals[k]) for k in range(KD) if vals[k] > 0]

        block = stack.enter_context(nc.Block())

        def run(ename, eng):
            waited = {}

            def wait(sem, val):
                key = id(sem)
                if waited.get(key, 0) < val:
                    eng.wait_ge(sem, val)
                    waited[key] = val

            for t in self.tasks[ename]:
                for d in t.deps:
                    if ename == "pe" and d.eng == "pe":
                        continue
                    wait(d.sem, d.val)
                if t.is_dma and t.prev_val > 0:
                    wait(t.sem, t.prev_val)
                r = t.emit(eng)
                if t.is_dma:
                    assert len(r) == t.ndma, (len(r), t.ndma)
                    for ins in r:
                        ins.then_inc(t.sem, 16)
                elif t.has_dep:
                    r.then_inc(t.sem, 1)
            if ename == "sp":
                for sem, val in all_dma:
                    wait(sem, val)

        @block.tensor
        def _(pe):
            run("pe", pe)

        @block.scalar
        def _(a):
            run("act", a)

        @block.vector
        def _(v):
            run("dve", v)

        @block.gpsimd
        def _(g):
            run("pool", g)

        @block.sync
        def _(sp):
            run("sp", sp)


def prep_cols(W):
    K, N = W.shape
    kc, ncn = K // 128, N // 128
    return np.ascontiguousarray(W.reshape(kc, 128, ncn, 128).transpose(2, 1, 0, 3)).reshape(ncn, 128, kc * 128)


def prep_ffn(wg, wu, wd):
    g = prep_cols(wg)
    u = prep_cols(wu)
    gu = np.ascontiguousarray(np.stack([g, u], axis=2)).reshape(44, 128, 4096)
    wds = []
    for q in range(4):
        c = prep_cols(wd[q * 1408:(q + 1) * 1408, :])
        c = c.reshape(8, 2, 128, 1408).transpose(0, 2, 1, 3).reshape(8, 128, 2816)
        wds.append(c)
    return gu, np.ascontiguousarray(np.concatenate(wds, axis=0))


def swap_half(W):
    return np.concatenate([W[..., 32:64], W[..., 0:32]], axis=-1)


def rope_tables(pos):
    half = 32
    inv = 10000.0 ** (-np.arange(half, dtype=np.float64) * 2.0 / 64)
    ang = pos.astype(np.float64)[:, None] * inv[None, :]
    c, s = np.cos(ang).astype(np.float32), np.sin(ang).astype(np.float32)
    C = np.concatenate([c, c], axis=1).T
    S = np.concatenate([-s, s], axis=1).T
    return np.ascontiguousarray(np.stack([C, S], axis=0)).astype(np.float32)


def build_program():
    nc = bass.Bass("TRN2", target_bir_lowering=False)
    S = Sched()

    def din(name, shape, dt=F32):
        return nc.dram_tensor(name, list(shape), dt, kind="ExternalInput").ap()

    def dout(name, shape, dt=F32):
        return nc.dram_tensor(name, list(shape), dt, kind="ExternalOutput").ap()

    x_own = din("x_own", [D, NOWN])
    x_prev = din("x_prev", [D, NPREV])
    x_smp = din("x_smp", [D, NSMP])
    cache_ckv = din("cache_ckv", [4, 1024, 512])
    cache_ckvT = din("cache_ckvT", [4, 512, 1024])
    cache_krT = din("cache_krT", [4, 64, 1024])
    state_pool = din("state_pool", [1024, 60])
    wgu1 = din("wgu1", [44, 128, 4096])
    wd1 = din("wd1", [32, 128, 2816])
    wgu2 = din("wgu2", [44, 128, 4096])
    wd2 = din("wd2", [32, 128, 2816])
    wkvq = din("wkvq", [4, 128, 4096])
    wkr_d = din("wkr", [1, 128, 2048])
    wz_d = din("wz", [4, 128, 4096])
    wgate = din("wgate", [16, 128, 4096])
    wpool = din("wpool", [1, 128, 2048])
    watt = din("watt", [16, 128, 2048])
    watt_s = din("watt_s", [16, 128, 1536])
    wuv_s = din("wuv_s", [2, 128, 4096])
    wmrg = din("wmrg", [16, 128, 3072])
    wout = din("wout", [8, 128, 4096])
    c_ident = din("c_ident", [128, 128])
    c_gains = din("c_gains", [128, 80])
    c_cs_main = din("c_cs_main", [2, 64, NT])
    c_cs_prev = din("c_cs_prev", [2, 64, NPREV])
    c_rc16 = din("c_rc16", [128, 64])
    c_prevb = din("c_prevb", [128, 1])

    y_own = dout("y_own", [D, NOWN])
    y_smp = dout("y_smp", [D, NSMP])
    ckvT_out = dout("ckvT_out", [512, NT])
    krT_out = dout("krT_out", [64, NT])
    poolT_own = dout("poolT_own", [1024, 15])
    poolT_smp = dout("poolT_smp", [1024, 60])

    hs = nc.dram_tensor("hs_scratch", [128, 16 * NT], F32).ap()
    import os
    DBG = os.environ.get("KDBG", "0") == "1"
    if DBG:
        dbg_mixed = dout("dbg_mixed", [128, 8 * NT], BF16)
        dbg_o = dout("dbg_o", [128, 16 * NT], BF16)
        dbg_m = dout("dbg_m", [128, 16 * NT], BF16)
        dbg_x1 = dout("dbg_x1", [128, 16 * NT], F32)
        dbg_pooled = dout("dbg_pooled", [128, 8 * NT], BF16)

    stack = ExitStack()
    arena_t = stack.enter_context(nc.sbuf_tensor("arena", [128, ARENA // 4], F32))
    psum_t = stack.enter_context(nc.psum_tensor("psum", [128, 4096], F32))

    def carve(off, shape, dt):
        esz = 4 if dt == F32 else 2
        P = shape[0]
        n = 1
        for s_ in shape[1:]:
            n *= s_
        nb = n * esz
        assert off % 4 == 0 and nb % 4 == 0 and off + nb <= ARENA, (off, shape)
        ap = arena_t[0:P, off // 4:(off + nb) // 4]
        if dt != F32:
            ap = ap.bitcast(dt)
        if len(shape) == 3:
            ap = ap.rearrange("p (a b) -> p a b", b=shape[2])
        elif len(shape) == 4:
            ap = ap.rearrange("p (a b c) -> p a b c", b=shape[2], c=shape[3])
        return ap

    class Alloc:
        def __init__(self, lo, hi):
            self.lo, self.hi, self.cur = lo, hi, lo

        def get(self, shape, dt):
            esz = 4 if dt == F32 else 2
            n = 1
            for s_ in shape[1:]:
                n *= s_
            nb = (n * esz + 31) // 32 * 32
            off = self.cur
            self.cur += nb
            assert self.cur <= self.hi, ("alloc overflow", shape, self.cur, self.hi)
            return carve(off, shape, dt)

    def psb(b, n=512, parts=128):
        return psum_t[0:parts, b * 512:b * 512 + n]

    def psb_bf(b, n, parts=128):
        return psum_t[0:parts, b * 512:b * 512 + 512].bitcast(BF16)[:, 0:n]

    ca = Alloc(0, 2176)
    ident_f = ca.get([128, 128], F32)
    ident_b = ca.get([128, 128], BF16)
    onesD = ca.get([128, 128], BF16)
    ones512 = ca.get([128, 128], BF16)
    ones1 = ca.get([128, 128], BF16)
    gains = ca.get([128, 80], F32)
    prevb = ca.get([128, 1], F32)
    epsc = ca.get([128, 1], F32)
    rc16 = ca.get([128, 4, 16], F32)
    G_FFN1, G_MIX, G_FFN2, G_FIN, G_Q, G_KV, G_PS = 0, 16, 32, 48, 64, 68, 72

    pa = Alloc(2176, 12928)
    ckvp_bf = pa.get([128, 4, 1024], BF16)
    krp_full = pa.get([128, 1024], BF16)
    krp_bf = krp_full[0:64, :]
    zhist = pa.get([128, 8, 16], F32)
    XN_OFF = 12928
    R0_OFF = XN_OFF + 36864
    E_OFF = R0_OFF + 73728
    assert E_OFF == 123520

    def dma(eng, out, in_, reads=(), writes=()):
        return S.add(eng, lambda e, o=out, i=in_: [e.dma_start(out=o, in_=i)], reads, writes, dma=1)

    dma("sp", ident_f, c_ident, (), ["ident_f"])
    dma("sp", gains, c_gains, (), ["gains"])
    dma("sp", prevb, c_prevb, (), ["prevb"])
    dma("sp", rc16.rearrange("p a b -> p (a b)"), c_rc16, (), ["rc16"])
    S.add("dve", lambda v: v.tensor_copy(out=ident_b, in_=ident_f), ["ident_f"], ["ident_b"])
    S.add("dve", lambda v: v.memset(onesD, 1.0 / 2048), (), ["onesD"])
    S.add("dve", lambda v: v.memset(ones512, 1.0 / 512), (), ["ones512"])
    S.add("dve", lambda v: v.memset(ones1, 1.0), (), ["ones1"])
    S.add("dve", lambda v: v.memset(epsc, EPS), (), ["epsc"])

    ring_state = {"u": 0}

    def wload(src_ap, ncols):
        u = ring_state["u"]
        ring_state["u"] += 1
        slot = u % RING_SLOTS
        assert ncols * 2 <= SLOT_B
        view = carve(RING_OFF + slot * SLOT_B, [128, ncols], BF16)
        key = RK(slot, u + 1, ring_state)
        dma("pool", view, src_ap, (), [key])
        return view, key

    prefetched = {}

    def wl(src, idx, ncols):
        k = (src.tensor.name, idx)
        if k in prefetched:
            return prefetched.pop(k)
        return wload(src[idx], ncols)

    def pf(src, idx, ncols):
        prefetched[(src.tensor.name, idx)] = wload(src[idx], ncols)

    psrot = {"i": 0}

    def bank(group):
        i = psrot.setdefault(id(group), 0)
        psrot[id(group)] = i + 1
        return group[i % len(group)]

    PS_A = [0, 1, 2, 3]
    PS_B = [4, 5]
    PS_C = [6, 7]

    def rms_stats(src_fn, nch, ones_ap, n, rstd_out, keyr, sq_bufs, sqt_buf, tag):
        b = bank(PS_C)
        for kc in range(nch):
            ap, keys = src_fn(kc)
            sq = sq_bufs[kc % 2]
            S.add("act", lambda a, o=sq[:, 0:n], i=ap: a.activation(out=o, in_=i, func=AF.Square),
                  list(keys), [("sq", tag, kc % 2)])
            S.add("pe", lambda pe, o=psb(b, n), r=sq[:, 0:n], st=(kc == 0), sp_=(kc == nch - 1):
                  pe.matmul(o, ones_ap, r, start=st, stop=sp_),
                  [("sq", tag, kc % 2), "onesD", "ones512"], [("ps", b)])
        S.add("act", lambda a, o=sqt_buf[:, 0:n], i=psb(b, n): a.activation(out=o, in_=i, func=AF.Sqrt, bias=epsc[:, 0:1], scale=1.0),
              [("ps", b), "epsc"], [("sqt", tag)])
        S.add("dve", lambda v, o=rstd_out, i=sqt_buf[:, 0:n]: v.reciprocal(out=o, in_=i),
              [("sqt", tag)], [keyr])

    def load_xT(x_dram, ntok, xT, col0, stage_bufs, tag):
        src3 = x_dram.rearrange("(k p) t -> p k t", p=128)
        for c0 in range(0, ntok, 512):
            n = min(512, ntok - c0)
            S.add("sp", lambda e, c0=c0, n=n: [e.dma_start(out=xT[:, g * 4:(g + 1) * 4, col0 + c0:col0 + c0 + n], in_=src3[:, g * 4:(g + 1) * 4, c0:c0 + n]) for g in range(4)],
                  (), [(tag, kc, c) for kc in range(16) for c in range(col0 + c0, col0 + c0 + n, 128)], dma=4)

    def xkeys(tag, kc, c0, n):
        return [(tag, kc, c) for c in range(c0 - c0 % 128, c0 + n, 128)]

    def norm_to_bf(xT, xtag, tiles, gcol, xn, xntag, ea):
        sq_bufs = [ea["sq0"], ea["sq1"]]
        for (c0, n) in tiles:
            rstd = ea["rstd"][:, 0:n]
            rms_stats(lambda kc: (xT[:, kc, c0:c0 + n], xkeys(xtag, kc, c0, n)), 16, onesD, n, rstd, ("rstd", xntag), sq_bufs, ea["sqt"], xntag)
            for kc in range(16):
                S.add("dve", lambda v, o=xn[:, kc, c0:c0 + n], i=xT[:, kc, c0:c0 + n], s=gains[:, gcol + kc:gcol + kc + 1], r=rstd:
                      v.scalar_tensor_tensor(out=o, in0=i, scalar=s, in1=r, op0=ALU.mult, op1=ALU.mult),
                      [("rstd", xntag), "gains"] + xkeys(xtag, kc, c0, n), xkeys(xntag, kc, c0, n))

    def ffn(xT, xtag, xn, xntag, tiles, wgu, wd, ea):
        hT = ea["hT"]
        for q in range(4):
            for fl in range(11):
                fc = q * 11 + fl
                wv, wk = wl(wgu, fc, 4096)
                wv4 = wv.rearrange("p (a b c) -> p a b c", a=2, b=16, c=128)
                for (c0, n) in tiles:
                    bg = bank(PS_A)
                    bu = bank(PS_A)
                    for which, b in ((0, bg), (1, bu)):
                        def mm(pe, which=which, b=b, c0=c0, n=n, wv4=wv4):
                            r = None
                            for kc in range(16):
                                r = pe.matmul(psb(b, n), wv4[:, which, kc, :], xn[:, kc, c0:c0 + n], start=(kc == 0), stop=(kc == 15))
                            return r
                        S.add("pe", mm, [wk] + [k for kc in range(16) for k in xkeys(xntag, kc, c0, n)], [("ps", b)])
                    sg = ea["sg"][bg % 2 if False else (psrot.setdefault("sg", 0) % 2)]
                    sgi = psrot["sg"] % 2
                    psrot["sg"] += 1
                    S.add("act", lambda a, o=sg[:, 0:n], i=psb(bg, n): a.activation(out=o, in_=i, func=AF.Silu),
                          [("ps", bg)], [("sg", sgi)])
                    S.add("dve", lambda v, o=hT[:, fl, c0:c0 + n], i0=sg[:, 0:n], i1=psb(bu, n): v.tensor_tensor(out=o, in0=i0, in1=i1, op=ALU.mult),
                          [("sg", sgi), ("ps", bu)], [("hT", fl, c0)])
            for dcp in range(8):
                wv, wk = wl(wd, q * 8 + dcp, 2816)
                wv4 = wv.rearrange("p (a b c) -> p a b c", a=2, b=11, c=128)
                for d2 in range(2):
                    dc = dcp * 2 + d2
                    for (c0, n) in tiles:
                        b = bank(PS_B)

                        def mm(pe, d2=d2, b=b, c0=c0, n=n, wv4=wv4):
                            r = None
                            for fl in range(11):
                                r = pe.matmul(psb(b, n), wv4[:, d2, fl, :], hT[:, fl, c0:c0 + n], start=(fl == 0), stop=(fl == 10))
                            return r
                        S.add("pe", mm, [wk] + [("hT", fl, c0) for fl in range(11)], [("ps", b)])
                        S.add("dve", lambda v, o=xT[:, dc, c0:c0 + n], i0=psb(b, n): v.scalar_tensor_tensor(out=o, in0=i0, scalar=0.5, in1=o, op0=ALU.mult, op1=ALU.add),
                              [("ps", b)] + xkeys(xtag, dc, c0, n), xkeys(xtag, dc, c0, n))

    def proj(wview, KC, xin, xintag, c0, n, b, M=128, mcol0=0, kc_keys=None):
        def mm(pe):
            r = None
            for kc in range(KC):
                r = pe.matmul(psb(b, n, M), wview[:, kc, mcol0:mcol0 + M], xin[:, kc, c0:c0 + n], start=(kc == 0), stop=(kc == KC - 1))
            return r
        return mm

    def feat_norm512(raw, rawtag, n, gcol, ea, outs, tagn):
        rstd = ea["rstd"][:, 0:n]
        rk_ = (lambda kc: (rawtag + (kc,)) if isinstance(rawtag, tuple) else (rawtag, kc))
        rms_stats(lambda kc: (raw[:, kc, 0:n], [rk_(kc)]), 4, ones512, n, rstd, ("rstd", tagn), [ea["sq0"], ea["sq1"]], ea["sqt"], tagn)
        for kc in range(4):
            for (o_ap, okey) in outs:
                S.add("dve", lambda v, o=o_ap(kc), i=raw[:, kc, 0:n], s=gains[:, gcol + kc:gcol + kc + 1], r=rstd:
                      v.scalar_tensor_tensor(out=o, in0=i, scalar=s, in1=r, op0=ALU.mult, op1=ALU.mult),
                      [("rstd", tagn), rk_(kc), "gains"], okey(kc))

    def dup_rows(dupI, full, c0, n, key):
        b = bank(PS_A)
        S.add("pe", lambda pe, o=psb(b, n), l=dupI, r=full[0:64, c0:c0 + n]: pe.matmul(o, l, r, start=True, stop=True),
              [key, "dupI"], [("ps", b)])
        S.add("act", lambda a, o=full[64:128, c0:c0 + n], i=psum_t[64:128, b * 512:b * 512 + n]: a.copy(out=o, in_=i),
              [("ps", b)], [(key, "dup")])

    def make_dupI(dupI):
        S.add("dve", lambda v, o=dupI[:, 0:64], i=ident_b[0:64, 0:64]: v.tensor_copy(out=o, in_=i), ["ident_b"], ["dupI0"])
        S.add("dve", lambda v, o=dupI[:, 64:128], i=ident_b[0:64, 0:64]: v.tensor_copy(out=o, in_=i), ["ident_b", "dupI0"], ["dupI"])

    def rope_from_ps(b0, b1, n, cs, ccol0, t1, t2, outs):
        S.add("dve", lambda v, o=t1[:, 0:n], i0=psb(b0, n, 64), i1=cs[:, 0, ccol0:ccol0 + n]: v.tensor_tensor(out=o, in0=i0, in1=i1, op=ALU.mult),
              [("ps", b0), "cs"], ["rt1"])
        S.add("dve", lambda v, o=t2[:, 0:n], i0=psb(b1, n, 64), i1=cs[:, 1, ccol0:ccol0 + n]: v.tensor_tensor(out=o, in0=i0, in1=i1, op=ALU.mult),
              [("ps", b1), "cs"], ["rt2"])
        for (o_ap, okeys) in outs:
            S.add("dve", lambda v, o=o_ap, i0=t1[:, 0:n], i1=t2[:, 0:n]: v.tensor_tensor(out=o, in0=i0, in1=i1, op=ALU.add),
                  ["rt1", "rt2"], okeys)

    def ffn_alloc(ea_alloc, ntok):
        ea = {}
        ea["hT"] = ea_alloc.get([128, 11, ntok], BF16)
        ea["sg"] = [ea_alloc.get([128, 512], F32) for _ in range(2)]
        ea["sq0"] = ea_alloc.get([128, 512], BF16)
        ea["sq1"] = ea_alloc.get([128, 512], BF16)
        ea["rstd"] = ea_alloc.get([128, 512], F32)
        ea["sqt"] = ea_alloc.get([128, 512], F32)
        return ea

    MAIN_TILES = [(0, 512), (512, 512), (1024, 128)]
    LIN_TILES = [(0, 384), (384, 384), (768, 384)]
    PREV_TILES = [(0, 512), (512, 512)]

    xn_prev = carve(XN_OFF, [128, 16, NPREV], BF16)
    xT_prev = carve(R0_OFF, [128, 16, NPREV], F32)
    eal = Alloc(E_OFF, RING_OFF)
    ea = ffn_alloc(eal, NT)
    xstage = [eal.get([128, D], F32) for _ in range(2)]

    load_xT(x_prev, NPREV, xT_prev, 0, xstage, "xp")
    norm_to_bf(xT_prev, "xp", PREV_TILES, G_FFN1, xn_prev, "xnp", ea)
    ffn(xT_prev, "xp", xn_prev, "xnp", PREV_TILES, wgu1, wd1, ea)
    norm_to_bf(xT_prev, "xp", PREV_TILES, G_MIX, xn_prev, "xnp", ea)
    pf(wkvq, 0, 4096)
    pf(wkvq, 1, 4096)
    pf(wkr_d, 0, 2048)
    S.barrier()
    xn = carve(XN_OFF, [128, 16, NT], BF16)
    xT = carve(R0_OFF, [128, 16, NT], F32)
    pal = Alloc(E_OFF, RING_OFF)
    p_raw = pal.get([128, 4, 512], F32)
    p_cs = pal.get([64, 2, NPREV], F32)
    p_t1 = pal.get([64, 512], F32)
    p_t2 = pal.get([64, 512], F32)
    p_ea = {"sq0": pal.get([128, 512], BF16), "sq1": pal.get([128, 512], BF16),
            "rstd": pal.get([128, 512], F32), "sqt": pal.get([128, 512], F32)}
    p_dupI = pal.get([64, 128], BF16)
    make_dupI(p_dupI)
    dma("sp", p_cs, c_cs_prev.rearrange("a p t -> p a t"), (), ["cs"])
    load_xT(x_own, NOWN, xT, 0, xstage, "x")
    load_xT(x_smp, NSMP, xT, NOWN, xstage, "x")
    wkv = []
    for u_ in range(2):
        wv, wk = wl(wkvq, u_, 4096)
        w4_ = wv.rearrange("p (a b c) -> p a b c", a=2, b=16, c=128)
        wkv += [(w4_[:, 0], wk), (w4_[:, 1], wk)]
    def prev_kv(ti, c0, n):
        for ch in range(4):
            b = bank(PS_A)
            S.add("pe", proj(wkv[ch][0], 16, xn_prev, "xnp", c0, n, b),
                  [wkv[ch][1]] + [k for kc in range(16) for k in xkeys("xnp", kc, c0, n)], [("ps", b)])
            S.add("act", lambda a, o=p_raw[:, ch, 0:n], i=psb(b, n): a.copy(out=o, in_=i), [("ps", b)], [("praw", ch)])
        feat_norm512(p_raw, "praw", n, G_KV, p_ea,
                     [(lambda kc, c0=c0, n=n: ckvp_bf[:, kc, c0:c0 + n], lambda kc, c0=c0: [("ckvp", kc, c0)])], "pkv")
    prev_kv(0, *PREV_TILES[0])
    wv, wk = wl(wkr_d, 0, 2048)
    wkr = wv.rearrange("p (a b) -> p a b", b=128)
    for ti, (c0, n) in enumerate(PREV_TILES):
        b0 = bank(PS_A)
        b1 = bank(PS_A)
        rk = [wk] + [k for kc in range(16) for k in xkeys("xnp", kc, c0, n)]
        S.add("pe", proj(wkr, 16, xn_prev, "xnp", c0, n, b0, M=64, mcol0=0), rk, [("ps", b0)])
        S.add("pe", proj(wkr, 16, xn_prev, "xnp", c0, n, b1, M=64, mcol0=64), rk, [("ps", b1)])
        rope_from_ps(b0, b1, n, p_cs, c0, p_t1, p_t2, [(krp_bf[:, c0:c0 + n], [("krp", c0)])])
        dup_rows(p_dupI, krp_full, c0, n, ("krp", c0))
    prev_kv(1, *PREV_TILES[1])
    for zc in range(8):
        if zc % 2 == 0:
            wv, wk = wl(wz_d, zc // 2, 4096)
            wz4 = wv.rearrange("p (a b c) -> p a b c", a=2, b=16, c=128)
        wz = wz4[:, zc % 2]
        b = bank(PS_A)
        S.add("pe", proj(wz, 16, xn_prev, "xnp", NPREV - 16, 16, b),
              [wk] + [k for kc in range(16) for k in xkeys("xnp", kc, NPREV - 16, 16)], [("ps", b)])
        S.add("act", lambda a, o=zhist[:, zc, :], i=psb(b, 16): a.copy(out=o, in_=i), [("ps", b)], [("zhist", zc)])
    rstd3 = carve(E_OFF + 36864, [128, NT], F32)
    for (c0, n) in LIN_TILES:
        rms_stats(lambda kc: (xT[:, kc, c0:c0 + n], xkeys("x", kc, c0, n)), 16, onesD, n, rstd3[:, c0:c0 + n], ("rstd3", c0),
                  [ea["sq0"], ea["sq1"]], ea["sqt"], "xn")
    S.barrier()

    for (c0, n) in LIN_TILES:
        for kc in range(16):
            S.add("dve", lambda v, o=xn[:, kc, c0:c0 + n], i=xT[:, kc, c0:c0 + n], s=gains[:, G_FFN1 + kc:G_FFN1 + kc + 1], r=rstd3[:, c0:c0 + n]:
                  v.scalar_tensor_tensor(out=o, in0=i, scalar=s, in1=r, op0=ALU.mult, op1=ALU.mult),
                  ["gains"], xkeys("xn", kc, c0, n))
    ffn(xT, "x", xn, "xn", LIN_TILES, wgu1, wd1, ea)
    norm_to_bf(xT, "x", LIN_TILES, G_MIX, xn, "xn", ea)
    pf(wkvq, 0, 4096)
    pf(wkvq, 1, 4096)
    pf(wkr_d, 0, 2048)
    S.barrier()
    spill_t = S.add("sp", lambda e: [e.dma_start(out=hs[:, kc * NT:(kc + 1) * NT], in_=xT[:, kc, :]) for kc in range(16)], (), ["hs"], dma=16)
    spill_t.has_dep = True
    S.pending["dve"] = S.pending.get("dve", set()) | {spill_t}

    r0 = Alloc(R0_OFF, E_OFF)
    oT = r0.get([128, 16, NT], BF16)
    mixedT = r0.get([128, 8, NT], BF16)
    qln = r0.get([128, 4, NT], BF16)
    ckvbf = r0.get([128, 4, NT], BF16)
    pooledT = carve(R0_OFF, [128, 8, NT], BF16)
    me = Alloc(E_OFF, RING_OFF)
    krbf_full = me.get([128, NT], BF16)
    krbf = krbf_full[0:64, :]
    cs = me.get([64, 2, NT], F32)
    m_dupI = me.get([64, 128], BF16)
    make_dupI(m_dupI)
    T_OFF = me.cur
    dma("sp", cs, c_cs_main.rearrange("a p t -> p a t"), (), ["cs"])

    ta = Alloc(T_OFF, RING_OFF)
    raw2 = [ta.get([128, 4, 512], F32) for _ in range(2)]
    kvn2 = [ta.get([128, 4, 512], F32)] * 2
    t1 = ta.get([64, 512], F32)
    t2 = ta.get([64, 512], F32)
    krn2 = [ta.get([64, 512], F32) for _ in range(2)]
    ckvT_out3 = ckvT_out.rearrange("(k p) t -> p k t", p=128)
    m_ea = {"sq0": ta.get([128, 512], BF16), "sq1": ta.get([128, 512], BF16),
            "rstd": ta.get([128, 512], F32), "sqt": ta.get([128, 512], F32)}

    wkv = []
    for u_ in range(2):
        wv, wk = wl(wkvq, u_, 4096)
        w4_ = wv.rearrange("p (a b c) -> p a b c", a=2, b=16, c=128)
        wkv += [(w4_[:, 0], wk), (w4_[:, 1], wk)]
    wv, wk = wl(wkr_d, 0, 2048)
    wkr = (wv.rearrange("p (a b) -> p a b", b=128), wk)
    for ti, (c0, n) in enumerate(MAIN_TILES):
        for ch in range(4):
            b = bank(PS_A)
            S.add("pe", proj(wkv[ch][0], 16, xn, "xn", c0, n, b),
                  [wkv[ch][1]] + [k for kc in range(16) for k in xkeys("xn", kc, c0, n)], [("ps", b)])
            S.add("act", lambda a, o=raw2[ti % 2][:, ch, 0:n], i=psb(b, n): a.copy(out=o, in_=i), [("ps", b)], [("rawkv", ti % 2, ch)])
        kvn = kvn2[ti % 2]
        krn = krn2[ti % 2]
        feat_norm512(raw2[ti % 2], ("rawkv", ti % 2), n, G_KV, m_ea,
                     [(lambda kc, n=n, kvn=kvn: kvn[:, kc, 0:n], lambda kc, ti=ti: [("kvn", 0, kc)])], "kv")
        for kc in range(4):
            S.add("act", lambda a, o=ckvbf[:, kc, c0:c0 + n], i=kvn[:, kc, 0:n]: a.copy(out=o, in_=i),
                  [("kvn", 0, kc)], [("ckvbf", kc, c0)])
        b0 = bank(PS_A)
        b1 = bank(PS_A)
        rk = [wkr[1]] + [k for kc in range(16) for k in xkeys("xn", kc, c0, n)]
        S.add("pe", proj(wkr[0], 16, xn, "xn", c0, n, b0, M=64, mcol0=0), rk, [("ps", b0)])
        S.add("pe", proj(wkr[0], 16, xn, "xn", c0, n, b1, M=64, mcol0=64), rk, [("ps", b1)])
        rope_from_ps(b0, b1, n, cs, c0, t1, t2, [(krn[:, 0:n], [("krn", ti % 2)]), (krbf[:, c0:c0 + n], [("krbf", c0)])])
        if c0 < NOWN:
            dup_rows(m_dupI, krbf_full, c0, n, ("krbf", c0))
        dma("sp", ckvT_out3[:, :, c0:c0 + n], kvn[:, :, 0:n], [("kvn", 0, kc) for kc in range(4)], [])
        dma("sp", krT_out[:, c0:c0 + n], krn[:, 0:n], [("krn", ti % 2)], [])
    wq = []
    for u_ in range(2):
        wv, wk = wl(wkvq, 2 + u_, 4096)
        w4_ = wv.rearrange("p (a b c) -> p a b c", a=2, b=16, c=128)
        wq += [(w4_[:, 0], wk), (w4_[:, 1], wk)]
    for ti, (c0, n) in enumerate(MAIN_TILES):
        for ch in range(4):
            b = bank(PS_A)
            S.add("pe", proj(wq[ch][0], 16, xn, "xn", c0, n, b),
                  [wq[ch][1]] + [k for kc in range(16) for k in xkeys("xn", kc, c0, n)], [("ps", b)])
            S.add("act", lambda a, o=raw2[(ti + 1) % 2][:, ch, 0:n], i=psb(b, n): a.copy(out=o, in_=i), [("ps", b)], [("rawq", ti % 2, ch)])
        feat_norm512(raw2[(ti + 1) % 2], ("rawq", ti % 2), n, G_Q, m_ea,
                     [(lambda kc, c0=c0, n=n: qln[:, kc, c0:c0 + n], lambda kc, c0=c0: [("qln", kc, c0)])], "ql")

    pf(wz_d, 0, 4096)
    pf(wz_d, 1, 4096)
    S.barrier()
    tb_ = Alloc(T_OFF, RING_OFF)
    zxb = [tb_.get([128, 1232], F32) for _ in range(2)]
    ppa = tb_.get([128, 1232], F32)
    ppb = tb_.get([128, 1232], F32)
    fx16 = tb_.get([128, 16], F32)
    SEG = [(0, 1024, 0)] + [(1040 + 48 * j, 32, NOWN + 32 * j) for j in range(4)]
    for zc in range(8):
        g = zc // 2
        w = 2 << g
        if zc % 2 == 0:
            wv, wk = wl(wz_d, zc // 2, 4096)
            wz4 = wv.rearrange("p (a b c) -> p a b c", a=2, b=16, c=128)
        wz = wz4[:, zc % 2]
        zx = zxb[zc % 2]
        zk = ("zx", zc % 2)
        S.add("dve", lambda v, o=zx[:, 0:16], i=zhist[:, zc, :]: v.tensor_copy(out=o, in_=i), [("zhist", zc)], [(zk, "h0")])
        dma("sp", zx[:, 1040:1232].rearrange("p (j t) -> p j t", t=48)[:, :, 1:16],
            state_pool[zc * 128:(zc + 1) * 128, :].rearrange("p (j t) -> p j t", t=15), (), [(zk, "h", j) for j in range(4)])
        for (c0, n) in MAIN_TILES:
            b = bank(PS_A)
            S.add("pe", proj(wz, 16, xn, "xn", c0, n, b),
                  [wk] + [k for kc in range(16) for k in xkeys("xn", kc, c0, n)], [("ps", b)])
            if c0 < NOWN:
                S.add("act", lambda a, o=zx[:, 16 + c0:16 + c0 + n], i=psb(b, n): a.copy(out=o, in_=i), [("ps", b)], [(zk, "z", c0)])
            else:
                dst = zx[:, 1040:1232].rearrange("p (j t) -> p j t", t=48)[:, :, 16:48]
                src = psb(b, n).rearrange("p (j t) -> p j t", t=32)
                S.add("act", lambda a, o=dst, i=src: a.copy(out=o, in_=i), [("ps", b)], [(zk, "z", c0)])
        allz = [(zk, "h0")] + [(zk, "h", j) for j in range(4)] + [(zk, "z", c0) for (c0, n) in MAIN_TILES]
        dma("sp", poolT_own[zc * 128:(zc + 1) * 128, :], zx[:, 1025:1040], allz, [])
        dma("sp", poolT_smp[zc * 128:(zc + 1) * 128, :].rearrange("p (j t) -> p j t", t=15),
            zx[:, 1040:1232].rearrange("p (j t) -> p j t", t=48)[:, :, 33:48], allz, [])
        cur, curk = zx, allz
        k = 1
        tog = 0
        while k < w:
            nxt = ppa if tog == 0 else ppb
            nk = ["ppa"] if tog == 0 else ["ppb"]
            S.add("dve", lambda v, o=nxt[:, k:1232], i0=cur[:, k:1232], i1=cur[:, 0:1232 - k]: v.tensor_tensor(out=o, in0=i0, in1=i1, op=ALU.add),
                  curk, nk)
            S.add("dve", lambda v, o=nxt[:, 0:k], i=cur[:, 0:k]: v.tensor_copy(out=o, in_=i), curk, [nk[0] + "h"])
            cur, curk = nxt, nk + [nk[0] + "h"]
            tog ^= 1
            k *= 2
        for si_, (h0, nt, tc) in enumerate(SEG):
            S.add("dve", lambda v, o=pooledT[:, zc, tc:tc + nt], i0=cur[:, h0 + 16:h0 + 16 + nt], i1=zx[:, h0 + 16:h0 + 16 + nt], sc=1.0 / w:
                  v.scalar_tensor_tensor(out=o, in0=i0, scalar=sc, in1=i1, op0=ALU.mult, op1=ALU.subtract),
                  curk + allz, [("pooled", zc, tc)])
        S.add("dve", lambda v, o=fx16, i0=cur[:, 16:32], i1=rc16[:, g, :]:
              v.tensor_tensor(out=o, in0=i0, in1=i1, op=ALU.mult), curk + ["rc16"], ["fix16"])
        S.add("dve", lambda v, o=pooledT[:, zc, 0:16], i0=fx16, i1=zx[:, 16:32]: v.tensor_tensor(out=o, in0=i0, in1=i1, op=ALU.subtract),
              ["fix16"] + allz + [("pooled", zc, 0)], [("pooled", zc, 0)])
    wv, wpk = wl(wpool, 0, 2048)
    wp4 = wv.rearrange("p (a b c) -> p a b c", a=4, b=2, c=256)
    for (c0, n) in MAIN_TILES:
        for g in range(4):
            for dd in range(2):
                b = bank(PS_A)

                def mm(pe, g=g, dd=dd, b=b, c0=c0, n=n):
                    r = None
                    for cc in range(2):
                        r = pe.matmul(psb(b, n), wp4[:, g, cc, dd * 128:(dd + 1) * 128], pooledT[:, 2 * g + cc, c0:c0 + n], start=(cc == 0), stop=(cc == 1))
                    return r
                rk = [wpk] + [("pooled", 2 * g + cc, tc) for cc in range(2) for tc in ([0, 512] if c0 < NOWN else [NOWN + 32 * j for j in range(4)])]
                S.add("pe", mm, rk, [("ps", b)])
                S.add("dve", lambda v, o=mixedT[:, 2 * g + dd, c0:c0 + n], i=psb(b, n), s=gains[:, G_PS + 2 * g + dd:G_PS + 2 * g + dd + 1]:
                      v.tensor_scalar_mul(out=o, in0=i, scalar1=s), [("ps", b), "gains"], [("mixed", 2 * g + dd, c0)])
    pf(watt, 0, 2048)
    pf(watt, 1, 2048)
    S.barrier()
    if DBG:
        dma("sp", dbg_pooled, pooledT.rearrange("p a b -> p (a b)"), (), [])
        dma("sp", dbg_mixed, mixedT.rearrange("p a b -> p (a b)"), (), [])
        S.barrier()

    aa = Alloc(T_OFF, RING_OFF)
    WS = []
    for _ in range(2):
        WS.append({"qn": aa.get([128, 1024], BF16), "qr": aa.get([128, 1024], BF16),
                   "kn": aa.get([128, 2048], BF16), "v": aa.get([128, 16, 128], BF16)})
    pts = [aa.get([128, 512], BF16) for _ in range(4)]
    rs = aa.get([128, 512], F32)
    cs2 = aa.get([128, 1024], F32)
    dma("sp", cs2[0:64, :], c_cs_main[0, :, 0:1024], (), ["cs2a"])
    dma("sp", cs2[64:128, :], c_cs_main[1, :, 0:1024], (), ["cs2b"])
    A2_OFF = aa.cur
    PS_S = [0, 1, 2]
    PS_O = [3]
    PS_SUM = [4]
    PS_P = [5, 6, 7]
    bo_sb = [aa.get([128, 512], F32) for _ in range(2)]
    bs_sb = [aa.get([128, 512], F32) for _ in range(2)]
    fin_i = [0]
    pt_i = [0]

    def attend(ws, wsk, h, qcol0, nq, ocol0, blocks):
        LOOK = 2
        bo = bank(PS_O)
        bs = bank(PS_SUM)
        nb = len(blocks)
        ptinfo = {}
        for it in range(nb + LOOK):
            if it < nb:
                (kn_ap, kr_ap, v_ap, nk, bias_ap, qoff, mask, keys) = blocks[it]
                nqq = nq - qoff
                b = bank(PS_S)

                def mm(pe, b=b, kn_ap=kn_ap, kr_ap=kr_ap, nk=nk, qoff=qoff, nqq=nqq):
                    pe.matmul(psb(b, nqq, nk), kn_ap, ws["qn"][:, qcol0 + qoff:qcol0 + qoff + nqq], start=True, stop=False)
                    return pe.matmul(psb(b, nqq, nk), kr_ap, ws["qr"][:, qcol0 + qoff:qcol0 + qoff + nqq], start=False, stop=True)
                S.add("pe", mm, keys + [(wsk, "qn"), (wsk, "qr")], [("ps", b)])
                pi = pt_i[0] % len(pts)
                pt_i[0] += 1
                pt = pts[pi]
                ptinfo[it] = (pi, pt)
                if bias_ap is None:
                    S.add("act", lambda a, o=pt[0:nk, 0:nqq], i=psb(b, nqq, nk): a.activation(out=o, in_=i, func=AF.Exp, scale=SM_SCALE),
                          [("ps", b)], [("pt", pi)])
                else:
                    S.add("act", lambda a, o=pt[0:nk, 0:nqq], i=psb(b, nqq, nk), bb=bias_ap: a.activation(out=o, in_=i, func=AF.Exp, bias=bb, scale=SM_SCALE),
                          [("ps", b), "prevb"], [("pt", pi)])
                if mask:
                    S.add("dve", lambda v, o=pt[64:128, 0:64]: v.memset(o, 0.0), [("pt", pi)], [("pt", pi)])
            bi = it - LOOK
            if bi >= 0:
                (kn_ap, kr_ap, v_ap, nk, bias_ap, qoff, mask, keys) = blocks[bi]
                nqq = nq - qoff
                pi, pt = ptinfo[bi]
                S.add("pe", lambda pe, o=psb(bo, nq)[:, qoff:nq], l=v_ap, r=pt[0:nk, 0:nqq], st=(bi == 0), sp_=(bi == nb - 1):
                      pe.matmul(o, l, r, start=st, stop=sp_), [("pt", pi), (wsk, "v")] + keys, [("ps", bo)])
                S.add("pe", lambda pe, o=psb(bs, nq)[:, qoff:nq], l=ones1[0:nk, :], r=pt[0:nk, 0:nqq], st=(bi == 0), sp_=(bi == nb - 1):
                      pe.matmul(o, l, r, start=st, stop=sp_), [("pt", pi), "ones1"], [("ps", bs)])
        ai = fin_i[0] % 2
        fin_i[0] += 1
        S.add("act", lambda a, o=bs_sb[ai][:, 0:nq], i=psb(bs, nq): a.copy(out=o, in_=i), [("ps", bs)], [("bs_sb", ai)])
        S.add("act", lambda a, o=bo_sb[ai][:, 0:nq], i=psb(bo, nq): a.copy(out=o, in_=i), [("ps", bo)], [("bo_sb", ai)])
        def fin():
            S.add("dve", lambda v, o=rs[:, 0:nq], i=bs_sb[ai][:, 0:nq]: v.reciprocal(out=o, in_=i), [("bs_sb", ai)], ["rs"])
            S.add("dve", lambda v, o=oT[:, h, ocol0:ocol0 + nq], i0=bo_sb[ai][:, 0:nq], i1=rs[:, 0:nq]: v.tensor_tensor(out=o, in0=i0, in1=i1, op=ALU.mult),
                  [("bo_sb", ai), "rs"], [("oT", h, ocol0)])
        return fin

    head_full = {}

    def head_weights(h):
        wv, wk = wl(watt, h, 2048)
        w4 = wv.rearrange("p (a b c) -> p a b c", a=4, b=4, c=128)
        return w4, wk

    def head_weights_s(h):
        wv, wk = wl(watt_s, h, 1536)
        head_full[h] = wv
        w4 = wv[:, 0:1024].rearrange("p (a b c) -> p a b c", a=2, b=4, c=128)
        return w4, wk

    def q_proj(w4, wk, ws, wsk, c0, n, dcol0, cscol0):
        rk = [wk] + [("qln", kc, c) for kc in range(4) for c in range(c0 - c0 % 128, c0 + n, 128)] + \
             [("qln", kc, cc) for kc in range(4) for cc in (0, 512, 1024)]
        b = bank(PS_P)
        S.add("pe", proj(w4[:, 0], 4, qln, "qln", c0, n, b), rk, [("ps", b)])
        S.add("act", lambda a, o=ws["qn"][:, dcol0:dcol0 + n], i=psb(b, n): a.copy(out=o, in_=i), [("ps", b)], [(wsk, "qn")])
        b0 = bank(PS_P)
        S.add("pe", proj(w4[:, 1], 4, qln, "qln", c0, n, b0, M=128, mcol0=0), rk, [("ps", b0)])
        S.add("dve", lambda v, o=ws["qr"][:, dcol0:dcol0 + n], i0=psb(b0, n), i1=cs2[:, cscol0:cscol0 + n]: v.tensor_tensor(out=o, in0=i0, in1=i1, op=ALU.mult),
              [("ps", b0), "cs2a", "cs2b"], [(wsk, "qr")])

    def k_proj(w4, wk, ws, wsk, src, srckeys, c0, n, dcol0):
        b = bank(PS_P)
        S.add("pe", proj(w4[:, 2], 4, src, None, c0, n, b), [wk] + srckeys, [("ps", b)])
        S.add("dve", lambda v, o=ws["kn"][:, dcol0:dcol0 + n], i=psb(b, n): v.tensor_copy(out=o, in_=i), [("ps", b)], [(wsk, "kn")])

    def v_proj(w4, wk, ws, wsk, src, srckeys, c0, nk, blk):
        b = bank(PS_P)

        def mm(pe):
            r = None
            for kc in range(4):
                r = pe.matmul(psb(b, 128, nk), src[:, kc, c0:c0 + nk], w4[:, 3, kc, :], start=(kc == 0), stop=(kc == 3))
            return r
        S.add("pe", mm, [wk] + srckeys, [("ps", b)])
        S.add("act", lambda a, o=ws["v"][0:nk, blk, :], i=psb(b, 128, nk): a.copy(out=o, in_=i), [("ps", b)], [(wsk, "v")])

    def v_proj4(w4, wk, ws, wsk, src, c0, blk0):
        b = bank(PS_P)

        def mm(pe):
            r = None
            for j in range(4):
                for kc in range(4):
                    r = pe.matmul(psb(b)[:, j * 128:(j + 1) * 128], src[:, kc, c0 + j * 128:c0 + (j + 1) * 128], w4[:, 3, kc, :],
                                  start=(kc == 0), stop=(kc == 3))
            return r
        S.add("pe", mm, [wk], [("ps", b)])
        S.add("act", lambda a, o=ws["v"][:, blk0:blk0 + 4, :], i=psb(b).rearrange("p (a b) -> p a b", b=128): a.copy(out=o, in_=i),
              [("ps", b)], [(wsk, "v")])

    ckvp_keys = []
    own_keys = []
    def head_proj(h):
        ws = WS[h % 2]
        wsk = ("ws", h % 2)
        w4, wk = head_weights(h)
        for (c0, n) in [(0, 512), (512, 512)]:
            q_proj(w4, wk, ws, wsk, c0, n, c0, c0)
        for (c0, n) in [(0, 512), (512, 512)]:
            k_proj(w4, wk, ws, wsk, ckvp_bf, [], c0, n, c0)
            k_proj(w4, wk, ws, wsk, ckvbf, [], c0, n, 1024 + c0)
        for g4 in range(2):
            v_proj4(w4, wk, ws, wsk, ckvp_bf, g4 * 512, g4 * 4)
        for g4 in range(2):
            v_proj4(w4, wk, ws, wsk, ckvbf, g4 * 512, 8 + g4 * 4)

    def head_attend(h, qt):
        ws = WS[h % 2]
        wsk = ("ws", h % 2)
        q0 = qt * 512
        blocks = []
        for blk in range(8):
            blocks.append((ws["kn"][:, blk * 128:(blk + 1) * 128], krp_full[:, blk * 128:(blk + 1) * 128], ws["v"][:, blk, :], 128,
                           prevb[:, 0:1], 0, False, [(wsk, "kn")]))
        for blk in range(4 * qt):
            blocks.append((ws["kn"][:, 1024 + blk * 128:1024 + (blk + 1) * 128], krbf_full[:, blk * 128:(blk + 1) * 128], ws["v"][:, 8 + blk, :], 128,
                           None, 0, False, [(wsk, "kn")]))
        for j in range(4):
            blk = 4 * qt + j
            blocks.append((ws["kn"][:, 1024 + blk * 128:1024 + (blk + 1) * 128], krbf_full[:, blk * 128:(blk + 1) * 128], ws["v"][:, 8 + blk, :], 128,
                           None, j * 128, True, [(wsk, "kn")]))
        return attend(ws, wsk, h, q0, 512, q0, blocks)

    head_proj(0)
    for h in range(16):
        fin0 = head_attend(h, 0)
        if h + 1 < 16:
            head_proj(h + 1)
        fin0()
        fin1 = head_attend(h, 1)
        fin1()
    pf(watt_s, 0, 1536)
    pf(watt_s, 1, 1536)
    S.barrier()

    SC0 = NOWN
    sa = Alloc(E_OFF + 2304 + 9216, RING_OFF)
    qabs = sa.get([128, 4, 4, 512], BF16)
    qrs = sa.get([64, 4, 512], BF16)
    qn_s = [sa.get([128, 128], BF16) for _ in range(2)]
    s_t1 = sa.get([64, 128], F32)
    s_t2 = sa.get([64, 128], F32)
    S2_OFF = sa.cur
    PS_Q = [0, 1, 2, 3]
    sb_ = Alloc(E_OFF + 2304, E_OFF + 2304 + 9216)
    sc_ = Alloc(S2_OFF, RING_OFF)
    ctok2 = [sc_.get([128, 4, 512], BF16), sb_.get([128, 4, 512], BF16)]
    cacheT2 = [sc_.get([128, 4, 512], BF16), sb_.get([128, 4, 512], BF16)]
    ckrT2 = [sc_.get([64, 512], BF16), sb_.get([64, 512], BF16)]
    new_tok = sc_.get([32, 512], BF16)
    pts = [sc_.get([128, 512], BF16) for _ in range(3)]
    rs = sc_.get([128, 512], F32)
    PS_S2 = [0, 1]
    PS_PV = [2, 3, 4, 5]
    PS_SM = 6
    PS_T = [7]
    pt_j = [0]

    def cache_load(u):
        bb, hf = u // 2, u % 2
        dma("pool", ctok2[u % 2], cache_ckv[bb, hf * 512:(hf + 1) * 512, :].rearrange("(k p) c -> p k c", p=128), (), [("ctok", u % 2)])
        dma("pool", cacheT2[u % 2], cache_ckvT[bb].rearrange("(k p) t -> p k t", p=128)[:, :, hf * 512:(hf + 1) * 512], (), [("cacheT", u % 2)])
        dma("pool", ckrT2[u % 2], cache_krT[bb][:, hf * 512:(hf + 1) * 512], (), [("ckrT", u % 2)])

    cache_load(0)
    s1 = {}

    def s1_a(h):
        w4, wk = head_weights_s(h)
        wukT = head_full[h][:, 1024:1536]
        rk = [wk] + [("qln", kc, SC0) for kc in range(4)]
        b = bank(PS_Q)
        S.add("pe", proj(w4[:, 0], 4, qln, "qln", SC0, 128, b), rk, [("ps", b)])
        qs = qn_s[h % 2]
        S.add("act", lambda a_, o=qs, i=psb(b, 128): a_.copy(out=o, in_=i), [("ps", b)], [("qn_s", h % 2)])
        b0 = bank(PS_Q)
        S.add("pe", proj(w4[:, 1], 4, qln, "qln", SC0, 128, b0, M=64, mcol0=0), rk, [("ps", b0)])
        b1 = bank(PS_Q)
        S.add("pe", proj(w4[:, 1], 4, qln, "qln", SC0, 128, b1, M=64, mcol0=64), rk, [("ps", b1)])
        S.add("dve", lambda v, o=s_t1, i0=psb(b0, 128, 64), i1=cs[:, 0, SC0:SC0 + 128]: v.tensor_tensor(out=o, in0=i0, in1=i1, op=ALU.mult),
              [("ps", b0), "cs"], ["s_t1"])
        S.add("dve", lambda v, o=s_t2, i0=psb(b1, 128, 64), i1=cs[:, 1, SC0:SC0 + 128]: v.tensor_tensor(out=o, in0=i0, in1=i1, op=ALU.mult),
              [("ps", b1), "cs"], ["s_t2"])
        S.add("dve", lambda v, o=qrs[:, :, h * 32:(h + 1) * 32], i0=s_t1.rearrange("p (b q) -> p b q", q=32), i1=s_t2.rearrange("p (b q) -> p b q", q=32):
              v.tensor_tensor(out=o, in0=i0, in1=i1, op=ALU.add), ["s_t1", "s_t2"], [("qrs", h)])
        s1[h] = (wukT, wk, qs)

    def s1_b(h):
        wukT, wk, qs = s1[h]
        b2 = bank(PS_Q)

        def mmq(pe, b2=b2, wukT=wukT, qs=qs):
            r = None
            for c in range(4):
                r = pe.matmul(psb(b2)[:, c * 128:(c + 1) * 128], wukT[:, c * 128:(c + 1) * 128], qs, start=True, stop=True)
            return r
        S.add("pe", mmq, [wk, ("qn_s", h % 2)], [("ps", b2)])
        S.add("act", lambda a_, o=qabs[:, :, :, h * 32:(h + 1) * 32], i=psb(b2).rearrange("p (c b q) -> p c b q", c=4, b=4): a_.copy(out=o, in_=i),
              [("ps", b2)], [("qabs", h)])

    s1_a(0)
    for h in range(16):
        if h + 1 < 16:
            s1_a(h + 1)
        s1_b(h)
    S.barrier()

    def cache_transposes(u):
        pass

    cache_load(1)
    cache_transposes(0)
    for bi_ in range(4):
        tc = NOWN + 32 * bi_
        b = bank(PS_T)
        for c in range(4):
            S.add("pe", lambda pe, o=psb_bf(b, 1024, 32)[:, c * 128:(c + 1) * 128], i=ckvbf[:, c, tc:tc + 32]: pe.transpose(o, i, ident_b),
                  ["ident_b"], [("ps", b)])
        S.add("dve", lambda v, o=new_tok, i=psb_bf(b, 512, 32): v.tensor_copy(out=o, in_=i), [("ps", b)], ["new_tok"])
        nb = 9
        info = {}
        for it in range(nb + 1):
            if it < nb:
                nk = 128 if it < 8 else 32
                u = 2 * bi_ + it // 4
                ub, lb = u % 2, it % 4
                if it == 2:
                    cache_transposes(2 * bi_ + 1)
                if it == 6 and bi_ < 3:
                    cache_transposes(2 * bi_ + 2)
                bsx = bank(PS_S2)

                def mms(pe, it=it, nk=nk, bsx=bsx, bi_=bi_, tc=tc, ub=ub, lb=lb):
                    for c in range(4):
                        l = cacheT2[ub][:, c, lb * 128:(lb + 1) * 128] if it < 8 else ckvbf[:, c, tc:tc + 32]
                        pe.matmul(psb(bsx, 512, nk), l, qabs[:, c, bi_, :], start=(c == 0), stop=False)
                    l = ckrT2[ub][:, lb * 128:(lb + 1) * 128] if it < 8 else krbf[:, tc:tc + 32]
                    return pe.matmul(psb(bsx, 512, nk), l, qrs[:, bi_, :], start=False, stop=True)
                rkeys = ([("cacheT", ub), ("ckrT", ub)] if it < 8 else []) + [("qabsb", bi_)]
                S.add("pe", mms, rkeys, [("ps", bsx)])
                pi = pt_j[0] % 3
                pt_j[0] += 1
                pt = pts[pi]
                info[it] = (pi, pt, nk, ub, lb)
                S.add("act", lambda a_, o=pt[0:nk, :], i=psb(bsx, 512, nk): a_.activation(out=o, in_=i, func=AF.Exp, scale=SM_SCALE),
                      [("ps", bsx)], [("pt", pi)])
            j = it - 1
            if j >= 0:
                pi, pt, nk, ub, lb = info[j]

                def mmpv(pe, j=j, nk=nk, pt=pt, ub=ub, lb=lb):
                    for c in range(4):
                        l = ctok2[ub][:, lb, c * 128:(c + 1) * 128] if j < 8 else new_tok[:, c * 128:(c + 1) * 128]
                        pe.matmul(psb(PS_PV[c]), l, pt[0:nk, :], start=(j == 0), stop=(j == nb - 1))
                    return pe.matmul(psb(PS_SM), ones1[0:nk, :], pt[0:nk, :], start=(j == 0), stop=(j == nb - 1))
                S.add("pe", mmpv, [("pt", pi), ("ctok", ub), "new_tok", "ones1"], [("ps", PS_PV[c]) for c in range(4)] + [("ps", PS_SM)])
                if j in (3, 7):
                    uu = 2 * bi_ + j // 4
                    if uu + 2 < 8:
                        cache_load(uu + 2)
        S.add("dve", lambda v, o=rs, i=psb(PS_SM): v.reciprocal(out=o, in_=i), [("ps", PS_SM)], ["rs"])
        for c in range(4):
            S.add("dve", lambda v, o=qabs[:, c, bi_, :], i0=psb(PS_PV[c]), i1=rs: v.tensor_tensor(out=o, in0=i0, in1=i1, op=ALU.mult),
                  [("ps", PS_PV[c]), "rs"], [("olat", bi_, c), ("qabsb", bi_)])
    for h in range(16):
        if h % 8 == 0:
            wv_, wk = wl(wuv_s, h // 8, 4096)
            wuv8 = wv_.rearrange("p (a b c) -> p a b c", a=8, b=4, c=128)
        b = bank(PS_S2)

        def mmo(pe, b=b, wuv8=wuv8, h=h):
            r = None
            for c in range(4):
                r = pe.matmul(psb(b, 128), wuv8[:, h % 8, c, :], qabs[:, c, :, h * 32:(h + 1) * 32], start=(c == 0), stop=(c == 3))
            return r
        S.add("pe", mmo, [wk] + [("olat", bb, c) for bb in range(4) for c in range(4)], [("ps", b)])
        S.add("act", lambda a_, o=oT[:, h, SC0:SC0 + 128], i=psb(b, 128): a_.copy(out=o, in_=i), [("ps", b)], [("oT", h, SC0)])
    pf(wgate, 0, 4096)
    pf(wmrg, 0, 3072)
    pf(wgate, 1, 4096)
    S.barrier()

    if DBG:
        dma("sp", dbg_o, oT.rearrange("p a b -> p (a b)"), (), [])
        S.barrier()
    ma = Alloc(E_OFF, RING_OFF)
    mT = ma.get([128, 16, NT], BF16)
    sga = [ma.get([128, 512], F32) for _ in range(2)]
    sgb = [ma.get([128, 512], F32) for _ in range(2)]
    mt1 = [ma.get([128, 512], F32) for _ in range(2)]
    mi = [0]
    for dc in range(16):
        wv, wgk = wl(wgate, dc, 4096)
        wg4 = wv.rearrange("p (a b c) -> p a b c", a=2, b=16, c=128)
        wv, wmk = wl(wmrg, dc, 3072)
        wo3 = wv[:, 0:2048].rearrange("p (b c) -> p b c", c=128)
        wpo3 = wv[:, 2048:3072].rearrange("p (b c) -> p b c", c=128)
        for (c0, n) in LIN_TILES:
            i2 = mi[0] % 2
            mi[0] += 1
            bga = bank(PS_A)
            S.add("pe", proj(wg4[:, 0], 16, xn, None, c0, n, bga), [wgk], [("ps", bga)])
            bgb = bank(PS_A)
            S.add("pe", proj(wg4[:, 1], 16, xn, None, c0, n, bgb), [wgk], [("ps", bgb)])
            ba = bank(PS_B)
            S.add("pe", proj(wpo3, 8, mixedT, None, c0, n, ba), [wmk], [("ps", ba)])
            bb = bank(PS_C)
            S.add("pe", proj(wo3, 16, oT, None, c0, n, bb), [wmk], [("ps", bb)])
            S.add("act", lambda a, o=sga[i2][:, 0:n], i=psb(bga, n): a.activation(out=o, in_=i, func=AF.Sigmoid), [("ps", bga)], [("sga", i2)])
            S.add("act", lambda a, o=sgb[i2][:, 0:n], i=psb(bgb, n): a.activation(out=o, in_=i, func=AF.Sigmoid), [("ps", bgb)], [("sgb", i2)])
            S.add("dve", lambda v, o=mt1[i2][:, 0:n], i0=sga[i2][:, 0:n], i1=psb(ba, n): v.tensor_tensor(out=o, in0=i0, in1=i1, op=ALU.mult),
                  [("sga", i2), ("ps", ba)], [("mt1", i2)])
            S.add("dve", lambda v, o=sgb[i2][:, 0:n], i0=sgb[i2][:, 0:n], i1=psb(bb, n): v.tensor_tensor(out=o, in0=i0, in1=i1, op=ALU.mult),
                  [("sgb", i2), ("ps", bb)], [("sgb", i2)])
            S.add("dve", lambda v, o=mT[:, dc, c0:c0 + n], i0=mt1[i2][:, 0:n], i1=sgb[i2][:, 0:n]: v.tensor_tensor(out=o, in0=i0, in1=i1, op=ALU.add),
                  [("mt1", i2), ("sgb", i2)], [("mT", dc, c0)])
    pf(wout, 0, 4096)
    pf(wout, 1, 4096)
    S.barrier()
    if DBG:
        dma("sp", dbg_m, mT.rearrange("p a b -> p (a b)"), (), [])
        S.barrier()
    for kc in range(16):
        dma("sp", xT[:, kc, :], hs[:, kc * NT:(kc + 1) * NT], (), xkeys("x", kc, 0, NT))
    for up in range(8):
        wv, wk = wl(wout, up, 4096)
        w4o = wv.rearrange("p (a b c) -> p a b c", a=2, b=16, c=128)
        for d2 in range(2):
            dc = up * 2 + d2
            for (c0, n) in LIN_TILES:
                b = bank(PS_A)
                S.add("pe", proj(w4o[:, d2], 16, mT, None, c0, n, b), [wk], [("ps", b)])
                S.add("dve", lambda v, o=xT[:, dc, c0:c0 + n], i0=psb(b, n): v.tensor_tensor(out=o, in0=i0, in1=o, op=ALU.add),
                      [("ps", b)] + xkeys("x", dc, c0, n), xkeys("x", dc, c0, n))
    S.barrier()

    if DBG:
        dma("sp", dbg_x1, xT.rearrange("p a b -> p (a b)"), (), [])
        S.barrier()
    eal2 = Alloc(E_OFF, RING_OFF)
    ea2 = ffn_alloc(eal2, NT)
    ystage = [eal2.get([128, D], F32) for _ in range(2)]
    norm_to_bf(xT, "x", LIN_TILES, G_FFN2, xn, "xn", ea2)
    ffn(xT, "x", xn, "xn", LIN_TILES, wgu2, wd2, ea2)
    S.barrier()

    fa = Alloc(E_OFF, RING_OFF)
    f_ea = {"sq0": fa.get([128, 512], BF16), "sq1": fa.get([128, 512], BF16),
            "rstd": fa.get([128, 512], F32), "sqt": fa.get([128, 512], F32)}
    yT = [fa.get([128, 16, 384], F32) for _ in range(2)]
    y_own3 = y_own.rearrange("(k p) t -> p k t", p=128)
    y_smp3 = y_smp.rearrange("(k p) t -> p k t", p=128)
    FIN_TILES = [(0, 384), (384, 384), (768, 256), (1024, 128)]
    for ti, (c0, n) in enumerate(FIN_TILES):
        rstd = f_ea["rstd"][:, 0:n]
        rms_stats(lambda kc: (xT[:, kc, c0:c0 + n], []), 16, onesD, n, rstd, ("rstd", "fin"), [f_ea["sq0"], f_ea["sq1"]], f_ea["sqt"], "fin")
        yt = yT[ti % 2]
        for kc in range(16):
            S.add("dve", lambda v, o=yt[:, kc, 0:n], i=xT[:, kc, c0:c0 + n], s=gains[:, G_FIN + kc:G_FIN + kc + 1], r=rstd:
                  v.scalar_tensor_tensor(out=o, in0=i, scalar=s, in1=r, op0=ALU.mult, op1=ALU.mult),
                  [("rstd", "fin"), "gains"], [("yT", ti % 2, kc // 4)])
        for g in range(4):
            n_own = max(0, min(c0 + n, NOWN) - c0)
            lst = []
            if n_own > 0:
                lst.append((y_own3[:, g * 4:(g + 1) * 4, c0:c0 + n_own], yt[:, g * 4:(g + 1) * 4, 0:n_own]))
            if n_own < n:
                lst.append((y_smp3[:, g * 4:(g + 1) * 4, c0 + n_own - NOWN:c0 + n - NOWN], yt[:, g * 4:(g + 1) * 4, n_own:n]))
            S.add("sp", lambda e, lst=lst: [e.dma_start(out=o, in_=i) for (o, i) in lst], [("yT", ti % 2, g)], [], dma=len(lst))

    assert not prefetched, list(prefetched)
    S.emit_all(nc, stack)
    stack.close()
    return nc


_CACHE = {}


def kernel(x_prompt, x_sample, cache_ckv, cache_krope, state_pool,
           g_ffn1, w1_gate, w1_up, w1_down, g_mix, w_in, g_q_lat, g_kv_lat,
           w_uq, w_uk, w_uv, w_o_attn, w_pool, pool_scale, w_pool_out, w_out,
           g_ffn2, w2_gate, w2_up, w2_down, g_final):
    f = np.float32
    A = lambda a: np.ascontiguousarray(np.asarray(a, dtype=f))
    x_prompt, x_sample = A(x_prompt), A(x_sample)
    cache_ckv, cache_krope, state_pool = A(cache_ckv)[0], A(cache_krope)[0], A(state_pool)[0]
    wgu1, wd1 = prep_ffn(A(w1_gate)[0], A(w1_up)[0], A(w1_down)[0])
    wgu2, wd2 = prep_ffn(A(w2_gate)[0], A(w2_up)[0], A(w2_down)[0])
    win = A(w_in)[0]
    z_c, ql_c, kv_c, kr_c = win[:, 0:1024], win[:, 1024:1536], win[:, 1536:2048], win[:, 2048:2112]
    gA_c, gB_c = win[:, 2112:2112 + 2048], win[:, 2112 + 2048:2112 + 4096]
    kr2 = np.concatenate([kr_c, swap_half(kr_c)], axis=1)
    def pair_units(pc):
        n_ = pc.shape[0]
        return np.ascontiguousarray(pc.reshape(n_ // 2, 2, 128, 2048).transpose(0, 2, 1, 3)).reshape(n_ // 2, 128, 4096)
    wkvq = pair_units(prep_cols(np.concatenate([kv_c, ql_c], axis=1)))
    wkr_h = prep_cols(kr2)
    wz_h = pair_units(prep_cols(z_c))
    wgate = np.ascontiguousarray(np.stack([prep_cols(gA_c), prep_cols(gB_c)], axis=2)).reshape(16, 128, 4096)
    wp = A(w_pool)[0]
    wpool = np.ascontiguousarray(wp.reshape(4, 2, 128, 256).transpose(2, 0, 1, 3)).reshape(1, 128, 2048)
    uq = A(w_uq)[0]
    uk = A(w_uk)[0]
    uv = A(w_uv)[0]
    uqn = uq[:, :, 0:128]
    uqr = uq[:, :, 128:192]
    uqr2 = np.concatenate([uqr, swap_half(uqr)], axis=2)

    def per_head(Wh):
        return np.ascontiguousarray(Wh.reshape(4, 128, 16, 128).transpose(2, 1, 0, 3)).reshape(16, 128, 512)
    watt4 = np.ascontiguousarray(np.stack([per_head(uqn), per_head(uqr2), per_head(uk), per_head(uv)], axis=2)).reshape(16, 128, 2048)
    ukT = np.ascontiguousarray(uk.transpose(1, 2, 0))
    watt = watt4
    watt_s = np.ascontiguousarray(np.concatenate([per_head(uqn), per_head(uqr2), ukT], axis=2))
    wuv_s = np.ascontiguousarray(per_head(uv).reshape(2, 8, 128, 512).transpose(0, 2, 1, 3)).reshape(2, 128, 4096)
    wo_c = prep_cols(A(w_o_attn)[0])
    wpo_c = prep_cols(A(w_pool_out)[0])
    wmrg = np.ascontiguousarray(np.concatenate([wo_c, wpo_c], axis=2))
    wout_c = prep_cols(A(w_out)[0])
    wout = np.ascontiguousarray(wout_c.reshape(8, 2, 128, 2048).transpose(0, 2, 1, 3)).reshape(8, 128, 4096)

    def gcol(g, n):
        return np.asarray(g, dtype=f).reshape(n, 128).T
    gains = np.ascontiguousarray(np.concatenate([
        gcol(A(g_ffn1)[0], 16), gcol(A(g_mix)[0], 16), gcol(A(g_ffn2)[0], 16), gcol(A(g_final), 16),
        gcol(A(g_q_lat)[0], 4), gcol(A(g_kv_lat)[0], 4), gcol(A(pool_scale)[0], 8)], axis=1))
    ident = np.eye(128, dtype=f)
    cs_prev = rope_tables(np.arange(1024))
    pos_s = 1024 + (np.arange(128) % 32)

    shared = dict(wgu1=wgu1, wd1=wd1, wgu2=wgu2, wd2=wd2, wkvq=wkvq, wkr=wkr_h, wz=wz_h, wgate=wgate, wpool=wpool, watt=watt, watt_s=watt_s, wuv_s=wuv_s,
                  wmrg=wmrg, wout=wout, c_ident=ident, c_gains=gains, c_cs_prev=cs_prev)
    in_maps = []
    zeros_prev = np.zeros((D, NPREV), dtype=f)
    for c in range(8):
        b, half = c // 2, c % 2
        pos_o = half * 1024 + np.arange(1024)
        cs_main = np.ascontiguousarray(np.concatenate([rope_tables(pos_o), rope_tables(pos_s)], axis=2))
        rc = np.zeros((4, 16), dtype=f)
        for g, w in enumerate((2, 4, 8, 16)):
            rc[g] = 1.0 / np.minimum(pos_o[:16] + 1, w)
        m = dict(shared)
        m.update(
            x_own=np.ascontiguousarray(x_prompt[b, half * 1024:(half + 1) * 1024].T),
            x_prev=np.ascontiguousarray(x_prompt[b, 0:1024].T) if half == 1 else zeros_prev,
            x_smp=np.ascontiguousarray(x_sample[4 * c:4 * c + 4].reshape(128, D).T),
            cache_ckv=np.ascontiguousarray(cache_ckv[4 * c:4 * c + 4]),
            cache_ckvT=np.ascontiguousarray(cache_ckv[4 * c:4 * c + 4].transpose(0, 2, 1)),
            cache_krT=np.ascontiguousarray(cache_krope[4 * c:4 * c + 4].transpose(0, 2, 1)),
            state_pool=np.ascontiguousarray(state_pool[4 * c:4 * c + 4].reshape(60, 1024).T),
            c_cs_main=cs_main,
            c_rc16=np.ascontiguousarray(np.broadcast_to(rc.reshape(1, 64), (128, 64))).astype(f),
            c_prevb=np.full((128, 1), 0.0 if half == 1 else -1e30, dtype=f),
        )
        in_maps.append(m)

    if "nc" not in _CACHE:
        _CACHE["nc"] = build_program()
    nc = _CACHE["nc"]
    res = run_bass_kernel_spmd(nc, in_maps, core_ids=list(range(8)))
    R = res.results
    _CACHE["last"] = R

    y_prompt = np.zeros((4, 2048, D), f)
    y_sample = np.zeros((32, 32, D), f)
    ckv_p = np.zeros((1, 4, 2048, 512), f)
    kr_p = np.zeros((1, 4, 2048, 64), f)
    pool_p = np.zeros((1, 4, 15, 1024), f)
    ckv_s = np.zeros((1, 32, 32, 512), f)
    kr_s = np.zeros((1, 32, 32, 64), f)
    pool_s = np.zeros((1, 32, 15, 1024), f)
    for c in range(8):
        b, half = c // 2, c % 2
        r = R[c]
        sl = slice(half * 1024, (half + 1) * 1024)
        y_prompt[b, sl] = np.asarray(r["y_own"]).T
        ckvT = np.asarray(r["ckvT_out"])
        krT = np.asarray(r["krT_out"])
        ckv_p[0, b, sl] = ckvT[:, :NOWN].T
        kr_p[0, b, sl] = krT[:, :NOWN].T
        if half == 1:
            pool_p[0, b] = np.asarray(r["poolT_own"]).T
        y_sample[4 * c:4 * c + 4] = np.asarray(r["y_smp"]).T.reshape(4, 32, D)
        ckv_s[0, 4 * c:4 * c + 4] = ckvT[:, NOWN:].T.reshape(4, 32, 512)
        kr_s[0, 4 * c:4 * c + 4] = krT[:, NOWN:].T.reshape(4, 32, 64)
        pool_s[0, 4 * c:4 * c + 4] = np.asarray(r["poolT_smp"]).reshape(1024, 4, 15).transpose(1, 2, 0)
    return (y_prompt, y_sample, ckv_p, kr_p, pool_p, ckv_s, kr_s, pool_s)
```

```python
import numpy as np
from contextlib import ExitStack
import concourse.bass as bass
import concourse.mybir as mybir
from concourse.bass_utils import run_bass_kernel_spmd

F32 = mybir.dt.float32
BF16 = mybir.dt.bfloat16
AF = mybir.ActivationFunctionType
ALU = mybir.AluOpType

D = 2048
DFF = 5632
NOWN = 1024
NSMP = 128
NT = NOWN + NSMP
NPREV = 1024
EPS = 1e-6
SM_SCALE = 192 ** -0.5
ARENA = 211968
RING_SLOTS = 4
SLOT_B = 8192
RING_OFF = ARENA - RING_SLOTS * SLOT_B

ENGS = ["pe", "act", "dve", "pool", "sp"]


class Task:
    __slots__ = ("eng", "emit", "deps", "is_dma", "sem", "val", "prev_val", "has_dep", "ndma")

    def __init__(self, eng, emit, is_dma, ndma):
        self.eng = eng
        self.emit = emit
        self.deps = set()
        self.is_dma = is_dma
        self.ndma = ndma
        self.sem = None
        self.val = 0
        self.prev_val = 0
        self.has_dep = False


class RK:
    __slots__ = ("slot", "u", "state")

    def __init__(self, slot, u, state):
        self.slot, self.u, self.state = slot, u, state

    def __hash__(self):
        return hash(("ring", self.slot))

    def __eq__(self, o):
        return isinstance(o, RK) and o.slot == self.slot

    def check(self):
        assert self.state["u"] - self.u < RING_SLOTS, "stale ring slot use"


class Sched:
    def __init__(self):
        self.tasks = {e: [] for e in ENGS}
        self.lastw = {}
        self.readers = {}
        self.pending = {}
        self.dma_since = []

    def add(self, eng, emit, reads=(), writes=(), dma=0):
        t = Task(eng, emit, dma > 0, dma)
        deps = set()
        for r in reads:
            if isinstance(r, RK):
                r.check()
            w = self.lastw.get(r)
            if w is not None:
                deps.add(w)
        for k in writes:
            w = self.lastw.get(k)
            if w is not None:
                deps.add(w)
            rs = self.readers.get(k)
            if rs:
                deps.update(rs)
        for r in reads:
            self.readers.setdefault(r, []).append(t)
        for k in writes:
            self.lastw[k] = t
            self.readers[k] = []
        if eng in self.pending:
            deps |= self.pending.pop(eng)
        deps.discard(t)
        t.deps = deps
        for d in deps:
            d.has_dep = True
        self.tasks[eng].append(t)
        if t.is_dma:
            self.dma_since.append(t)
        return t

    def barrier(self):
        s = set(self.dma_since)
        for e in ENGS:
            if self.tasks[e]:
                s.add(self.tasks[e][-1])
        for t in s:
            t.has_dep = True
        for e in ENGS:
            self.pending[e] = self.pending.get(e, set()) | s
        self.dma_since = []
        self.lastw.clear()
        self.readers.clear()

    def emit_all(self, nc, stack):
        KD = 8
        csem = {}
        for e in ["pe", "act", "dve"]:
            n = sum(1 for t in self.tasks[e] if t.has_dep)
            ngen = n // 30000 + 1
            csem[e] = [stack.enter_context(nc.semaphore(f"c_{e}_{g}")) for g in range(ngen)]
            cnt = 0
            for t in self.tasks[e]:
                if t.has_dep:
                    t.sem = csem[e][cnt // 30000]
                    t.val = cnt % 30000 + 1
                    cnt += 1
        all_dma = []
        for e in ["pool", "sp"]:
            pool = [stack.enter_context(nc.semaphore(f"d_{e}_{k}")) for k in range(KD)]
            vals = [0] * KD
            for i, t in enumerate(self.tasks[e]):
                assert t.is_dma
                k = i % KD
                t.sem = pool[k]
                t.prev_val = vals[k]
                vals[k] += 16 * t.ndma
                t.val = vals[k]
            all_dma += [(pool[k], vals[k]) for k in range(KD) if vals[k] > 0]

        block = stack.enter_context(nc.Block())

        def run(ename, eng):
            waited = {}

            def wait(sem, val):
                key = id(sem)
                if waited.get(key, 0) < val:
                    eng.wait_ge(sem, val)
                    waited[key] = val

            for t in self.tasks[ename]:
                for d in t.deps:
                    if ename == "pe" and d.eng == "pe":
                        continue
                    wait(d.sem, d.val)
                if t.is_dma and t.prev_val > 0:
                    wait(t.sem, t.prev_val)
                r = t.emit(eng)
                if t.is_dma:
                    assert len(r) == t.ndma, (len(r), t.ndma)
                    for ins in r:
                        ins.then_inc(t.sem, 16)
                elif t.has_dep:
                    r.then_inc(t.sem, 1)
            if ename == "sp":
                for sem, val in all_dma:
                    wait(sem, val)

        @block.tensor
        def _(pe):
            run("pe", pe)

        @block.scalar
        def _(a):
            run("act", a)

        @block.vector
        def _(v):
            run("dve", v)

        @block.gpsimd
        def _(g):
            run("pool", g)

        @block.sync
        def _(sp):
            run("sp", sp)


def prep_cols(W):
    K, N = W.shape
    kc, ncn = K // 128, N // 128
    return np.ascontiguousarray(W.reshape(kc, 128, ncn, 128).transpose(2, 1, 0, 3)).reshape(ncn, 128, kc * 128)


def prep_ffn(wg, wu, wd):
    g = prep_cols(wg)
    u = prep_cols(wu)
    gu = np.ascontiguousarray(np.stack([g, u], axis=2)).reshape(44, 128, 4096)
    wds = []
    for q in range(4):
        c = prep_cols(wd[q * 1408:(q + 1) * 1408, :])
        c = c.reshape(8, 2, 128, 1408).transpose(0, 2, 1, 3).reshape(8, 128, 2816)
        wds.append(c)
    return gu, np.ascontiguousarray(np.concatenate(wds, axis=0))


def swap_half(W):
    return np.concatenate([W[..., 32:64], W[..., 0:32]], axis=-1)


def rope_tables(pos):
    half = 32
    inv = 10000.0 ** (-np.arange(half, dtype=np.float64) * 2.0 / 64)
    ang = pos.astype(np.float64)[:, None] * inv[None, :]
    c, s = np.cos(ang).astype(np.float32), np.sin(ang).astype(np.float32)
    C = np.concatenate([c, c], axis=1).T
    S = np.concatenate([-s, s], axis=1).T
    return np.ascontiguousarray(np.stack([C, S], axis=0)).astype(np.float32)


def build_program():
    nc = bass.Bass("TRN2", target_bir_lowering=False)
    S = Sched()

    def din(name, shape, dt=F32):
        return nc.dram_tensor(name, list(shape), dt, kind="ExternalInput").ap()

    def dout(name, shape, dt=F32):
        return nc.dram_tensor(name, list(shape), dt, kind="ExternalOutput").ap()

    x_own = din("x_own", [D, NOWN])
    x_prev = din("x_prev", [D, NPREV])
    x_smp = din("x_smp", [D, NSMP])
    cache_ckv = din("cache_ckv", [4, 1024, 512])
    cache_ckvT = din("cache_ckvT", [4, 512, 1024])
    cache_krT = din("cache_krT", [4, 64, 1024])
    state_pool = din("state_pool", [1024, 60])
    wgu1 = din("wgu1", [44, 128, 4096])
    wd1 = din("wd1", [32, 128, 2816])
    wgu2 = din("wgu2", [44, 128, 4096])
    wd2 = din("wd2", [32, 128, 2816])
    wkvq = din("wkvq", [4, 128, 4096])
    wkr_d = din("wkr", [1, 128, 2048])
    wz_d = din("wz", [4, 128, 4096])
    wgate = din("wgate", [16, 128, 4096])
    wpool = din("wpool", [1, 128, 2048])
    watt = din("watt", [16, 128, 2048])
    watt_s = din("watt_s", [16, 128, 1536])
    wuv_s = din("wuv_s", [2, 128, 4096])
    wmrg = din("wmrg", [16, 128, 3072])
    wout = din("wout", [8, 128, 4096])
    c_ident = din("c_ident", [128, 128])
    c_gains = din("c_gains", [128, 80])
    c_cs_main = din("c_cs_main", [2, 64, NT])
    c_cs_prev = din("c_cs_prev", [2, 64, NPREV])
    c_rc16 = din("c_rc16", [128, 64])
    c_prevb = din("c_prevb", [128, 1])

    y_own = dout("y_own", [D, NOWN])
    y_smp = dout("y_smp", [D, NSMP])
    ckvT_out = dout("ckvT_out", [512, NT])
    krT_out = dout("krT_out", [64, NT])
    poolT_own = dout("poolT_own", [1024, 15])
    poolT_smp = dout("poolT_smp", [1024, 60])

    hs = nc.dram_tensor("hs_scratch", [128, 16 * NT], F32).ap()
    import os
    DBG = os.environ.get("KDBG", "0") == "1"
    if DBG:
        dbg_mixed = dout("dbg_mixed", [128, 8 * NT], BF16)
        dbg_o = dout("dbg_o", [128, 16 * NT], BF16)
        dbg_m = dout("dbg_m", [128, 16 * NT], BF16)
        dbg_x1 = dout("dbg_x1", [128, 16 * NT], F32)
        dbg_pooled = dout("dbg_pooled", [128, 8 * NT], BF16)

    stack = ExitStack()
    arena_t = stack.enter_context(nc.sbuf_tensor("arena", [128, ARENA // 4], F32))
    psum_t = stack.enter_context(nc.psum_tensor("psum", [128, 4096], F32))

    def carve(off, shape, dt):
        esz = 4 if dt == F32 else 2
        P = shape[0]
        n = 1
        for s_ in shape[1:]:
            n *= s_
        nb = n * esz
        assert off % 4 == 0 and nb % 4 == 0 and off + nb <= ARENA, (off, shape)
        ap = arena_t[0:P, off // 4:(off + nb) // 4]
        if dt != F32:
            ap = ap.bitcast(dt)
        if len(shape) == 3:
            ap = ap.rearrange("p (a b) -> p a b", b=shape[2])
        elif len(shape) == 4:
            ap = ap.rearrange("p (a b c) -> p a b c", b=shape[2], c=shape[3])
        return ap

    class Alloc:
        def __init__(self, lo, hi):
            self.lo, self.hi, self.cur = lo, hi, lo

        def get(self, shape, dt):
            esz = 4 if dt == F32 else 2
            n = 1
            for s_ in shape[1:]:
                n *= s_
            nb = (n * esz + 31) // 32 * 32
            off = self.cur
            self.cur += nb
            assert self.cur <= self.hi, ("alloc overflow", shape, self.cur, self.hi)
            return carve(off, shape, dt)

    def psb(b, n=512, parts=128):
        return psum_t[0:parts, b * 512:b * 512 + n]

    def psb_bf(b, n, parts=128):
        return psum_t[0:parts, b * 512:b * 512 + 512].bitcast(BF16)[:, 0:n]

    ca = Alloc(0, 2176)
    ident_f = ca.get([128, 128], F32)
    ident_b = ca.get([128, 128], BF16)
    onesD = ca.get([128, 128], BF16)
    ones512 = ca.get([128, 128], BF16)
    ones1 = ca.get([128, 128], BF16)
    gains = ca.get([128, 80], F32)
    prevb = ca.get([128, 1], F32)
    epsc = ca.get([128, 1], F32)
    rc16 = ca.get([128, 4, 16], F32)
    G_FFN1, G_MIX, G_FFN2, G_FIN, G_Q, G_KV, G_PS = 0, 16, 32, 48, 64, 68, 72

    pa = Alloc(2176, 12928)
    ckvp_bf = pa.get([128, 4, 1024], BF16)
    krp_full = pa.get([128, 1024], BF16)
    krp_bf = krp_full[0:64, :]
    zhist = pa.get([128, 8, 16], F32)
    XN_OFF = 12928
    R0_OFF = XN_OFF + 36864
    E_OFF = R0_OFF + 73728
    assert E_OFF == 123520

    def dma(eng, out, in_, reads=(), writes=()):
        return S.add(eng, lambda e, o=out, i=in_: [e.dma_start(out=o, in_=i)], reads, writes, dma=1)

    dma("sp", ident_f, c_ident, (), ["ident_f"])
    dma("sp", gains, c_gains, (), ["gains"])
    dma("sp", prevb, c_prevb, (), ["prevb"])
    dma("sp", rc16.rearrange("p a b -> p (a b)"), c_rc16, (), ["rc16"])
    S.add("dve", lambda v: v.tensor_copy(out=ident_b, in_=ident_f), ["ident_f"], ["ident_b"])
    S.add("dve", lambda v: v.memset(onesD, 1.0 / 2048), (), ["onesD"])
    S.add("dve", lambda v: v.memset(ones512, 1.0 / 512), (), ["ones512"])
    S.add("dve", lambda v: v.memset(ones1, 1.0), (), ["ones1"])
    S.add("dve", lambda v: v.memset(epsc, EPS), (), ["epsc"])

    ring_state = {"u": 0}

    def wload(src_ap, ncols):
        u = ring_state["u"]
        ring_state["u"] += 1
        slot = u % RING_SLOTS
        assert ncols * 2 <= SLOT_B
        view = carve(RING_OFF + slot * SLOT_B, [128, ncols], BF16)
        key = RK(slot, u + 1, ring_state)
        dma("pool", view, src_ap, (), [key])
        return view, key

    prefetched = {}

    def wl(src, idx, ncols):
        k = (src.tensor.name, idx)
        if k in prefetched:
            return prefetched.pop(k)
        return wload(src[idx], ncols)

    def pf(src, idx, ncols):
        prefetched[(src.tensor.name, idx)] = wload(src[idx], ncols)

    psrot = {"i": 0}

    def bank(group):
        i = psrot.setdefault(id(group), 0)
        psrot[id(group)] = i + 1
        return group[i % len(group)]

    PS_A = [0, 1, 2, 3]
    PS_B = [4, 5]
    PS_C = [6, 7]

    def rms_stats(src_fn, nch, ones_ap, n, rstd_out, keyr, sq_bufs, sqt_buf, tag):
        b = bank(PS_C)
        for kc in range(nch):
            ap, keys = src_fn(kc)
            sq = sq_bufs[kc % 2]
            S.add("act", lambda a, o=sq[:, 0:n], i=ap: a.activation(out=o, in_=i, func=AF.Square),
                  list(keys), [("sq", tag, kc % 2)])
            S.add("pe", lambda pe, o=psb(b, n), r=sq[:, 0:n], st=(kc == 0), sp_=(kc == nch - 1):
                  pe.matmul(o, ones_ap, r, start=st, stop=sp_),
                  [("sq", tag, kc % 2), "onesD", "ones512"], [("ps", b)])
        S.add("act", lambda a, o=sqt_buf[:, 0:n], i=psb(b, n): a.activation(out=o, in_=i, func=AF.Sqrt, bias=epsc[:, 0:1], scale=1.0),
              [("ps", b), "epsc"], [("sqt", tag)])
        S.add("dve", lambda v, o=rstd_out, i=sqt_buf[:, 0:n]: v.reciprocal(out=o, in_=i),
              [("sqt", tag)], [keyr])

    def load_xT(x_dram, ntok, xT, col0, stage_bufs, tag):
        src3 = x_dram.rearrange("(k p) t -> p k t", p=128)
        for c0 in range(0, ntok, 512):
            n = min(512, ntok - c0)
            S.add("sp", lambda e, c0=c0, n=n: [e.dma_start(out=xT[:, g * 4:(g + 1) * 4, col0 + c0:col0 + c0 + n], in_=src3[:, g * 4:(g + 1) * 4, c0:c0 + n]) for g in range(4)],
                  (), [(tag, kc, c) for kc in range(16) for c in range(col0 + c0, col0 + c0 + n, 128)], dma=4)

    def xkeys(tag, kc, c0, n):
        return [(tag, kc, c) for c in range(c0 - c0 % 128, c0 + n, 128)]

    def norm_to_bf(xT, xtag, tiles, gcol, xn, xntag, ea):
        sq_bufs = [ea["sq0"], ea["sq1"]]
        for (c0, n) in tiles:
            rstd = ea["rstd"][:, 0:n]
            rms_stats(lambda kc: (xT[:, kc, c0:c0 + n], xkeys(xtag, kc, c0, n)), 16, onesD, n, rstd, ("rstd", xntag), sq_bufs, ea["sqt"], xntag)
            for kc in range(16):
                S.add("dve", lambda v, o=xn[:, kc, c0:c0 + n], i=xT[:, kc, c0:c0 + n], s=gains[:, gcol + kc:gcol + kc + 1], r=rstd:
                      v.scalar_tensor_tensor(out=o, in0=i, scalar=s, in1=r, op0=ALU.mult, op1=ALU.mult),
                      [("rstd", xntag), "gains"] + xkeys(xtag, kc, c0, n), xkeys(xntag, kc, c0, n))

    def ffn(xT, xtag, xn, xntag, tiles, wgu, wd, ea):
        hT = ea["hT"]
        for q in range(4):
            for fl in range(11):
                fc = q * 11 + fl
                wv, wk = wl(wgu, fc, 4096)
                wv4 = wv.rearrange("p (a b c) -> p a b c", a=2, b=16, c=128)
                for (c0, n) in tiles:
                    bg = bank(PS_A)
                    bu = bank(PS_A)
                    for which, b in ((0, bg), (1, bu)):
                        def mm(pe, which=which, b=b, c0=c0, n=n, wv4=wv4):
                            r = None
                            for kc in range(16):
                                r = pe.matmul(psb(b, n), wv4[:, which, kc, :], xn[:, kc, c0:c0 + n], start=(kc == 0), stop=(kc == 15))
                            return r
                        S.add("pe", mm, [wk] + [k for kc in range(16) for k in xkeys(xntag, kc, c0, n)], [("ps", b)])
                    sg = ea["sg"][bg % 2 if False else (psrot.setdefault("sg", 0) % 2)]
                    sgi = psrot["sg"] % 2
                    psrot["sg"] += 1
                    S.add("act", lambda a, o=sg[:, 0:n], i=psb(bg, n): a.activation(out=o, in_=i, func=AF.Silu),
                          [("ps", bg)], [("sg", sgi)])
                    S.add("dve", lambda v, o=hT[:, fl, c0:c0 + n], i0=sg[:, 0:n], i1=psb(bu, n): v.tensor_tensor(out=o, in0=i0, in1=i1, op=ALU.mult),
                          [("sg", sgi), ("ps", bu)], [("hT", fl, c0)])
            for dcp in range(8):
                wv, wk = wl(wd, q * 8 + dcp, 2816)
                wv4 = wv.rearrange("p (a b c) -> p a b c", a=2, b=11, c=128)
                for d2 in range(2):
                    dc = dcp * 2 + d2
                    for (c0, n) in tiles:
                        b = bank(PS_B)

                        def mm(pe, d2=d2, b=b, c0=c0, n=n, wv4=wv4):
                            r = None
                            for fl in range(11):
                                r = pe.matmul(psb(b, n), wv4[:, d2, fl, :], hT[:, fl, c0:c0 + n], start=(fl == 0), stop=(fl == 10))
                            return r
                        S.add("pe", mm, [wk] + [("hT", fl, c0) for fl in range(11)], [("ps", b)])
                        S.add("dve", lambda v, o=xT[:, dc, c0:c0 + n], i0=psb(b, n): v.scalar_tensor_tensor(out=o, in0=i0, scalar=0.5, in1=o, op0=ALU.mult, op1=ALU.add),
                              [("ps", b)] + xkeys(xtag, dc, c0, n), xkeys(xtag, dc, c0, n))

    def proj(wview, KC, xin, xintag, c0, n, b, M=128, mcol0=0, kc_keys=None):
        def mm(pe):
            r = None
            for kc in range(KC):
                r = pe.matmul(psb(b, n, M), wview[:, kc, mcol0:mcol0 + M], xin[:, kc, c0:c0 + n], start=(kc == 0), stop=(kc == KC - 1))
            return r
        return mm

    def feat_norm512(raw, rawtag, n, gcol, ea, outs, tagn):
        rstd = ea["rstd"][:, 0:n]
        rk_ = (lambda kc: (rawtag + (kc,)) if isinstance(rawtag, tuple) else (rawtag, kc))
        rms_stats(lambda kc: (raw[:, kc, 0:n], [rk_(kc)]), 4, ones512, n, rstd, ("rstd", tagn), [ea["sq0"], ea["sq1"]], ea["sqt"], tagn)
        for kc in range(4):
            for (o_ap, okey) in outs:
                S.add("dve", lambda v, o=o_ap(kc), i=raw[:, kc, 0:n], s=gains[:, gcol + kc:gcol + kc + 1], r=rstd:
                      v.scalar_tensor_tensor(out=o, in0=i, scalar=s, in1=r, op0=ALU.mult, op1=ALU.mult),
                      [("rstd", tagn), rk_(kc), "gains"], okey(kc))

    def dup_rows(dupI, full, c0, n, key):
        b = bank(PS_A)
        S.add("pe", lambda pe, o=psb(b, n), l=dupI, r=full[0:64, c0:c0 + n]: pe.matmul(o, l, r, start=True, stop=True),
              [key, "dupI"], [("ps", b)])
        S.add("act", lambda a, o=full[64:128, c0:c0 + n], i=psum_t[64:128, b * 512:b * 512 + n]: a.copy(out=o, in_=i),
              [("ps", b)], [(key, "dup")])

    def make_dupI(dupI):
        S.add("dve", lambda v, o=dupI[:, 0:64], i=ident_b[0:64, 0:64]: v.tensor_copy(out=o, in_=i), ["ident_b"], ["dupI0"])
        S.add("dve", lambda v, o=dupI[:, 64:128], i=ident_b[0:64, 0:64]: v.tensor_copy(out=o, in_=i), ["ident_b", "dupI0"], ["dupI"])

    def rope_from_ps(b0, b1, n, cs, ccol0, t1, t2, outs):
        S.add("dve", lambda v, o=t1[:, 0:n], i0=psb(b0, n, 64), i1=cs[:, 0, ccol0:ccol0 + n]: v.tensor_tensor(out=o, in0=i0, in1=i1, op=ALU.mult),
              [("ps", b0), "cs"], ["rt1"])
        S.add("dve", lambda v, o=t2[:, 0:n], i0=psb(b1, n, 64), i1=cs[:, 1, ccol0:ccol0 + n]: v.tensor_tensor(out=o, in0=i0, in1=i1, op=ALU.mult),
              [("ps", b1), "cs"], ["rt2"])
        for (o_ap, okeys) in outs:
            S.add("dve", lambda v, o=o_ap, i0=t1[:, 0:n], i1=t2[:, 0:n]: v.tensor_tensor(out=o, in0=i0, in1=i1, op=ALU.add),
                  ["rt1", "rt2"], okeys)

    def ffn_alloc(ea_alloc, ntok):
        ea = {}
        ea["hT"] = ea_alloc.get([128, 11, ntok], BF16)
        ea["sg"] = [ea_alloc.get([128, 512], F32) for _ in range(2)]
        ea["sq0"] = ea_alloc.get([128, 512], BF16)
        ea["sq1"] = ea_alloc.get([128, 512], BF16)
        ea["rstd"] = ea_alloc.get([128, 512], F32)
        ea["sqt"] = ea_alloc.get([128, 512], F32)
        return ea

    MAIN_TILES = [(0, 512), (512, 512), (1024, 128)]
    LIN_TILES = [(0, 384), (384, 384), (768, 384)]
    PREV_TILES = [(0, 512), (512, 512)]

    xn_prev = carve(XN_OFF, [128, 16, NPREV], BF16)
    xT_prev = carve(R0_OFF, [128, 16, NPREV], F32)
    eal = Alloc(E_OFF, RING_OFF)
    ea = ffn_alloc(eal, NT)
    xstage = [eal.get([128, D], F32) for _ in range(2)]

    load_xT(x_prev, NPREV, xT_prev, 0, xstage, "xp")
    norm_to_bf(xT_prev, "xp", PREV_TILES, G_FFN1, xn_prev, "xnp", ea)
    ffn(xT_prev, "xp", xn_prev, "xnp", PREV_TILES, wgu1, wd1, ea)
    norm_to_bf(xT_prev, "xp", PREV_TILES, G_MIX, xn_prev, "xnp", ea)
    pf(wkvq, 0, 4096)
    pf(wkvq, 1, 4096)
    pf(wkr_d, 0, 2048)
    S.barrier()
    xn = carve(XN_OFF, [128, 16, NT], BF16)
    xT = carve(R0_OFF, [128, 16, NT], F32)
    pal = Alloc(E_OFF, RING_OFF)
    p_raw = pal.get([128, 4, 512], F32)
    p_cs = pal.get([64, 2, NPREV], F32)
    p_t1 = pal.get([64, 512], F32)
    p_t2 = pal.get([64, 512], F32)
    p_ea = {"sq0": pal.get([128, 512], BF16), "sq1": pal.get([128, 512], BF16),
            "rstd": pal.get([128, 512], F32), "sqt": pal.get([128, 512], F32)}
    p_dupI = pal.get([64, 128], BF16)
    make_dupI(p_dupI)
    dma("sp", p_cs, c_cs_prev.rearrange("a p t -> p a t"), (), ["cs"])
    load_xT(x_own, NOWN, xT, 0, xstage, "x")
    load_xT(x_smp, NSMP, xT, NOWN, xstage, "x")
    wkv = []
    for u_ in range(2):
        wv, wk = wl(wkvq, u_, 4096)
        w4_ = wv.rearrange("p (a b c) -> p a b c", a=2, b=16, c=128)
        wkv += [(w4_[:, 0], wk), (w4_[:, 1], wk)]
    def prev_kv(ti, c0, n):
        for ch in range(4):
            b = bank(PS_A)
            S.add("pe", proj(wkv[ch][0], 16, xn_prev, "xnp", c0, n, b),
                  [wkv[ch][1]] + [k for kc in range(16) for k in xkeys("xnp", kc, c0, n)], [("ps", b)])
            S.add("act", lambda a, o=p_raw[:, ch, 0:n], i=psb(b, n): a.copy(out=o, in_=i), [("ps", b)], [("praw", ch)])
        feat_norm512(p_raw, "praw", n, G_KV, p_ea,
                     [(lambda kc, c0=c0, n=n: ckvp_bf[:, kc, c0:c0 + n], lambda kc, c0=c0: [("ckvp", kc, c0)])], "pkv")
    prev_kv(0, *PREV_TILES[0])
    wv, wk = wl(wkr_d, 0, 2048)
    wkr = wv.rearrange("p (a b) -> p a b", b=128)
    for ti, (c0, n) in enumerate(PREV_TILES):
        b0 = bank(PS_A)
        b1 = bank(PS_A)
        rk = [wk] + [k for kc in range(16) for k in xkeys("xnp", kc, c0, n)]
        S.add("pe", proj(wkr, 16, xn_prev, "xnp", c0, n, b0, M=64, mcol0=0), rk, [("ps", b0)])
        S.add("pe", proj(wkr, 16, xn_prev, "xnp", c0, n, b1, M=64, mcol0=64), rk, [("ps", b1)])
        rope_from_ps(b0, b1, n, p_cs, c0, p_t1, p_t2, [(krp_bf[:, c0:c0 + n], [("krp", c0)])])
        dup_rows(p_dupI, krp_full, c0, n, ("krp", c0))
    prev_kv(1, *PREV_TILES[1])
    for zc in range(8):
        if zc % 2 == 0:
            wv, wk = wl(wz_d, zc // 2, 4096)
            wz4 = wv.rearrange("p (a b c) -> p a b c", a=2, b=16, c=128)
        wz = wz4[:, zc % 2]
        b = bank(PS_A)
        S.add("pe", proj(wz, 16, xn_prev, "xnp", NPREV - 16, 16, b),
              [wk] + [k for kc in range(16) for k in xkeys("xnp", kc, NPREV - 16, 16)], [("ps", b)])
        S.add("act", lambda a, o=zhist[:, zc, :], i=psb(b, 16): a.copy(out=o, in_=i), [("ps", b)], [("zhist", zc)])
    rstd3 = carve(E_OFF + 36864, [128, NT], F32)
    for (c0, n) in LIN_TILES:
        rms_stats(lambda kc: (xT[:, kc, c0:c0 + n], xkeys("x", kc, c0, n)), 16, onesD, n, rstd3[:, c0:c0 + n], ("rstd3", c0),
                  [ea["sq0"], ea["sq1"]], ea["sqt"], "xn")
    S.barrier()

    for (c0, n) in LIN_TILES:
        for kc in range(16):
            S.add("dve", lambda v, o=xn[:, kc, c0:c0 + n], i=xT[:, kc, c0:c0 + n], s=gains[:, G_FFN1 + kc:G_FFN1 + kc + 1], r=rstd3[:, c0:c0 + n]:
                  v.scalar_tensor_tensor(out=o, in0=i, scalar=s, in1=r, op0=ALU.mult, op1=ALU.mult),
                  ["gains"], xkeys("xn", kc, c0, n))
    ffn(xT, "x", xn, "xn", LIN_TILES, wgu1, wd1, ea)
    norm_to_bf(xT, "x", LIN_TILES, G_MIX, xn, "xn", ea)
    pf(wkvq, 0, 4096)
    pf(wkvq, 1, 4096)
    pf(wkr_d, 0, 2048)
    S.barrier()
    spill_t = S.add("sp", lambda e: [e.dma_start(out=hs[:, kc * NT:(kc + 1) * NT], in_=xT[:, kc, :]) for kc in range(16)], (), ["hs"], dma=16)
    spill_t.has_dep = True
    S.pending["dve"] = S.pending.get("dve", set()) | {spill_t}

    r0 = Alloc(R0_OFF, E_OFF)
    oT = r0.get([128, 16, NT], BF16)
    mixedT = r0.get([128, 8, NT], BF16)
    qln = r0.get([128, 4, NT], BF16)
    ckvbf = r0.get([128, 4, NT], BF16)
    pooledT = carve(R0_OFF, [128, 8, NT], BF16)
    me = Alloc(E_OFF, RING_OFF)
    krbf_full = me.get([128, NT], BF16)
    krbf = krbf_full[0:64, :]
    cs = me.get([64, 2, NT], F32)
    m_dupI = me.get([64, 128], BF16)
    make_dupI(m_dupI)
    T_OFF = me.cur
    dma("sp", cs, c_cs_main.rearrange("a p t -> p a t"), (), ["cs"])

    ta = Alloc(T_OFF, RING_OFF)
    raw2 = [ta.get([128, 4, 512], F32) for _ in range(2)]
    kvn2 = [ta.get([128, 4, 512], F32)] * 2
    t1 = ta.get([64, 512], F32)
    t2 = ta.get([64, 512], F32)
    krn2 = [ta.get([64, 512], F32) for _ in range(2)]
    ckvT_out3 = ckvT_out.rearrange("(k p) t -> p k t", p=128)
    m_ea = {"sq0": ta.get([128, 512], BF16), "sq1": ta.get([128, 512], BF16),
            "rstd": ta.get([128, 512], F32), "sqt": ta.get([128, 512], F32)}

    wkv = []
    for u_ in range(2):
        wv, wk = wl(wkvq, u_, 4096)
        w4_ = wv.rearrange("p (a b c) -> p a b c", a=2, b=16, c=128)
        wkv += [(w4_[:, 0], wk), (w4_[:, 1], wk)]
    wv, wk = wl(wkr_d, 0, 2048)
    wkr = (wv.rearrange("p (a b) -> p a b", b=128), wk)
    for ti, (c0, n) in enumerate(MAIN_TILES):
        for ch in range(4):
            b = bank(PS_A)
            S.add("pe", proj(wkv[ch][0], 16, xn, "xn", c0, n, b),
                  [wkv[ch][1]] + [k for kc in range(16) for k in xkeys("xn", kc, c0, n)], [("ps", b)])
            S.add("act", lambda a, o=raw2[ti % 2][:, ch, 0:n], i=psb(b, n): a.copy(out=o, in_=i), [("ps", b)], [("rawkv", ti % 2, ch)])
        kvn = kvn2[ti % 2]
        krn = krn2[ti % 2]
        feat_norm512(raw2[ti % 2], ("rawkv", ti % 2), n, G_KV, m_ea,
                     [(lambda kc, n=n, kvn=kvn: kvn[:, kc, 0:n], lambda kc, ti=ti: [("kvn", 0, kc)])], "kv")
        for kc in range(4):
            S.add("act", lambda a, o=ckvbf[:, kc, c0:c0 + n], i=kvn[:, kc, 0:n]: a.copy(out=o, in_=i),
                  [("kvn", 0, kc)], [("ckvbf", kc, c0)])
        b0 = bank(PS_A)
        b1 = bank(PS_A)
        rk = [wkr[1]] + [k for kc in range(16) for k in xkeys("xn", kc, c0, n)]
        S.add("pe", proj(wkr[0], 16, xn, "xn", c0, n, b0, M=64, mcol0=0), rk, [("ps", b0)])
        S.add("pe", proj(wkr[0], 16, xn, "xn", c0, n, b1, M=64, mcol0=64), rk, [("ps", b1)])
        rope_from_ps(b0, b1, n, cs, c0, t1, t2, [(krn[:, 0:n], [("krn", ti % 2)]), (krbf[:, c0:c0 + n], [("krbf", c0)])])
        if c0 < NOWN:
            dup_rows(m_dupI, krbf_full, c0, n, ("krbf", c0))
        dma("sp", ckvT_out3[:, :, c0:c0 + n], kvn[:, :, 0:n], [("kvn", 0, kc) for kc in range(4)], [])
        dma("sp", krT_out[:, c0:c0 + n], krn[:, 0:n], [("krn", ti % 2)], [])
    wq = []
    for u_ in range(2):
        wv, wk = wl(wkvq, 2 + u_, 4096)
        w4_ = wv.rearrange("p (a b c) -> p a b c", a=2, b=16, c=128)
        wq += [(w4_[:, 0], wk), (w4_[:, 1], wk)]
    for ti, (c0, n) in enumerate(MAIN_TILES):
        for ch in range(4):
            b = bank(PS_A)
            S.add("pe", proj(wq[ch][0], 16, xn, "xn", c0, n, b),
                  [wq[ch][1]] + [k for kc in range(16) for k in xkeys("xn", kc, c0, n)], [("ps", b)])
            S.add("act", lambda a, o=raw2[(ti + 1) % 2][:, ch, 0:n], i=psb(b, n): a.copy(out=o, in_=i), [("ps", b)], [("rawq", ti % 2, ch)])
        feat_norm512(raw2[(ti + 1) % 2], ("rawq", ti % 2), n, G_Q, m_ea,
                     [(lambda kc, c0=c0, n=n: qln[:, kc, c0:c0 + n], lambda kc, c0=c0: [("qln", kc, c0)])], "ql")

    pf(wz_d, 0, 4096)
    pf(wz_d, 1, 4096)
    S.barrier()
    tb_ = Alloc(T_OFF, RING_OFF)
    zxb = [tb_.get([128, 1232], F32) for _ in range(2)]
    ppa = tb_.get([128, 1232], F32)
    ppb = tb_.get([128, 1232], F32)
    fx16 = tb_.get([128, 16], F32)
    SEG = [(0, 1024, 0)] + [(1040 + 48 * j, 32, NOWN + 32 * j) for j in range(4)]
    for zc in range(8):
        g = zc // 2
        w = 2 << g
        if zc % 2 == 0:
            wv, wk = wl(wz_d, zc // 2, 4096)
            wz4 = wv.rearrange("p (a b c) -> p a b c", a=2, b=16, c=128)
        wz = wz4[:, zc % 2]
        zx = zxb[zc % 2]
        zk = ("zx", zc % 2)
        S.add("dve", lambda v, o=zx[:, 0:16], i=zhist[:, zc, :]: v.tensor_copy(out=o, in_=i), [("zhist", zc)], [(zk, "h0")])
        dma("sp", zx[:, 1040:1232].rearrange("p (j t) -> p j t", t=48)[:, :, 1:16],
            state_pool[zc * 128:(zc + 1) * 128, :].rearrange("p (j t) -> p j t", t=15), (), [(zk, "h", j) for j in range(4)])
        for (c0, n) in MAIN_TILES:
            b = bank(PS_A)
            S.add("pe", proj(wz, 16, xn, "xn", c0, n, b),
                  [wk] + [k for kc in range(16) for k in xkeys("xn", kc, c0, n)], [("ps", b)])
            if c0 < NOWN:
                S.add("act", lambda a, o=zx[:, 16 + c0:16 + c0 + n], i=psb(b, n): a.copy(out=o, in_=i), [("ps", b)], [(zk, "z", c0)])
            else:
                dst = zx[:, 1040:1232].rearrange("p (j t) -> p j t", t=48)[:, :, 16:48]
                src = psb(b, n).rearrange("p (j t) -> p j t", t=32)
                S.add("act", lambda a, o=dst, i=src: a.copy(out=o, in_=i), [("ps", b)], [(zk, "z", c0)])
        allz = [(zk, "h0")] + [(zk, "h", j) for j in range(4)] + [(zk, "z", c0) for (c0, n) in MAIN_TILES]
        dma("sp", poolT_own[zc * 128:(zc + 1) * 128, :], zx[:, 1025:1040], allz, [])
        dma("sp", poolT_smp[zc * 128:(zc + 1) * 128, :].rearrange("p (j t) -> p j t", t=15),
            zx[:, 1040:1232].rearrange("p (j t) -> p j t", t=48)[:, :, 33:48], allz, [])
        cur, curk = zx, allz
        k = 1
        tog = 0
        while k < w:
            nxt = ppa if tog == 0 else ppb
            nk = ["ppa"] if tog == 0 else ["ppb"]
            S.add("dve", lambda v, o=nxt[:, k:1232], i0=cur[:, k:1232], i1=cur[:, 0:1232 - k]: v.tensor_tensor(out=o, in0=i0, in1=i1, op=ALU.add),
                  curk, nk)
            S.add("dve", lambda v, o=nxt[:, 0:k], i=cur[:, 0:k]: v.tensor_copy(out=o, in_=i), curk, [nk[0] + "h"])
            cur, curk = nxt, nk + [nk[0] + "h"]
            tog ^= 1
            k *= 2
        for si_, (h0, nt, tc) in enumerate(SEG):
            S.add("dve", lambda v, o=pooledT[:, zc, tc:tc + nt], i0=cur[:, h0 + 16:h0 + 16 + nt], i1=zx[:, h0 + 16:h0 + 16 + nt], sc=1.0 / w:
                  v.scalar_tensor_tensor(out=o, in0=i0, scalar=sc, in1=i1, op0=ALU.mult, op1=ALU.subtract),
                  curk + allz, [("pooled", zc, tc)])
        S.add("dve", lambda v, o=fx16, i0=cur[:, 16:32], i1=rc16[:, g, :]:
              v.tensor_tensor(out=o, in0=i0, in1=i1, op=ALU.mult), curk + ["rc16"], ["fix16"])
        S.add("dve", lambda v, o=pooledT[:, zc, 0:16], i0=fx16, i1=zx[:, 16:32]: v.tensor_tensor(out=o, in0=i0, in1=i1, op=ALU.subtract),
              ["fix16"] + allz + [("pooled", zc, 0)], [("pooled", zc, 0)])
    wv, wpk = wl(wpool, 0, 2048)
    wp4 = wv.rearrange("p (a b c) -> p a b c", a=4, b=2, c=256)
    for (c0, n) in MAIN_TILES:
        for g in range(4):
            for dd in range(2):
                b = bank(PS_A)

                def mm(pe, g=g, dd=dd, b=b, c0=c0, n=n):
                    r = None
                    for cc in range(2):
                        r = pe.matmul(psb(b, n), wp4[:, g, cc, dd * 128:(dd + 1) * 128], pooledT[:, 2 * g + cc, c0:c0 + n], start=(cc == 0), stop=(cc == 1))
                    return r
                rk = [wpk] + [("pooled", 2 * g + cc, tc) for cc in range(2) for tc in ([0, 512] if c0 < NOWN else [NOWN + 32 * j for j in range(4)])]
                S.add("pe", mm, rk, [("ps", b)])
                S.add("dve", lambda v, o=mixedT[:, 2 * g + dd, c0:c0 + n], i=psb(b, n), s=gains[:, G_PS + 2 * g + dd:G_PS + 2 * g + dd + 1]:
                      v.tensor_scalar_mul(out=o, in0=i, scalar1=s), [("ps", b), "gains"], [("mixed", 2 * g + dd, c0)])
    pf(watt, 0, 2048)
    pf(watt, 1, 2048)
    S.barrier()
    if DBG:
        dma("sp", dbg_pooled, pooledT.rearrange("p a b -> p (a b)"), (), [])
        dma("sp", dbg_mixed, mixedT.rearrange("p a b -> p (a b)"), (), [])
        S.barrier()

    aa = Alloc(T_OFF, RING_OFF)
    WS = []
    for _ in range(2):
        WS.append({"qn": aa.get([128, 1024], BF16), "qr": aa.get([128, 1024], BF16),
                   "kn": aa.get([128, 2048], BF16), "v": aa.get([128, 16, 128], BF16)})
    pts = [aa.get([128, 512], BF16) for _ in range(4)]
    rs = aa.get([128, 512], F32)
    cs2 = aa.get([128, 1024], F32)
    dma("sp", cs2[0:64, :], c_cs_main[0, :, 0:1024], (), ["cs2a"])
    dma("sp", cs2[64:128, :], c_cs_main[1, :, 0:1024], (), ["cs2b"])
    A2_OFF = aa.cur
    PS_S = [0, 1, 2]
    PS_O = [3]
    PS_SUM = [4]
    PS_P = [5, 6, 7]
    bo_sb = [aa.get([128, 512], F32) for _ in range(2)]
    bs_sb = [aa.get([128, 512], F32) for _ in range(2)]
    fin_i = [0]
    pt_i = [0]

    def attend(ws, wsk, h, qcol0, nq, ocol0, blocks):
        LOOK = 2
        bo = bank(PS_O)
        bs = bank(PS_SUM)
        nb = len(blocks)
        ptinfo = {}
        for it in range(nb + LOOK):
            if it < nb:
                (kn_ap, kr_ap, v_ap, nk, bias_ap, qoff, mask, keys) = blocks[it]
                nqq = nq - qoff
                b = bank(PS_S)

                def mm(pe, b=b, kn_ap=kn_ap, kr_ap=kr_ap, nk=nk, qoff=qoff, nqq=nqq):
                    pe.matmul(psb(b, nqq, nk), kn_ap, ws["qn"][:, qcol0 + qoff:qcol0 + qoff + nqq], start=True, stop=False)
                    return pe.matmul(psb(b, nqq, nk), kr_ap, ws["qr"][:, qcol0 + qoff:qcol0 + qoff + nqq], start=False, stop=True)
                S.add("pe", mm, keys + [(wsk, "qn"), (wsk, "qr")], [("ps", b)])
                pi = pt_i[0] % len(pts)
                pt_i[0] += 1
                pt = pts[pi]
                ptinfo[it] = (pi, pt)
                if bias_ap is None:
                    S.add("act", lambda a, o=pt[0:nk, 0:nqq], i=psb(b, nqq, nk): a.activation(out=o, in_=i, func=AF.Exp, scale=SM_SCALE),
                          [("ps", b)], [("pt", pi)])
                else:
                    S.add("act", lambda a, o=pt[0:nk, 0:nqq], i=psb(b, nqq, nk), bb=bias_ap: a.activation(out=o, in_=i, func=AF.Exp, bias=bb, scale=SM_SCALE),
                          [("ps", b), "prevb"], [("pt", pi)])
                if mask:
                    S.add("dve", lambda v, o=pt[64:128, 0:64]: v.memset(o, 0.0), [("pt", pi)], [("pt", pi)])
            bi = it - LOOK
            if bi >= 0:
                (kn_ap, kr_ap, v_ap, nk, bias_ap, qoff, mask, keys) = blocks[bi]
                nqq = nq - qoff
                pi, pt = ptinfo[bi]
                S.add("pe", lambda pe, o=psb(bo, nq)[:, qoff:nq], l=v_ap, r=pt[0:nk, 0:nqq], st=(bi == 0), sp_=(bi == nb - 1):
                      pe.matmul(o, l, r, start=st, stop=sp_), [("pt", pi), (wsk, "v")] + keys, [("ps", bo)])
                S.add("pe", lambda pe, o=psb(bs, nq)[:, qoff:nq], l=ones1[0:nk, :], r=pt[0:nk, 0:nqq], st=(bi == 0), sp_=(bi == nb - 1):
                      pe.matmul(o, l, r, start=st, stop=sp_), [("pt", pi), "ones1"], [("ps", bs)])
        ai = fin_i[0] % 2
        fin_i[0] += 1
        S.add("act", lambda a, o=bs_sb[ai][:, 0:nq], i=psb(bs, nq): a.copy(out=o, in_=i), [("ps", bs)], [("bs_sb", ai)])
        S.add("act", lambda a, o=bo_sb[ai][:, 0:nq], i=psb(bo, nq): a.copy(out=o, in_=i), [("ps", bo)], [("bo_sb", ai)])
        def fin():
            S.add("dve", lambda v, o=rs[:, 0:nq], i=bs_sb[ai][:, 0:nq]: v.reciprocal(out=o, in_=i), [("bs_sb", ai)], ["rs"])
            S.add("dve", lambda v, o=oT[:, h, ocol0:ocol0 + nq], i0=bo_sb[ai][:, 0:nq], i1=rs[:, 0:nq]: v.tensor_tensor(out=o, in0=i0, in1=i1, op=ALU.mult),
                  [("bo_sb", ai), "rs"], [("oT", h, ocol0)])
        return fin

    head_full = {}

    def head_weights(h):
        wv, wk = wl(watt, h, 2048)
        w4 = wv.rearrange("p (a b c) -> p a b c", a=4, b=4, c=128)
        return w4, wk

    def head_weights_s(h):
        wv, wk = wl(watt_s, h, 1536)
        head_full[h] = wv
        w4 = wv[:, 0:1024].rearrange("p (a b c) -> p a b c", a=2, b=4, c=128)
        return w4, wk

    def q_proj(w4, wk, ws, wsk, c0, n, dcol0, cscol0):
        rk = [wk] + [("qln", kc, c) for kc in range(4) for c in range(c0 - c0 % 128, c0 + n, 128)] + \
             [("qln", kc, cc) for kc in range(4) for cc in (0, 512, 1024)]
        b = bank(PS_P)
        S.add("pe", proj(w4[:, 0], 4, qln, "qln", c0, n, b), rk, [("ps", b)])
        S.add("act", lambda a, o=ws["qn"][:, dcol0:dcol0 + n], i=psb(b, n): a.copy(out=o, in_=i), [("ps", b)], [(wsk, "qn")])
        b0 = bank(PS_P)
        S.add("pe", proj(w4[:, 1], 4, qln, "qln", c0, n, b0, M=128, mcol0=0), rk, [("ps", b0)])
        S.add("dve", lambda v, o=ws["qr"][:, dcol0:dcol0 + n], i0=psb(b0, n), i1=cs2[:, cscol0:cscol0 + n]: v.tensor_tensor(out=o, in0=i0, in1=i1, op=ALU.mult),
              [("ps", b0), "cs2a", "cs2b"], [(wsk, "qr")])

    def k_proj(w4, wk, ws, wsk, src, srckeys, c0, n, dcol0):
        b = bank(PS_P)
        S.add("pe", proj(w4[:, 2], 4, src, None, c0, n, b), [wk] + srckeys, [("ps", b)])
        S.add("dve", lambda v, o=ws["kn"][:, dcol0:dcol0 + n], i=psb(b, n): v.tensor_copy(out=o, in_=i), [("ps", b)], [(wsk, "kn")])

    def v_proj(w4, wk, ws, wsk, src, srckeys, c0, nk, blk):
        b = bank(PS_P)

        def mm(pe):
            r = None
            for kc in range(4):
                r = pe.matmul(psb(b, 128, nk), src[:, kc, c0:c0 + nk], w4[:, 3, kc, :], start=(kc == 0), stop=(kc == 3))
            return r
        S.add("pe", mm, [wk] + srckeys, [("ps", b)])
        S.add("act", lambda a, o=ws["v"][0:nk, blk, :], i=psb(b, 128, nk): a.copy(out=o, in_=i), [("ps", b)], [(wsk, "v")])

    def v_proj4(w4, wk, ws, wsk, src, c0, blk0):
        b = bank(PS_P)

        def mm(pe):
            r = None
            for j in range(4):
                for kc in range(4):
                    r = pe.matmul(psb(b)[:, j * 128:(j + 1) * 128], src[:, kc, c0 + j * 128:c0 + (j + 1) * 128], w4[:, 3, kc, :],
                                  start=(kc == 0), stop=(kc == 3))
            return r
        S.add("pe", mm, [wk], [("ps", b)])
        S.add("act", lambda a, o=ws["v"][:, blk0:blk0 + 4, :], i=psb(b).rearrange("p (a b) -> p a b", b=128): a.copy(out=o, in_=i),
              [("ps", b)], [(wsk, "v")])

    ckvp_keys = []
    own_keys = []
    def head_proj(h):
        ws = WS[h % 2]
        wsk = ("ws", h % 2)
        w4, wk = head_weights(h)
        for (c0, n) in [(0, 512), (512, 512)]:
            q_proj(w4, wk, ws, wsk, c0, n, c0, c0)
        for (c0, n) in [(0, 512), (512, 512)]:
            k_proj(w4, wk, ws, wsk, ckvp_bf, [], c0, n, c0)
            k_proj(w4, wk, ws, wsk, ckvbf, [], c0, n, 1024 + c0)
        for g4 in range(2):
            v_proj4(w4, wk, ws, wsk, ckvp_bf, g4 * 512, g4 * 4)
        for g4 in range(2):
            v_proj4(w4, wk, ws, wsk, ckvbf, g4 * 512, 8 + g4 * 4)

    def head_attend(h, qt):
        ws = WS[h % 2]
        wsk = ("ws", h % 2)
        q0 = qt * 512
        blocks = []
        for blk in range(8):
            blocks.append((ws["kn"][:, blk * 128:(blk + 1) * 128], krp_full[:, blk * 128:(blk + 1) * 128], ws["v"][:, blk, :], 128,
                           prevb[:, 0:1], 0, False, [(wsk, "kn")]))
        for blk in range(4 * qt):
            blocks.append((ws["kn"][:, 1024 + blk * 128:1024 + (blk + 1) * 128], krbf_full[:, blk * 128:(blk + 1) * 128], ws["v"][:, 8 + blk, :], 128,
                           None, 0, False, [(wsk, "kn")]))
        for j in range(4):
            blk = 4 * qt + j
            blocks.append((ws["kn"][:, 1024 + blk * 128:1024 + (blk + 1) * 128], krbf_full[:, blk * 128:(blk + 1) * 128], ws["v"][:, 8 + blk, :], 128,
                           None, j * 128, True, [(wsk, "kn")]))
        return attend(ws, wsk, h, q0, 512, q0, blocks)

    head_proj(0)
    for h in range(16):
        fin0 = head_attend(h, 0)
        if h + 1 < 16:
            head_proj(h + 1)
        fin0()
        fin1 = head_attend(h, 1)
        fin1()
    pf(watt_s, 0, 1536)
    pf(watt_s, 1, 1536)
    S.barrier()

    SC0 = NOWN
    sa = Alloc(E_OFF + 2304 + 9216, RING_OFF)
    qabs = sa.get([128, 4, 4, 512], BF16)
    qrs = sa.get([64, 4, 512], BF16)
    qn_s = [sa.get([128, 128], BF16) for _ in range(2)]
    s_t1 = sa.get([64, 128], F32)
    s_t2 = sa.get([64, 128], F32)
    S2_OFF = sa.cur
    PS_Q = [0, 1, 2, 3]
    sb_ = Alloc(E_OFF + 2304, E_OFF + 2304 + 9216)
    sc_ = Alloc(S2_OFF, RING_OFF)
    ctok2 = [sc_.get([128, 4, 512], BF16), sb_.get([128, 4, 512], BF16)]
    cacheT2 = [sc_.get([128, 4, 512], BF16), sb_.get([128, 4, 512], BF16)]
    ckrT2 = [sc_.get([64, 512], BF16), sb_.get([64, 512], BF16)]
    new_tok = sc_.get([32, 512], BF16)
    pts = [sc_.get([128, 512], BF16) for _ in range(3)]
    rs = sc_.get([128, 512], F32)
    PS_S2 = [0, 1]
    PS_PV = [2, 3, 4, 5]
    PS_SM = 6
    PS_T = [7]
    pt_j = [0]

    def cache_load(u):
        bb, hf = u // 2, u % 2
        dma("pool", ctok2[u % 2], cache_ckv[bb, hf * 512:(hf + 1) * 512, :].rearrange("(k p) c -> p k c", p=128), (), [("ctok", u % 2)])
        dma("pool", cacheT2[u % 2], cache_ckvT[bb].rearrange("(k p) t -> p k t", p=128)[:, :, hf * 512:(hf + 1) * 512], (), [("cacheT", u % 2)])
        dma("pool", ckrT2[u % 2], cache_krT[bb][:, hf * 512:(hf + 1) * 512], (), [("ckrT", u % 2)])

    cache_load(0)
    s1 = {}

    def s1_a(h):
        w4, wk = head_weights_s(h)
        wukT = head_full[h][:, 1024:1536]
        rk = [wk] + [("qln", kc, SC0) for kc in range(4)]
        b = bank(PS_Q)
        S.add("pe", proj(w4[:, 0], 4, qln, "qln", SC0, 128, b), rk, [("ps", b)])
        qs = qn_s[h % 2]
        S.add("act", lambda a_, o=qs, i=psb(b, 128): a_.copy(out=o, in_=i), [("ps", b)], [("qn_s", h % 2)])
        b0 = bank(PS_Q)
        S.add("pe", proj(w4[:, 1], 4, qln, "qln", SC0, 128, b0, M=64, mcol0=0), rk, [("ps", b0)])
        b1 = bank(PS_Q)
        S.add("pe", proj(w4[:, 1], 4, qln, "qln", SC0, 128, b1, M=64, mcol0=64), rk, [("ps", b1)])
        S.add("dve", lambda v, o=s_t1, i0=psb(b0, 128, 64), i1=cs[:, 0, SC0:SC0 + 128]: v.tensor_tensor(out=o, in0=i0, in1=i1, op=ALU.mult),
              [("ps", b0), "cs"], ["s_t1"])
        S.add("dve", lambda v, o=s_t2, i0=psb(b1, 128, 64), i1=cs[:, 1, SC0:SC0 + 128]: v.tensor_tensor(out=o, in0=i0, in1=i1, op=ALU.mult),
              [("ps", b1), "cs"], ["s_t2"])
        S.add("dve", lambda v, o=qrs[:, :, h * 32:(h + 1) * 32], i0=s_t1.rearrange("p (b q) -> p b q", q=32), i1=s_t2.rearrange("p (b q) -> p b q", q=32):
              v.tensor_tensor(out=o, in0=i0, in1=i1, op=ALU.add), ["s_t1", "s_t2"], [("qrs", h)])
        s1[h] = (wukT, wk, qs)

    def s1_b(h):
        wukT, wk, qs = s1[h]
        b2 = bank(PS_Q)

        def mmq(pe, b2=b2, wukT=wukT, qs=qs):
            r = None
            for c in range(4):
                r = pe.matmul(psb(b2)[:, c * 128:(c + 1) * 128], wukT[:, c * 128:(c + 1) * 128], qs, start=True, stop=True)
            return r
        S.add("pe", mmq, [wk, ("qn_s", h % 2)], [("ps", b2)])
        S.add("act", lambda a_, o=qabs[:, :, :, h * 32:(h + 1) * 32], i=psb(b2).rearrange("p (c b q) -> p c b q", c=4, b=4): a_.copy(out=o, in_=i),
              [("ps", b2)], [("qabs", h)])

    s1_a(0)
    for h in range(16):
        if h + 1 < 16:
            s1_a(h + 1)
        s1_b(h)
    S.barrier()

    def cache_transposes(u):
        pass

    cache_load(1)
    cache_transposes(0)
    for bi_ in range(4):
        tc = NOWN + 32 * bi_
        b = bank(PS_T)
        for c in range(4):
            S.add("pe", lambda pe, o=psb_bf(b, 1024, 32)[:, c * 128:(c + 1) * 128], i=ckvbf[:, c, tc:tc + 32]: pe.transpose(o, i, ident_b),
                  ["ident_b"], [("ps", b)])
        S.add("dve", lambda v, o=new_tok, i=psb_bf(b, 512, 32): v.tensor_copy(out=o, in_=i), [("ps", b)], ["new_tok"])
        nb = 9
        info = {}
        for it in range(nb + 1):
            if it < nb:
                nk = 128 if it < 8 else 32
                u = 2 * bi_ + it // 4
                ub, lb = u % 2, it % 4
                if it == 2:
                    cache_transposes(2 * bi_ + 1)
                if it == 6 and bi_ < 3:
                    cache_transposes(2 * bi_ + 2)
                bsx = bank(PS_S2)

                def mms(pe, it=it, nk=nk, bsx=bsx, bi_=bi_, tc=tc, ub=ub, lb=lb):
                    for c in range(4):
                        l = cacheT2[ub][:, c, lb * 128:(lb + 1) * 128] if it < 8 else ckvbf[:, c, tc:tc + 32]
                        pe.matmul(psb(bsx, 512, nk), l, qabs[:, c, bi_, :], start=(c == 0), stop=False)
                    l = ckrT2[ub][:, lb * 128:(lb + 1) * 128] if it < 8 else krbf[:, tc:tc + 32]
                    return pe.matmul(psb(bsx, 512, nk), l, qrs[:, bi_, :], start=False, stop=True)
                rkeys = ([("cacheT", ub), ("ckrT", ub)] if it < 8 else []) + [("qabsb", bi_)]
                S.add("pe", mms, rkeys, [("ps", bsx)])
                pi = pt_j[0] % 3
                pt_j[0] += 1
                pt = pts[pi]
                info[it] = (pi, pt, nk, ub, lb)
                S.add("act", lambda a_, o=pt[0:nk, :], i=psb(bsx, 512, nk): a_.activation(out=o, in_=i, func=AF.Exp, scale=SM_SCALE),
                      [("ps", bsx)], [("pt", pi)])
            j = it - 1
            if j >= 0:
                pi, pt, nk, ub, lb = info[j]

                def mmpv(pe, j=j, nk=nk, pt=pt, ub=ub, lb=lb):
                    for c in range(4):
                        l = ctok2[ub][:, lb, c * 128:(c + 1) * 128] if j < 8 else new_tok[:, c * 128:(c + 1) * 128]
                        pe.matmul(psb(PS_PV[c]), l, pt[0:nk, :], start=(j == 0), stop=(j == nb - 1))
                    return pe.matmul(psb(PS_SM), ones1[0:nk, :], pt[0:nk, :], start=(j == 0), stop=(j == nb - 1))
                S.add("pe", mmpv, [("pt", pi), ("ctok", ub), "new_tok", "ones1"], [("ps", PS_PV[c]) for c in range(4)] + [("ps", PS_SM)])
                if j in (3, 7):
                    uu = 2 * bi_ + j // 4
                    if uu + 2 < 8:
                        cache_load(uu + 2)
        S.add("dve", lambda v, o=rs, i=psb(PS_SM): v.reciprocal(out=o, in_=i), [("ps", PS_SM)], ["rs"])
        for c in range(4):
            S.add("dve", lambda v, o=qabs[:, c, bi_, :], i0=psb(PS_PV[c]), i1=rs: v.tensor_tensor(out=o, in0=i0, in1=i1, op=ALU.mult),
                  [("ps", PS_PV[c]), "rs"], [("olat", bi_, c), ("qabsb", bi_)])
    for h in range(16):
        if h % 8 == 0:
            wv_, wk = wl(wuv_s, h // 8, 4096)
            wuv8 = wv_.rearrange("p (a b c) -> p a b c", a=8, b=4, c=128)
        if h % 4 == 0:
            b = bank(PS_S2)

        def mmo(pe, b=b, wuv8=wuv8, h=h):
            r = None
            for c in range(4):
                r = pe.matmul(psb(b)[:, (h % 4) * 128:(h % 4 + 1) * 128], wuv8[:, h % 8, c, :], qabs[:, c, :, h * 32:(h + 1) * 32], start=(c == 0), stop=(c == 3))
            return r
        S.add("pe", mmo, [wk] + [("olat", bb, c) for bb in range(4) for c in range(4)], [("ps", b)])
        if h % 4 == 3:
            h0 = h - 3
            S.add("act", lambda a_, o=oT[:, h0:h0 + 4, SC0:SC0 + 128], i=psb(b).rearrange("p (a b) -> p a b", b=128): a_.copy(out=o, in_=i),
                  [("ps", b)], [("oT", h0, SC0)])
    pf(wgate, 0, 4096)
    pf(wmrg, 0, 3072)
    S.barrier()

    if DBG:
        dma("sp", dbg_o, oT.rearrange("p a b -> p (a b)"), (), [])
        S.barrier()
    ma = Alloc(E_OFF, RING_OFF)
    mT = ma.get([128, 16, NT], BF16)
    sga = [ma.get([128, 512], F32) for _ in range(2)]
    sgb = [ma.get([128, 512], F32) for _ in range(2)]
    mt1 = [ma.get([128, 512], F32) for _ in range(2)]
    mi = [0]
    for dc in range(16):
        wv, wgk = wl(wgate, dc, 4096)
        wg4 = wv.rearrange("p (a b c) -> p a b c", a=2, b=16, c=128)
        wv, wmk = wl(wmrg, dc, 3072)
        wo3 = wv[:, 0:2048].rearrange("p (b c) -> p b c", c=128)
        wpo3 = wv[:, 2048:3072].rearrange("p (b c) -> p b c", c=128)
        for (c0, n) in LIN_TILES:
            i2 = mi[0] % 2
            mi[0] += 1
            bga = bank(PS_A)
            S.add("pe", proj(wg4[:, 0], 16, xn, None, c0, n, bga), [wgk], [("ps", bga)])
            bgb = bank(PS_A)
            S.add("pe", proj(wg4[:, 1], 16, xn, None, c0, n, bgb), [wgk], [("ps", bgb)])
            ba = bank(PS_B)
            S.add("pe", proj(wpo3, 8, mixedT, None, c0, n, ba), [wmk], [("ps", ba)])
            bb = bank(PS_C)
            S.add("pe", proj(wo3, 16, oT, None, c0, n, bb), [wmk], [("ps", bb)])
            S.add("act", lambda a, o=sga[i2][:, 0:n], i=psb(bga, n): a.activation(out=o, in_=i, func=AF.Sigmoid), [("ps", bga)], [("sga", i2)])
            S.add("act", lambda a, o=sgb[i2][:, 0:n], i=psb(bgb, n): a.activation(out=o, in_=i, func=AF.Sigmoid), [("ps", bgb)], [("sgb", i2)])
            S.add("dve", lambda v, o=mt1[i2][:, 0:n], i0=sga[i2][:, 0:n], i1=psb(ba, n): v.tensor_tensor(out=o, in0=i0, in1=i1, op=ALU.mult),
                  [("sga", i2), ("ps", ba)], [("mt1", i2)])
            S.add("dve", lambda v, o=sgb[i2][:, 0:n], i0=sgb[i2][:, 0:n], i1=psb(bb, n): v.tensor_tensor(out=o, in0=i0, in1=i1, op=ALU.mult),
                  [("sgb", i2), ("ps", bb)], [("sgb", i2)])
            S.add("dve", lambda v, o=mT[:, dc, c0:c0 + n], i0=mt1[i2][:, 0:n], i1=sgb[i2][:, 0:n]: v.tensor_tensor(out=o, in0=i0, in1=i1, op=ALU.add),
                  [("mt1", i2), ("sgb", i2)], [("mT", dc, c0)])
    pf(wout, 0, 4096)
    S.barrier()
    if DBG:
        dma("sp", dbg_m, mT.rearrange("p a b -> p (a b)"), (), [])
        S.barrier()
    for kc in range(16):
        dma("sp", xT[:, kc, :], hs[:, kc * NT:(kc + 1) * NT], (), xkeys("x", kc, 0, NT))
    for up in range(8):
        wv, wk = wl(wout, up, 4096)
        w4o = wv.rearrange("p (a b c) -> p a b c", a=2, b=16, c=128)
        for d2 in range(2):
            dc = up * 2 + d2
            for (c0, n) in LIN_TILES:
                b = bank(PS_A)
                S.add("pe", proj(w4o[:, d2], 16, mT, None, c0, n, b), [wk], [("ps", b)])
                S.add("dve", lambda v, o=xT[:, dc, c0:c0 + n], i0=psb(b, n): v.tensor_tensor(out=o, in0=i0, in1=o, op=ALU.add),
                      [("ps", b)] + xkeys("x", dc, c0, n), xkeys("x", dc, c0, n))
    S.barrier()

    if DBG:
        dma("sp", dbg_x1, xT.rearrange("p a b -> p (a b)"), (), [])
        S.barrier()
    eal2 = Alloc(E_OFF, RING_OFF)
    ea2 = ffn_alloc(eal2, NT)
    ystage = [eal2.get([128, D], F32) for _ in range(2)]
    norm_to_bf(xT, "x", LIN_TILES, G_FFN2, xn, "xn", ea2)
    ffn(xT, "x", xn, "xn", LIN_TILES, wgu2, wd2, ea2)
    S.barrier()

    fa = Alloc(E_OFF, RING_OFF)
    f_ea = {"sq0": fa.get([128, 512], BF16), "sq1": fa.get([128, 512], BF16),
            "rstd": fa.get([128, 512], F32), "sqt": fa.get([128, 512], F32)}
    yT = [fa.get([128, 16, 384], F32) for _ in range(2)]
    y_own3 = y_own.rearrange("(k p) t -> p k t", p=128)
    y_smp3 = y_smp.rearrange("(k p) t -> p k t", p=128)
    FIN_TILES = [(0, 384), (384, 384), (768, 256), (1024, 128)]
    for ti, (c0, n) in enumerate(FIN_TILES):
        rstd = f_ea["rstd"][:, 0:n]
        rms_stats(lambda kc: (xT[:, kc, c0:c0 + n], []), 16, onesD, n, rstd, ("rstd", "fin"), [f_ea["sq0"], f_ea["sq1"]], f_ea["sqt"], "fin")
        yt = yT[ti % 2]
        for kc in range(16):
            S.add("dve", lambda v, o=yt[:, kc, 0:n], i=xT[:, kc, c0:c0 + n], s=gains[:, G_FIN + kc:G_FIN + kc + 1], r=rstd:
                  v.scalar_tensor_tensor(out=o, in0=i, scalar=s, in1=r, op0=ALU.mult, op1=ALU.mult),
                  [("rstd", "fin"), "gains"], [("yT", ti % 2, kc // 4)])
        for g in range(4):
            n_own = max(0, min(c0 + n, NOWN) - c0)
            lst = []
            if n_own > 0:
                lst.append((y_own3[:, g * 4:(g + 1) * 4, c0:c0 + n_own], yt[:, g * 4:(g + 1) * 4, 0:n_own]))
            if n_own < n:
                lst.append((y_smp3[:, g * 4:(g + 1) * 4, c0 + n_own - NOWN:c0 + n - NOWN], yt[:, g * 4:(g + 1) * 4, n_own:n]))
            S.add("sp", lambda e, lst=lst: [e.dma_start(out=o, in_=i) for (o, i) in lst], [("yT", ti % 2, g)], [], dma=len(lst))

    assert not prefetched, list(prefetched)
    S.emit_all(nc, stack)
    stack.close()
    return nc


_CACHE = {}


def kernel(x_prompt, x_sample, cache_ckv, cache_krope, state_pool,
           g_ffn1, w1_gate, w1_up, w1_down, g_mix, w_in, g_q_lat, g_kv_lat,
           w_uq, w_uk, w_uv, w_o_attn, w_pool, pool_scale, w_pool_out, w_out,
           g_ffn2, w2_gate, w2_up, w2_down, g_final):
    f = np.float32
    A = lambda a: np.ascontiguousarray(np.asarray(a, dtype=f))
    x_prompt, x_sample = A(x_prompt), A(x_sample)
    cache_ckv, cache_krope, state_pool = A(cache_ckv)[0], A(cache_krope)[0], A(state_pool)[0]
    wgu1, wd1 = prep_ffn(A(w1_gate)[0], A(w1_up)[0], A(w1_down)[0])
    wgu2, wd2 = prep_ffn(A(w2_gate)[0], A(w2_up)[0], A(w2_down)[0])
    win = A(w_in)[0]
    z_c, ql_c, kv_c, kr_c = win[:, 0:1024], win[:, 1024:1536], win[:, 1536:2048], win[:, 2048:2112]
    gA_c, gB_c = win[:, 2112:2112 + 2048], win[:, 2112 + 2048:2112 + 4096]
    kr2 = np.concatenate([kr_c, swap_half(kr_c)], axis=1)
    def pair_units(pc):
        n_ = pc.shape[0]
        return np.ascontiguousarray(pc.reshape(n_ // 2, 2, 128, 2048).transpose(0, 2, 1, 3)).reshape(n_ // 2, 128, 4096)
    wkvq = pair_units(prep_cols(np.concatenate([kv_c, ql_c], axis=1)))
    wkr_h = prep_cols(kr2)
    wz_h = pair_units(prep_cols(z_c))
    wgate = np.ascontiguousarray(np.stack([prep_cols(gA_c), prep_cols(gB_c)], axis=2)).reshape(16, 128, 4096)
    wp = A(w_pool)[0]
    wpool = np.ascontiguousarray(wp.reshape(4, 2, 128, 256).transpose(2, 0, 1, 3)).reshape(1, 128, 2048)
    uq = A(w_uq)[0]
    uk = A(w_uk)[0]
    uv = A(w_uv)[0]
    uqn = uq[:, :, 0:128]
    uqr = uq[:, :, 128:192]
    uqr2 = np.concatenate([uqr, swap_half(uqr)], axis=2)

    def per_head(Wh):
        return np.ascontiguousarray(Wh.reshape(4, 128, 16, 128).transpose(2, 1, 0, 3)).reshape(16, 128, 512)
    watt4 = np.ascontiguousarray(np.stack([per_head(uqn), per_head(uqr2), per_head(uk), per_head(uv)], axis=2)).reshape(16, 128, 2048)
    ukT = np.ascontiguousarray(uk.transpose(1, 2, 0))
    watt = watt4
    watt_s = np.ascontiguousarray(np.concatenate([per_head(uqn), per_head(uqr2), ukT], axis=2))
    wuv_s = np.ascontiguousarray(per_head(uv).reshape(2, 8, 128, 512).transpose(0, 2, 1, 3)).reshape(2, 128, 4096)
    wo_c = prep_cols(A(w_o_attn)[0])
    wpo_c = prep_cols(A(w_pool_out)[0])
    wmrg = np.ascontiguousarray(np.concatenate([wo_c, wpo_c], axis=2))
    wout_c = prep_cols(A(w_out)[0])
    wout = np.ascontiguousarray(wout_c.reshape(8, 2, 128, 2048).transpose(0, 2, 1, 3)).reshape(8, 128, 4096)

    def gcol(g, n):
        return np.asarray(g, dtype=f).reshape(n, 128).T
    gains = np.ascontiguousarray(np.concatenate([
        gcol(A(g_ffn1)[0], 16), gcol(A(g_mix)[0], 16), gcol(A(g_ffn2)[0], 16), gcol(A(g_final), 16),
        gcol(A(g_q_lat)[0], 4), gcol(A(g_kv_lat)[0], 4), gcol(A(pool_scale)[0], 8)], axis=1))
    ident = np.eye(128, dtype=f)
    cs_prev = rope_tables(np.arange(1024))
    pos_s = 1024 + (np.arange(128) % 32)

    shared = dict(wgu1=wgu1, wd1=wd1, wgu2=wgu2, wd2=wd2, wkvq=wkvq, wkr=wkr_h, wz=wz_h, wgate=wgate, wpool=wpool, watt=watt, watt_s=watt_s, wuv_s=wuv_s,
                  wmrg=wmrg, wout=wout, c_ident=ident, c_gains=gains, c_cs_prev=cs_prev)
    in_maps = []
    zeros_prev = np.zeros((D, NPREV), dtype=f)
    for c in range(8):
        b, half = c // 2, c % 2
        pos_o = half * 1024 + np.arange(1024)
        cs_main = np.ascontiguousarray(np.concatenate([rope_tables(pos_o), rope_tables(pos_s)], axis=2))
        rc = np.zeros((4, 16), dtype=f)
        for g, w in enumerate((2, 4, 8, 16)):
            rc[g] = 1.0 / np.minimum(pos_o[:16] + 1, w)
        m = dict(shared)
        m.update(
            x_own=np.ascontiguousarray(x_prompt[b, half * 1024:(half + 1) * 1024].T),
            x_prev=np.ascontiguousarray(x_prompt[b, 0:1024].T) if half == 1 else zeros_prev,
            x_smp=np.ascontiguousarray(x_sample[4 * c:4 * c + 4].reshape(128, D).T),
            cache_ckv=np.ascontiguousarray(cache_ckv[4 * c:4 * c + 4]),
            cache_ckvT=np.ascontiguousarray(cache_ckv[4 * c:4 * c + 4].transpose(0, 2, 1)),
            cache_krT=np.ascontiguousarray(cache_krope[4 * c:4 * c + 4].transpose(0, 2, 1)),
            state_pool=np.ascontiguousarray(state_pool[4 * c:4 * c + 4].reshape(60, 1024).T),
            c_cs_main=cs_main,
            c_rc16=np.ascontiguousarray(np.broadcast_to(rc.reshape(1, 64), (128, 64))).astype(f),
            c_prevb=np.full((128, 1), 0.0 if half == 1 else -1e30, dtype=f),
        )
        in_maps.append(m)

    if "nc" not in _CACHE:
        _CACHE["nc"] = build_program()
    nc = _CACHE["nc"]
    res = run_bass_kernel_spmd(nc, in_maps, core_ids=list(range(8)))
    R = res.results
    _CACHE["last"] = R

    y_prompt = np.zeros((4, 2048, D), f)
    y_sample = np.zeros((32, 32, D), f)
    ckv_p = np.zeros((1, 4, 2048, 512), f)
    kr_p = np.zeros((1, 4, 2048, 64), f)
    pool_p = np.zeros((1, 4, 15, 1024), f)
    ckv_s = np.zeros((1, 32, 32, 512), f)
    kr_s = np.zeros((1, 32, 32, 64), f)
    pool_s = np.zeros((1, 32, 15, 1024), f)
    for c in range(8):
        b, half = c // 2, c % 2
        r = R[c]
        sl = slice(half * 1024, (half + 1) * 1024)
        y_prompt[b, sl] = np.asarray(r["y_own"]).T
        ckvT = np.asarray(r["ckvT_out"])
        krT = np.asarray(r["krT_out"])
        ckv_p[0, b, sl] = ckvT[:, :NOWN].T
        kr_p[0, b, sl] = krT[:, :NOWN].T
        if half == 1:
            pool_p[0, b] = np.asarray(r["poolT_own"]).T
        y_sample[4 * c:4 * c + 4] = np.asarray(r["y_smp"]).T.reshape(4, 32, D)
        ckv_s[0, 4 * c:4 * c + 4] = ckvT[:, NOWN:].T.reshape(4, 32, 512)
        kr_s[0, 4 * c:4 * c + 4] = krT[:, NOWN:].T.reshape(4, 32, 64)
        pool_s[0, 4 * c:4 * c + 4] = np.asarray(r["poolT_smp"]).reshape(1024, 4, 15).transpose(1, 2, 0)
    return (y_prompt, y_sample, ckv_p, kr_p, pool_p, ckv_s, kr_s, pool_s)
```
